# Optimizing a Trainium2 kernel written in Bass

```python
import math
import jax, jax.numpy as jnp
from jax import lax
import numpy as np

D_MODEL = 1024
BATCH = 8
SEQ = 2048
DEPTH = 4
DEC_BATCH = 32
DEC_SEQ = 2048
PAST_LEN = 128

N_META = 16
GRID_W = 64
ROPE_THETA = 10000.0
RMS_EPS = 1e-6
DA_HEADS = 4
DA_HEAD = 64
DA_QK = DA_HEADS * 2 * DA_HEAD
DA_V = DA_HEADS * 2 * DA_HEAD
DA_BLOCK = 128
NA_HEADS = 8
NA_HEAD = 64
NA_W = NA_HEADS * NA_HEAD
NA_KH_MAX = 8
NA_KW = 16
SPLIT_SIZES = (DA_QK, DA_QK, DA_V, NA_W, NA_W, NA_W, D_MODEL, D_MODEL)
IN_COLS = sum(SPLIT_SIZES)
SPLITS = tuple(int(s) for s in np.cumsum(SPLIT_SIZES)[:-1])
PEER_HEADS = 8
PEER_NKEYS = 128
PEER_EXPERTS = PEER_NKEYS * PEER_NKEYS
PEER_TOPK = 16
PEER_DKEY = 256
PEER_HALF = PEER_DKEY // 2
PEER_BLOCK = 128

kernel_name = "hybrid_diffattn_natten_peer_encoder"

F32 = jnp.float32


def _rmsnorm(x, g):
    xf = x.astype(F32)
    y = xf * lax.rsqrt(jnp.mean(xf * xf, axis=-1, keepdims=True) + RMS_EPS)
    return (y * g.astype(F32)).astype(x.dtype)


def _rope_tables(L):
    half = DA_HEAD // 2
    inv = 1.0 / (ROPE_THETA ** (jnp.arange(half, dtype=F32) * 2.0 / DA_HEAD))
    ang = jnp.arange(L, dtype=F32)[:, None] * inv[None, :]
    return (jnp.cos(ang).reshape(1, L, 1, 1, half), jnp.sin(ang).reshape(1, L, 1, 1, half))


def _rope(x, cos, sin):
    half = DA_HEAD // 2
    xf = x.astype(F32)
    x1, x2 = xf[..., :half], xf[..., half:]
    return jnp.concatenate([x1 * cos - x2 * sin, x2 * cos + x1 * sin], axis=-1)


def _diff_attention(q, k, v, lam, subln_g, lam_init):
    B, L = q.shape[0], q.shape[1]
    S = L - N_META
    scale = DA_HEAD ** -0.5
    k32 = k.astype(F32)
    v32 = v.astype(F32)

    def attend(qb):
        s = jnp.einsum('bqhmd,bkhmd->bhmqk', qb.astype(F32), k32) * scale
        p = jax.nn.softmax(s, axis=-1)
        a = p[:, :, 0] - lam * p[:, :, 1]
        return jnp.einsum('bhqk,bkhe->bqhe', a, v32)

    o_meta = attend(q[:, :N_META])
    qr = q[:, N_META:].reshape(B, S // DA_BLOCK, DA_BLOCK, DA_HEADS, 2, DA_HEAD)
    o_real = lax.map(attend, jnp.moveaxis(qr, 1, 0))
    o_real = jnp.moveaxis(o_real, 0, 1).reshape(B, S, DA_HEADS, 2 * DA_HEAD)
    o = jnp.concatenate([o_meta, o_real], axis=1)
    o = _rmsnorm(o, subln_g) * (1.0 - lam_init)
    return o.reshape(B, L, DA_V)


def _neighbourhood_attention(q, k, v, rpb):
    B, L = q.shape[0], q.shape[1]
    S = L - N_META
    ROWS = S // GRID_W
    KH = min(NA_KH_MAX, ROWS)
    scale = NA_HEAD ** -0.5
    q32, k32, v32 = q.astype(F32), k.astype(F32), v.astype(F32)
    qm, km, vm = q32[:, :N_META], k32[:, :N_META], v32[:, :N_META]
    qg = q32[:, N_META:].reshape(B, ROWS, GRID_W, NA_HEADS, NA_HEAD)
    kg = k32[:, N_META:].reshape(B, ROWS, GRID_W, NA_HEADS, NA_HEAD)
    vg = v32[:, N_META:].reshape(B, ROWS, GRID_W, NA_HEADS, NA_HEAD)

    r = np.arange(ROWS)
    row_start = np.clip(r - KH // 2, 0, ROWS - KH)
    row_off = row_start[:, None] + np.arange(KH)[None, :] - r[:, None] + (NA_KH_MAX - 1)
    c = np.arange(GRID_W)
    col_start = np.clip(c - NA_KW // 2, 0, GRID_W - NA_KW)
    j = np.arange(GRID_W)
    col_valid = (j[None, :] >= col_start[:, None]) & (j[None, :] < col_start[:, None] + NA_KW)
    col_off = np.clip(j[None, :] - c[:, None], -(NA_KW - 1), NA_KW - 1) + (NA_KW - 1)
    col_bias = rpb.astype(F32)[:, :, col_off]
    col_bias = jnp.where(jnp.asarray(col_valid)[None, None], col_bias, -jnp.inf)

    def row_fn(args):
        q_row, rs, roff = args
        k_blk = lax.dynamic_slice_in_dim(kg, rs, KH, axis=1)
        v_blk = lax.dynamic_slice_in_dim(vg, rs, KH, axis=1)
        s = jnp.einsum('bqhd,bikhd->bhqik', q_row, k_blk) * scale
        s = s + jnp.transpose(col_bias[:, roff], (0, 2, 1, 3))[None]
        sm = jnp.einsum('bqhd,bmhd->bhqm', q_row, km) * scale
        p = jax.nn.softmax(jnp.concatenate([s.reshape(B, NA_HEADS, GRID_W, KH * GRID_W), sm], axis=-1), axis=-1)
        pg = p[..., :KH * GRID_W].reshape(B, NA_HEADS, GRID_W, KH, GRID_W)
        pm = p[..., KH * GRID_W:]
        return (jnp.einsum('bhqik,bikhd->bqhd', pg, v_blk)
                + jnp.einsum('bhqm,bmhd->bqhd', pm, vm))

    og = lax.map(row_fn, (jnp.transpose(qg, (1, 0, 2, 3, 4)),
                          jnp.asarray(row_start, dtype=jnp.int32),
                          jnp.asarray(row_off, dtype=jnp.int32)))
    og = jnp.transpose(og, (1, 0, 2, 3, 4)).reshape(B, S, NA_HEADS, NA_HEAD)
    pmm = jax.nn.softmax(jnp.einsum('bqhd,bmhd->bhqm', qm, km) * scale, axis=-1)
    om = jnp.einsum('bhqm,bmhd->bqhd', pmm, vm)
    return jnp.concatenate([om, og], axis=1).reshape(B, L, NA_W)


def _peer(xn, wq, keys, u, v):
    B, L, D = xn.shape
    xf = xn.reshape(B * L, D)
    T = xf.shape[0]
    pad = (-T) % PEER_BLOCK
    blocks = jnp.pad(xf, ((0, pad), (0, 0))).reshape(-1, PEER_BLOCK, D)
    keys32 = keys.astype(F32)

    def blk(xb):
        n = xb.shape[0]
        q = (xb @ wq).astype(F32).reshape(n, PEER_HEADS, 2, PEER_HALF)
        s = jnp.einsum('nhpd,hpkd->nhpk', q, keys32)
        s1, i1 = lax.top_k(s[:, :, 0], PEER_TOPK)
        s2, i2 = lax.top_k(s[:, :, 1], PEER_TOPK)
        cs = (s1[..., :, None] + s2[..., None, :]).reshape(n, PEER_HEADS, PEER_TOPK * PEER_TOPK)
        ci = (i1[..., :, None] * PEER_NKEYS + i2[..., None, :]).reshape(n, PEER_HEADS, PEER_TOPK * PEER_TOPK)
        sc, pos = lax.top_k(cs, PEER_TOPK)
        e = jnp.take_along_axis(ci, pos, axis=-1)
        g = jax.nn.softmax(sc, axis=-1)
        ue = jnp.take(u, e, axis=0)
        ve = jnp.take(v, e, axis=0)
        hid = jax.nn.gelu(jnp.einsum('nd,nhed->nhe', xb, ue).astype(F32), approximate=False)
        return jnp.einsum('nhe,nhed->nd', (g * hid).astype(ve.dtype), ve)

    out = lax.map(blk, blocks).reshape(-1, D)[:T]
    return out.reshape(B, L, D)


def _trunk(x, meta_tokens, norm_mix, w_in, lambda_q1, lambda_k1, lambda_q2, lambda_k2,
           subln_gain, na_rpb, w_branch_a, w_branch_b, w_out, norm_ffn,
           peer_wq, peer_keys, peer_u, peer_v, norm_final):
    dt = x.dtype
    B = x.shape[0]
    h = jnp.concatenate([jnp.broadcast_to(meta_tokens.astype(dt)[None], (B, N_META, D_MODEL)), x], axis=1)
    L = h.shape[1]
    cos, sin = _rope_tables(L)
    for l in range(DEPTH):
        a = _rmsnorm(h, norm_mix[l])
        z = a @ w_in[l]
        q_da, k_da, v_da, q_na, k_na, v_na, g_a, g_b = jnp.split(z, SPLITS, axis=-1)
        lam_init = 0.8 - 0.6 * math.exp(-0.3 * l)
        lam = (jnp.exp(jnp.sum(lambda_q1[l].astype(F32) * lambda_k1[l].astype(F32)))
               - jnp.exp(jnp.sum(lambda_q2[l].astype(F32) * lambda_k2[l].astype(F32))) + lam_init)
        q_da = _rope(q_da.reshape(B, L, DA_HEADS, 2, DA_HEAD), cos, sin)
        k_da = _rope(k_da.reshape(B, L, DA_HEADS, 2, DA_HEAD), cos, sin)
        v_da = v_da.reshape(B, L, DA_HEADS, 2 * DA_HEAD)
        o_da = _diff_attention(q_da, k_da, v_da, lam, subln_gain[l], lam_init).astype(dt)
        o_na = _neighbourhood_attention(q_na.reshape(B, L, NA_HEADS, NA_HEAD),
                                        k_na.reshape(B, L, NA_HEADS, NA_HEAD),
                                        v_na.reshape(B, L, NA_HEADS, NA_HEAD),
                                        na_rpb[l]).astype(dt)
        merged = jax.nn.sigmoid(g_a) * (o_da @ w_branch_a[l]) + jax.nn.sigmoid(g_b) * (o_na @ w_branch_b[l])
        h = h + (merged @ w_out[l]).astype(dt)
        c = _rmsnorm(h, norm_ffn[l])
        h = h + _peer(c, peer_wq[l], peer_keys[l], peer_u[l], peer_v[l]).astype(dt)
    return _rmsnorm(h, norm_final)[:, N_META:]


def setup_inputs(seed: int = 0) -> dict:
    key = jax.random.key(seed)
    ks = jax.random.split(key, 20)
    nrm = lambda k, shape, s: jax.random.normal(k, shape, dtype=F32) * s
    return {
        "x_prompt": nrm(ks[0], (BATCH, SEQ, D_MODEL), 1.0),
        "x_sample": nrm(ks[1], (DEC_BATCH, DEC_SEQ, D_MODEL), 1.0),
        "meta_tokens": nrm(ks[2], (N_META, D_MODEL), 1.0),
        "norm_mix": 1.0 + nrm(ks[3], (DEPTH, D_MODEL), 0.02),
        "w_in": nrm(ks[4], (DEPTH, D_MODEL, IN_COLS), D_MODEL ** -0.5),
        "lambda_q1": nrm(ks[5], (DEPTH, DA_HEAD), 0.1),
        "lambda_k1": nrm(ks[6], (DEPTH, DA_HEAD), 0.1),
        "lambda_q2": nrm(ks[7], (DEPTH, DA_HEAD), 0.1),
        "lambda_k2": nrm(ks[8], (DEPTH, DA_HEAD), 0.1),
        "subln_gain": 1.0 + nrm(ks[9], (DEPTH, 2 * DA_HEAD), 0.02),
        "na_rpb": nrm(ks[10], (DEPTH, NA_HEADS, 2 * NA_KH_MAX - 1, 2 * NA_KW - 1), 0.02),
        "w_branch_a": nrm(ks[11], (DEPTH, DA_V, D_MODEL), DA_V ** -0.5),
        "w_branch_b": nrm(ks[12], (DEPTH, NA_W, D_MODEL), NA_W ** -0.5),
        "w_out": nrm(ks[13], (DEPTH, D_MODEL, D_MODEL), D_MODEL ** -0.5),
        "norm_ffn": 1.0 + nrm(ks[14], (DEPTH, D_MODEL), 0.02),
        "peer_wq": nrm(ks[15], (DEPTH, D_MODEL, PEER_HEADS * PEER_DKEY), D_MODEL ** -0.5),
        "peer_keys": nrm(ks[16], (DEPTH, PEER_HEADS, 2, PEER_NKEYS, PEER_HALF), PEER_HALF ** -0.5),
        "peer_u": nrm(ks[17], (DEPTH, PEER_EXPERTS, D_MODEL), D_MODEL ** -0.5),
        "peer_v": nrm(ks[18], (DEPTH, PEER_EXPERTS, D_MODEL), (PEER_HEADS * PEER_TOPK) ** -0.5),
        "norm_final": 1.0 + nrm(ks[19], (D_MODEL,), 0.02),
    }


def reference(x_prompt, x_sample, meta_tokens, norm_mix, w_in, lambda_q1, lambda_k1, lambda_q2,
              lambda_k2, subln_gain, na_rpb, w_branch_a, w_branch_b, w_out, norm_ffn,
              peer_wq, peer_keys, peer_u, peer_v, norm_final):
    y_prompt = _trunk(x_prompt, meta_tokens, norm_mix, w_in, lambda_q1, lambda_k1, lambda_q2, lambda_k2,
                      subln_gain, na_rpb, w_branch_a, w_branch_b, w_out, norm_ffn,
                      peer_wq, peer_keys, peer_u, peer_v, norm_final)
    y_sample = _trunk(x_sample, meta_tokens, norm_mix, w_in, lambda_q1, lambda_k1, lambda_q2, lambda_k2,
                      subln_gain, na_rpb, w_branch_a, w_branch_b, w_out, norm_ffn,
                      peer_wq, peer_keys, peer_u, peer_v, norm_final)
    return (y_prompt, y_sample)
```

```python
import math
import numpy as np
from contextlib import ExitStack
import concourse.bass as bass
import concourse.mybir as mybir
from concourse.bass_utils import run_bass_kernel_spmd

F32 = mybir.dt.float32
BF16 = mybir.dt.bfloat16
I32 = mybir.dt.int32
U32 = mybir.dt.uint32
ALU = mybir.AluOpType
AF = mybir.ActivationFunctionType
AX = mybir.AxisListType

D = 1024
NMETA = 16
SEQ = 2048
L = SEQ + NMETA
DEPTH = 4
NCORES = 8
SEQ_PER_CORE = 5
EPS = 1e-6
TILES = [(0, 16)] + [(16 + 128 * i, 128) for i in range(16)]
NT = len(TILES)
QCHUNKS = [(16 + 512 * c, 512, [1 + 4 * c + j for j in range(4)]) for c in range(4)] + [(0, 16, [0])]
NEXP = 16384
ENGS = ("pe", "act", "dve", "pool", "sp")


class Prog:
    def __init__(self, nc, es, n_dma_sems=24):
        self.nc = nc
        self.es = es
        self.q = {e: [] for e in ENGS}
        self.cnt = {e: 0 for e in ENGS}
        self.waited = {e: {} for e in ENGS}
        self.res_w = {}
        self.res_r = {}
        self.sem = {}
        for e in ENGS:
            self.sem[e] = es.enter_context(nc.semaphore("s_" + e))
        self.dma_sems = []
        self.dma_val = {}
        for i in range(n_dma_sems):
            self.dma_sems.append(self.new_dma_sem("dma%d" % i))
        self.dma_rr = 0
        self.ninstr = 0

    def new_dma_sem(self, name):
        self.sem[name] = self.es.enter_context(self.nc.semaphore("s_" + name))
        self.dma_val[name] = 0
        return name

    def _deps(self, eng, reads, writes):
        ev = {}
        for r in reads:
            e = self.res_w.get(r)
            if e is not None and ev.get(e[0], 0) < e[1]:
                ev[e[0]] = e[1]
        for w in writes:
            e = self.res_w.get(w)
            if e is not None and ev.get(e[0], 0) < e[1]:
                ev[e[0]] = e[1]
            rr = self.res_r.get(w)
            if rr:
                for k, v in rr.items():
                    if ev.get(k, 0) < v:
                        ev[k] = v
        waits = []
        wd = self.waited[eng]
        for k, v in ev.items():
            if k == "pe" and eng == "pe":
                continue
            if wd.get(k, 0) < v:
                wd[k] = v
                waits.append((k, v))
        return waits

    def _commit(self, event, reads, writes):
        for w in writes:
            self.res_w[w] = event
            self.res_r[w] = {}
        k, v = event
        for r in reads:
            d = self.res_r.setdefault(r, {})
            if d.get(k, 0) < v:
                d[k] = v

    def op(self, eng, fn, reads=(), writes=()):
        waits = self._deps(eng, reads, writes)
        self.cnt[eng] += 1
        event = (eng, self.cnt[eng])
        self.q[eng].append((waits, fn, (eng, 1)))
        self._commit(event, reads, writes)
        self.ninstr += 1

    def dma(self, eng, fn, reads=(), writes=(), semkey=None):
        if semkey is None:
            semkey = self.dma_sems[self.dma_rr % len(self.dma_sems)]
            self.dma_rr += 1
        waits = self._deps(eng, reads, writes)
        pv = self.dma_val[semkey]
        wd = self.waited[eng]
        if pv > 0 and wd.get(semkey, 0) < pv:
            wd[semkey] = pv
            waits.append((semkey, pv))
        self.dma_val[semkey] = pv + 16
        event = (semkey, pv + 16)
        self.q[eng].append((waits, fn, (semkey, 16)))
        self._commit(event, reads, writes)
        self.ninstr += 1
        return event

    def barrier(self):
        tot = dict(self.dma_val)
        for e in ENGS:
            tot[e] = self.cnt[e]
        for eng in ENGS:
            waits = []
            wd = self.waited[eng]
            for k, v in tot.items():
                if v > 0 and k != eng and wd.get(k, 0) < v:
                    wd[k] = v
                    waits.append((k, v))
            if waits:
                self.q[eng].append((waits, None, None))
        self.res_w = {}
        self.res_r = {}

    def emit(self):
        nc = self.nc
        sem = self.sem
        q = self.q
        with nc.Block() as block:
            def run(engobj, lst):
                for waits, fn, inc in lst:
                    for k, v in waits:
                        engobj.wait_ge(sem[k], v)
                    if fn is not None:
                        getattr(engobj, fn[0])(**fn[1]).then_inc(sem[inc[0]], inc[1])

            @block.tensor
            def _(e):
                run(e, q["pe"])

            @block.scalar
            def _(e):
                run(e, q["act"])

            @block.vector
            def _(e):
                run(e, q["dve"])

            @block.gpsimd
            def _(e):
                run(e, q["pool"])

            @block.sync
            def _(e):
                run(e, q["sp"])


def host_consts():
    c = {}
    c["c_ident"] = np.eye(128, dtype=np.float32)
    c["c_iota16"] = np.tile(np.arange(16, dtype=np.float32)[None, :], (128, 1))
    half = 32
    inv = 1.0 / (10000.0 ** (np.arange(half, dtype=np.float64) * 2.0 / 64.0))
    c["c_invf"] = np.tile(inv, 4).reshape(128, 1).astype(np.float32) / np.float32(2 * math.pi)
    cc = np.arange(64)
    cs = np.clip(cc - 8, 0, 48)
    k = np.arange(64)
    c["c_colmask"] = ((k[:, None] >= cs[None, :]) & (k[:, None] < cs[None, :] + 16)).astype(np.float32)
    c["c_antiI"] = np.ascontiguousarray(np.eye(31, dtype=np.float32)[::-1])
    c["c_pos"] = np.tile(np.arange(L, dtype=np.float32)[None, :], (128, 1))
    return c


CONST_SHAPES = {"c_ident": [128, 128], "c_iota16": [128, 16], "c_invf": [128, 1],
                "c_colmask": [64, 64], "c_antiI": [31, 31], "c_pos": [128, L]}

WEIGHT_SHAPES = {
    "meta_tokens": [16, D], "norm_mix": [4, D], "w_in": [4, D, 5120],
    "lambda_q1": [4, 64], "lambda_k1": [4, 64], "lambda_q2": [4, 64], "lambda_k2": [4, 64],
    "subln_gain": [4, 128], "na_rpb": [4, 8, 15, 31], "w_branch_a": [4, 512, D],
    "w_branch_b": [4, 512, D], "w_out": [4, D, D], "norm_ffn": [4, D],
    "peer_wq": [4, D, 2048], "peer_keys": [4, 8, 2, 128, 128],
    "peer_u": [4, NEXP, D], "peer_v": [4, NEXP, D], "norm_final": [1, D],
}


def build(nseq=SEQ_PER_CORE, depth=DEPTH, dbg=False, stop_after=None):
    nc = bass.Bass("TRN2", target_bir_lowering=False)
    dr = {}
    dr["x"] = nc.dram_tensor("x", [nseq, SEQ, D], F32, kind="ExternalInput").ap()
    for k, shp in WEIGHT_SHAPES.items():
        dr[k] = nc.dram_tensor(k, shp, F32, kind="ExternalInput").ap()
    for k, shp in CONST_SHAPES.items():
        dr[k] = nc.dram_tensor(k, shp, F32, kind="ExternalInput").ap()
    y_d = nc.dram_tensor("y", [nseq, SEQ, D], F32, kind="ExternalOutput").ap()
    scr = {}
    for nm in ("QTda", "KTda", "QTna", "KTna", "ODT", "ONT"):
        scr[nm] = nc.dram_tensor("scr_" + nm, [4, 128, L], BF16).ap()
    scr["Vda"] = nc.dram_tensor("scr_Vda", [L, 4, 130], BF16).ap()
    scr["Vna"] = nc.dram_tensor("scr_Vna", [L, 8, 66], BF16).ap()
    scr["G"] = nc.dram_tensor("scr_G", [L, 2048], BF16).ap()
    PW = 160
    Dsk_t = nc.dram_tensor("scr_Dsk", [4, 120, 64, PW], F32)
    scr["Dsk"] = Dsk_t.ap()
    scr["E"] = nc.dram_tensor("scr_E", [4, 64, 7680], BF16).ap()

    with ExitStack() as es:
        P = Prog(nc, es)
        gsem = [P.new_dma_sem("gat%d" % i) for i in range(8)]

        uid = [0]

        def sb(es_, name, shape, dt):
            uid[0] += 1
            if shape[0] < 128:
                t_ = es_.enter_context(nc.sbuf_tensor("sb_%s_%d" % (name, uid[0]), [128] + list(shape[1:]), dt))
                return t_[0:shape[0]]
            return es_.enter_context(nc.sbuf_tensor("sb_%s_%d" % (name, uid[0]), shape, dt))

        def ps(es_, name, shape, dt):
            uid[0] += 1
            return es_.enter_context(nc.psum_tensor("ps_%s_%d" % (name, uid[0]), shape, dt))

        H = sb(es, "H", [128, NT, D], F32)
        ident_f = sb(es, "ident_f", [128, 128], F32)
        ident_b = sb(es, "ident_b", [128, 128], BF16)
        iota16 = sb(es, "iota16", [128, 16], F32)
        cosT = sb(es, "cosT", [128, L], BF16)
        sinT = sb(es, "sinT", [128, L], BF16)
        lam = sb(es, "lam", [128, 8], F32)
        gain_bc = sb(es, "gain_bc", [128, 4, 128], F32)
        gfin_bc = sb(es, "gfin_bc", [128, D], F32)
        small = sb(es, "small", [128, 64], F32)

        P.dma("sp", ("dma_start", dict(out=ident_f[:], in_=dr["c_ident"])), writes=["ident_f"])
        P.dma("sp", ("dma_start", dict(out=iota16[:], in_=dr["c_iota16"])), writes=["iota16"])
        P.op("dve", ("tensor_copy", dict(out=ident_b[:], in_=ident_f[:])), reads=["ident_f"], writes=["ident_b"])
        P.dma("sp", ("dma_start", dict(out=gfin_bc[:], in_=dr["norm_final"].partition_broadcast(128))),
              writes=["gfin_bc"])
        with ExitStack() as s1:
            invf = sb(s1, "invf", [128, 1], F32)
            pos = sb(s1, "pos", [128, L], F32)
            yy = sb(s1, "yy", [128, L], F32)
            yi = sb(s1, "yi", [128, L], I32)
            yf = sb(s1, "yf", [128, L], F32)
            lq = sb(s1, "lq", [128, 4, 4, 64], F32)
            junk = sb(s1, "junk", [128, 64], F32)
            P.dma("sp", ("dma_start", dict(out=invf[:], in_=dr["c_invf"])), writes=["invf"])
            P.dma("sp", ("dma_start", dict(out=pos[:], in_=dr["c_pos"])), writes=["pos"])
            for which, tab in ((0, sinT), (1, cosT)):
                P.op("dve", ("tensor_scalar", dict(
                    out=yy[:], in0=pos[:], scalar1=invf[:, 0:1], scalar2=0.25 * which,
                    op0=ALU.mult, op1=ALU.add)), reads=["pos", "invf"], writes=["yy"])
                P.op("dve", ("tensor_copy", dict(out=yi[:], in_=yy[:])), reads=["yy"], writes=["yi"])
                P.op("dve", ("tensor_copy", dict(out=yf[:], in_=yi[:])), reads=["yi"], writes=["yf"])
                P.op("dve", ("tensor_tensor", dict(out=yy[:], in0=yy[:], in1=yf[:], op=ALU.subtract)),
                     reads=["yy", "yf"], writes=["yy"])
                P.op("dve", ("tensor_scalar", dict(out=yy[:], in0=yy[:], scalar1=0.5, scalar2=-0.5,
                                                      op0=ALU.min, op1=ALU.max)), reads=["yy"], writes=["yy"])
                P.op("act", ("activation", dict(out=tab[:], in_=yy[:], func=AF.Sin,
                                                            scale=2 * math.pi)),
                     reads=["yy"], writes=["tab%d" % which])
            for i, nm in enumerate(("lambda_q1", "lambda_k1", "lambda_q2", "lambda_k2")):
                for l in range(4):
                    P.dma("sp", ("dma_start", dict(
                        out=lq[:, i, l, :], in_=dr[nm][l:l + 1, :].partition_broadcast(128))),
                        writes=["lq"])
            for l in range(4):
                lam_init = 0.8 - 0.6 * math.exp(-0.3 * l)
                P.op("dve", ("scalar_tensor_tensor", dict(
                    out=junk[:], in0=lq[:, 0, l, :], scalar=1.0, in1=lq[:, 1, l, :],
                    op0=ALU.mult, op1=ALU.mult, accum_out=small[:, 0:1])),
                    reads=["lq"], writes=["junk", "small"])
                P.op("dve", ("scalar_tensor_tensor", dict(
                    out=junk[:], in0=lq[:, 2, l, :], scalar=1.0, in1=lq[:, 3, l, :],
                    op0=ALU.mult, op1=ALU.mult, accum_out=small[:, 1:2])),
                    reads=["lq", "small"], writes=["junk", "small"])
                P.op("act", ("activation", dict(out=small[:, 2:4], in_=small[:, 0:2], func=AF.Exp)),
                     reads=["small"], writes=["small"])
                P.op("dve", ("scalar_tensor_tensor", dict(
                    out=lam[:, l:l + 1], in0=small[:, 2:3], scalar=lam_init, in1=small[:, 3:4],
                    op0=ALU.add, op1=ALU.subtract)), reads=["small"], writes=["lam"])
                P.op("dve", ("tensor_scalar", dict(
                    out=lam[:, 4 + l:5 + l], in0=lam[:, l:l + 1], scalar1=-1.0, scalar2=None, op0=ALU.mult)),
                    reads=["lam"], writes=["lam"])
                P.dma("sp", ("dma_start", dict(
                    out=gain_bc[:, l, :], in_=dr["subln_gain"][l:l + 1, :].partition_broadcast(128))),
                    writes=["gain_bc"])
                P.op("dve", ("tensor_scalar", dict(
                    out=gain_bc[:, l, :], in0=gain_bc[:, l, :], scalar1=1.0 - lam_init, scalar2=None, op0=ALU.mult)),
                    reads=["gain_bc"], writes=["gain_bc"])
            P.barrier()
        with ExitStack() as s1:
          if stop_after != "S0":
              rT = sb(s1, "rT", [31, 120], F32)
              antiI = sb(s1, "antiI", [31, 31], F32)
              colmask = sb(s1, "colmask", [64, 64], F32)
              Ppad = sb(s1, "Ppad", [120, PW], F32)
              Esk = sb(s1, "Esk", [64, 120, 64], F32)
              Ebf = sb(s1, "Ebf", [64, 120, 64], BF16)
              prev = ps(s1, "prev", [128, 512], F32)
              P.dma("sp", ("dma_start", dict(out=antiI[:], in_=dr["c_antiI"])), writes=["antiI"])
              P.dma("sp", ("dma_start", dict(out=colmask[:], in_=dr["c_colmask"])), writes=["colmask"])
              for l in range(depth):
                  P.dma("sp", ("dma_start", dict(
                      out=rT[:], in_=dr["na_rpb"][l].rearrange("h r j -> j (h r)"), allow_slow_non_contiguous=True)),
                      writes=["rT"])
                  P.op("pe", ("matmul", dict(out=prev[0:120, 0:31], lhsT=rT[:, :], rhs=antiI[:, :],
                                                start=True, stop=True)), reads=["rT", "antiI"], writes=["prev"])
                  P.op("dve", ("memset", dict(ap=Ppad[:], constant=-30000.0)), writes=["Ppad"])
                  P.op("dve", ("tensor_copy", dict(out=Ppad[:, 64:95], in_=prev[0:120, 0:31])),
                       reads=["prev"], writes=["Ppad"])
                  P.op("act", ("activation", dict(out=Ppad[:], in_=Ppad[:], func=AF.Exp)),
                       reads=["Ppad"], writes=["Ppad"])
                  P.dma("sp", ("dma_start", dict(
                      out=scr["Dsk"][l], in_=Ppad[:].unsqueeze(1).to_broadcast([120, 64, PW]))),
                      reads=["Ppad"], writes=["Dsk"])
                  src = bass.AP(Dsk_t, l * 120 * 64 * PW + 79, [[PW - 1, 64], [64 * PW, 120], [1, 64]])
                  P.dma("sp", ("dma_start", dict(out=Esk[:], in_=src)), reads=["Dsk"], writes=["Esk"])
                  P.op("dve", ("tensor_tensor", dict(
                      out=Ebf[:], in0=Esk[:], in1=colmask[:].unsqueeze(1).to_broadcast([64, 120, 64]),
                      op=ALU.mult)), reads=["Esk", "colmask"], writes=["Ebf"])
                  P.dma("sp", ("dma_start", dict(
                      out=scr["E"][l], in_=Ebf[:].rearrange("k a q -> k (a q)"))), reads=["Ebf"], writes=["E_d"])
              P.barrier()

        def rms_rstd(Xap, n, col, tag):
            width = Xap.shape[-1]
            P.op("act", ("activation", dict(out=sq_junk[0:n, 0:width], in_=Xap, func=AF.Square,
                                               accum_out=small[0:n, col:col + 1])),
                 reads=[tag], writes=["sq_junk", "small"])
            P.op("dve", ("tensor_scalar", dict(out=small[0:n, col:col + 1], in0=small[0:n, col:col + 1],
                                                  scalar1=1.0 / width, scalar2=EPS, op0=ALU.mult, op1=ALU.add)),
                 reads=["small"], writes=["small"])
            P.op("act", ("activation", dict(out=small[0:n, col:col + 1], in_=small[0:n, col:col + 1],
                                               func=AF.Sqrt)), reads=["small"], writes=["small"])
            P.op("dve", ("reciprocal", dict(out=small[0:n, col:col + 1], in_=small[0:n, col:col + 1])),
                 reads=["small"], writes=["small"])

        sq_junk = sb(es, "sq_junk", [128, D], F32)
        wstage = [None, None]
        wst_i = [0]

        def alloc_wstage(es_):
            for i in range(2):
                wstage[i] = sb(es_, "wstage%d" % i, [128, 4096], F32)

        def load_w_bf16(dst_tile, dst_ap, src_ap, dst_name, shape3):
            a, b = shape3
            i = wst_i[0] % 2
            wst_i[0] += 1
            st = wstage[i]
            stv = st[:, 0:a * b].rearrange("p (a b) -> p a b", a=a)
            P.dma("sp", ("dma_start", dict(out=stv, in_=src_ap)), writes=["wstage%d" % i])
            P.op("pool", ("tensor_copy", dict(out=dst_ap, in_=stv)), reads=["wstage%d" % i], writes=[dst_name])
            return stv, i

        def norm_transpose(t, gbc, AT_dst, pst, xkeep=None, tag_out="AT"):
            off, n = TILES[t]
            rms_rstd(H[0:n, t, :], n, 8, ("H", t))
            dst = xkeep if xkeep is not None else abuf
            P.op("dve", ("scalar_tensor_tensor", dict(
                out=dst[0:n, :], in0=H[0:n, t, :], scalar=small[0:n, 8:9], in1=gbc[0:n, :],
                op0=ALU.mult, op1=ALU.mult)), reads=[("H", t), "small", "gbc"], writes=["abuf"])
            if xkeep is not None:
                P.op("act", ("activation", dict(out=abuf[0:n, :], in_=xkeep[0:n, :], func=AF.Copy)),
                     reads=["abuf"], writes=["abuf_b"])
                rtag = "abuf_b"
            else:
                rtag = "abuf"
            for c in range(8):
                P.op("pe", ("transpose", dict(out=pst[:, c, 0:n], in_=abuf[0:n, c * 128:(c + 1) * 128],
                                                      identity=ident_b[0:n, 0:n])),
                     reads=[rtag, "ident_b"], writes=["pst"])
            P.op("act", ("activation", dict(out=AT_dst, in_=pst[:, :, 0:n], func=AF.Copy)),
                 reads=["pst"], writes=[tag_out])

        abuf = sb(es, "abuf", [128, D], BF16)

        for s in range(nseq if stop_after not in ("S0", "S") else 0):
            P.dma("sp", ("dma_start", dict(out=H[0:16, 0, :], in_=dr["meta_tokens"])), writes=[("H", 0)])
            for t in range(1, NT):
                P.dma("sp", ("dma_start", dict(out=H[:, t, :], in_=dr["x"][s, 128 * (t - 1):128 * t, :])),
                      writes=[("H", t)])
            for l in range(depth):
                with ExitStack() as pa:
                    alloc_wstage(pa)
                    AT = sb(pa, "AT", [128, 8, L], BF16)
                    gbc = sb(pa, "gbc", [128, D], F32)
                    abuf_f = None
                    wbf = [sb(pa, "wbf%d" % i, [128, 8, 512], BF16) for i in range(2)]
                    wrot = sb(pa, "wrot", [128, 8, 512], BF16)
                    ev_f = sb(pa, "ev_f", [128, 2, 512], F32)
                    ev_b = [sb(pa, "ev_b%d" % i, [128, 520], BF16) for i in range(2)]
                    vda_b = [sb(pa, "vda_b%d" % i, [128, 4, 130], BF16) for i in range(2)]
                    vna_b = [sb(pa, "vna_b%d" % i, [128, 8, 66], BF16) for i in range(2)]
                    pst = ps(pa, "pstA", [128, 8, 128], BF16)
                    pz = [ps(pa, "pzA%d" % i, [128, 512], F32) for i in range(4)]
                    P.dma("sp", ("dma_start", dict(out=gbc[:], in_=dr["norm_mix"][l:l + 1, :].partition_broadcast(128))),
                          writes=["gbc"])
                    for i in range(2):
                        P.op("pool", ("memset", dict(ap=vda_b[i][:], constant=1.0)), writes=["vda_b%d" % i])
                        P.op("pool", ("memset", dict(ap=vna_b[i][:], constant=1.0)), writes=["vna_b%d" % i])
                    for t in range(NT):
                        off, n = TILES[t]
                        norm_transpose(t, gbc, AT[:, :, off:off + n], pst, tag_out=("AT", t))
                    AT_all = [("AT", t) for t in range(NT)]
                    w3 = dr["w_in"][l].rearrange("(kc p) c -> p kc c", p=128)
                    ecnt = [0]
                    for cc in range(10):
                        wb = wbf[cc % 2]
                        wname = "wbf%d" % (cc % 2)
                        stv, sti = load_w_bf16(wb, wb[:], w3[:, :, cc * 512:(cc + 1) * 512], wname, (8, 512))
                        if cc in (0, 1):
                            st5 = stv.rearrange("p a (g two h) -> p a g two h", two=2, h=32)
                            wr5 = wrot[:].rearrange("p a (g two h) -> p a g two h", two=2, h=32)
                            for kc in range(8):
                                P.op("pool", ("tensor_scalar", dict(
                                    out=wr5[:, kc, :, 0, :], in0=st5[:, kc, :, 1, :], scalar1=-1.0, scalar2=None,
                                    op0=ALU.mult)), reads=["wstage%d" % sti], writes=["wrot"])
                                P.op("pool", ("tensor_copy", dict(
                                    out=wr5[:, kc, :, 1, :], in_=st5[:, kc, :, 0, :])),
                                    reads=["wstage%d" % sti], writes=["wrot"])
                        if cc in (0, 1, 3, 4):
                            dstn = {0: "QTda", 1: "KTda", 3: "QTna", 4: "KTna"}[cc]
                            for hb in range(4):
                                for (q0, nq, _tl) in QCHUNKS:
                                    k = ecnt[0]
                                    ecnt[0] += 1
                                    pa_ = pz[(2 * k) % 4]
                                    pb_ = pz[(2 * k + 1) % 4]
                                    pan = "pzA%d" % ((2 * k) % 4)
                                    pbn = "pzA%d" % ((2 * k + 1) % 4)
                                    evb = ev_b[k % 2]
                                    evn = "ev_b%d" % (k % 2)
                                    for kc in range(8):
                                        P.op("pe", ("matmul", dict(
                                            out=pa_[:, 0:nq], lhsT=wb[:, kc, hb * 128:(hb + 1) * 128],
                                            rhs=AT[:, kc, q0:q0 + nq], start=(kc == 0), stop=(kc == 7))),
                                            reads=[wname] + AT_all, writes=[pan])
                                    if cc in (0, 1):
                                        for kc in range(8):
                                            P.op("pe", ("matmul", dict(
                                                out=pb_[:, 0:nq], lhsT=wrot[:, kc, hb * 128:(hb + 1) * 128],
                                                rhs=AT[:, kc, q0:q0 + nq], start=(kc == 0), stop=(kc == 7))),
                                                reads=["wrot"] + AT_all, writes=[pbn])
                                        P.op("dve", ("tensor_tensor", dict(
                                            out=ev_f[:, 0, 0:nq], in0=pa_[:, 0:nq], in1=cosT[:, q0:q0 + nq], op=ALU.mult)),
                                            reads=[pan, "tab1"], writes=["ev_f0"])
                                        P.op("dve", ("tensor_tensor", dict(
                                            out=ev_f[:, 1, 0:nq], in0=pb_[:, 0:nq], in1=sinT[:, q0:q0 + nq], op=ALU.mult)),
                                            reads=[pbn, "tab0"], writes=["ev_f1"])
                                        P.op("dve", ("tensor_tensor", dict(
                                            out=evb[:, 0:nq], in0=ev_f[:, 0, 0:nq], in1=ev_f[:, 1, 0:nq], op=ALU.add)),
                                            reads=["ev_f0", "ev_f1"], writes=[evn])
                                    else:
                                        P.op("act", ("activation", dict(
                                            out=evb[:, 0:nq], in_=pa_[:, 0:nq], func=AF.Copy)),
                                            reads=[pan], writes=[evn])
                                    P.dma("sp", ("dma_start", dict(
                                        out=scr[dstn][hb, :, q0:q0 + nq], in_=evb[:, 0:nq])),
                                        reads=[evn], writes=[(dstn, hb, q0)])
                        else:
                            for t in range(NT):
                                off, n = TILES[t]
                                k = ecnt[0]
                                ecnt[0] += 1
                                pa_ = pz[k % 4]
                                pan = "pzA%d" % (k % 4)
                                for kc in range(8):
                                    P.op("pe", ("matmul", dict(
                                        out=pa_[0:n, :], lhsT=AT[:, kc, off:off + n], rhs=wb[:, kc, :],
                                        start=(kc == 0), stop=(kc == 7))), reads=[wname, ("AT", t)], writes=[pan])
                                if cc == 2:
                                    vb = vda_b[k % 2]
                                    vn = "vda_b%d" % (k % 2)
                                    P.op("act", ("activation", dict(
                                        out=vb[0:n, :, 0:128], in_=pa_[0:n, :].rearrange("p (h e) -> p h e", h=4),
                                        func=AF.Copy)), reads=[pan], writes=[vn])
                                    P.dma("sp", ("dma_start", dict(
                                        out=scr["Vda"][off:off + n], in_=vb[0:n])), reads=[vn], writes=[("Vda", t)])
                                elif cc == 5:
                                    vb = vna_b[k % 2]
                                    vn = "vna_b%d" % (k % 2)
                                    P.op("act", ("activation", dict(
                                        out=vb[0:n, :, 0:64], in_=pa_[0:n, :].rearrange("p (h e) -> p h e", h=8),
                                        func=AF.Copy)), reads=[pan], writes=[vn])
                                    P.dma("sp", ("dma_start", dict(
                                        out=scr["Vna"][off:off + n], in_=vb[0:n])), reads=[vn], writes=[("Vna", t)])
                                else:
                                    evb = ev_b[k % 2]
                                    evn = "ev_b%d" % (k % 2)
                                    P.op("act", ("activation", dict(
                                        out=evb[0:n, 0:512], in_=pa_[0:n, :], func=AF.Sigmoid)),
                                        reads=[pan], writes=[evn])
                                    gc0 = (cc - 6) * 512
                                    P.dma("sp", ("dma_start", dict(
                                        out=scr["G"][off:off + n, gc0:gc0 + 512], in_=evb[0:n, 0:512])),
                                        reads=[evn], writes=[("G", t, cc)])
                    P.barrier()
                if stop_after == "A":
                    break
                with ExitStack() as pb:
                    KT = sb(pb, "KT", [128, 4, L], BF16)
                    VA = sb(pb, "VA", [128, NT, 4, 130], BF16)
                    QT = [sb(pb, "QT%d" % i, [128, 4, 512], BF16) for i in range(2)]
                    PT = [sb(pb, "PT%d" % i, [128, 512], BF16) for i in range(3)]
                    OD = sb(pb, "OD", [128, 4, 512], BF16)
                    ODTs = sb(pb, "ODTs", [128, 4, 512], BF16)
                    of = sb(pb, "of", [128, 2, 128], F32)
                    psS = [ps(pb, "psS%d" % i, [128, 512], F32) for i in range(2)]
                    acc = [[ps(pb, "acc%d_%d" % (m, i), [128, 2, 256], F32) for i in range(2)] for m in range(2)]
                    pstB = ps(pb, "pstB", [128, 1024], BF16)[:, 0:512].rearrange("p (h t) -> p h t", h=4)
                    P.dma("sp", ("dma_start", dict(out=KT[:], in_=scr["KTda"].rearrange("h p t -> p h t"))),
                          reads=[("KTda", hb, q0) for hb in range(4) for (q0, _a, _b) in QCHUNKS], writes=["KT"])
                    for t in range(NT):
                        off, n = TILES[t]
                        P.dma("sp", ("dma_start", dict(out=VA[0:n, t], in_=scr["Vda"][off:off + n])),
                              reads=[("Vda", t)], writes=["VA"])
                    sc_i = 0
                    for ci, (q0, nq, tl) in enumerate(QCHUNKS):
                        qt = QT[ci % 2]
                        qtn = "QT%d" % (ci % 2)
                        P.dma("sp", ("dma_start", dict(
                            out=qt[:, :, 0:nq], in_=scr["QTda"][:, :, q0:q0 + nq].rearrange("h p t -> p h t"))),
                            reads=[("QTda", hb, q0) for hb in range(4)], writes=[qtn])
                        nsub = len(tl)
                        for h in range(4):
                            for m in range(2):
                                bp = 64 * m
                                for kt in range(NT):
                                    koff, kn = TILES[kt]
                                    pS = psS[sc_i % 2]
                                    pSn = "psS%d" % (sc_i % 2)
                                    pt = PT[sc_i % 3]
                                    ptn = "PT%d" % (sc_i % 3)
                                    sc_i += 1
                                    P.op("pe", ("matmul", dict(
                                        out=pS[0:kn, 0:nq], lhsT=KT[bp:bp + 64, h, koff:koff + kn],
                                        rhs=qt[bp:bp + 64, h, 0:nq], start=True, stop=True)),
                                        reads=["KT", qtn], writes=[pSn])
                                    P.op("act", ("activation", dict(
                                        out=pt[0:kn, 0:nq], in_=pS[0:kn, 0:nq], func=AF.Exp, scale=0.125)),
                                        reads=[pSn], writes=[ptn])
                                    for j in range(nsub):
                                        nqs = TILES[tl[j]][1]
                                        a_ = acc[m][j // 2]
                                        an = "acc%d_%d" % (m, j // 2)
                                        P.op("pe", ("matmul", dict(
                                            out=a_[0:nqs, j % 2, 0:129], lhsT=pt[0:kn, j * 128:j * 128 + nqs],
                                            rhs=VA[0:kn, kt, h, 0:129], start=(kt == 0 and j % 2 == 0), stop=(kt == NT - 1))),
                                            reads=[ptn, "VA"], writes=[an])
                            for j in range(nsub):
                                nqs = TILES[tl[j]][1]
                                a0 = acc[0][j // 2]
                                a1 = acc[1][j // 2]
                                a0n = "acc0_%d" % (j // 2)
                                a1n = "acc1_%d" % (j // 2)
                                P.op("dve", ("reciprocal", dict(
                                    out=small[0:nqs, 16:17], in_=a0[0:nqs, j % 2, 128:129])), reads=[a0n], writes=["small"])
                                P.op("dve", ("reciprocal", dict(
                                    out=small[0:nqs, 17:18], in_=a1[0:nqs, j % 2, 128:129])), reads=[a1n, "small"], writes=["small"])
                                P.op("dve", ("tensor_scalar", dict(
                                    out=small[0:nqs, 17:18], in0=small[0:nqs, 17:18], scalar1=lam[0:nqs, 4 + l:5 + l],
                                    scalar2=None, op0=ALU.mult)), reads=["small", "lam"], writes=["small"])
                                P.op("dve", ("tensor_scalar", dict(
                                    out=of[0:nqs, 0, :], in0=a1[0:nqs, j % 2, 0:128], scalar1=small[0:nqs, 17:18],
                                    scalar2=None, op0=ALU.mult)), reads=[a1n, "small"], writes=["of0"])
                                P.op("dve", ("scalar_tensor_tensor", dict(
                                    out=of[0:nqs, 1, :], in0=a0[0:nqs, j % 2, 0:128], scalar=small[0:nqs, 16:17],
                                    in1=of[0:nqs, 0, :], op0=ALU.mult, op1=ALU.add)),
                                    reads=[a0n, "small", "of0"], writes=["of1"])
                                rms_rstd(of[0:nqs, 1, :], nqs, 18, "of1")
                                P.op("dve", ("scalar_tensor_tensor", dict(
                                    out=OD[0:nqs, j, h * 128:(h + 1) * 128], in0=of[0:nqs, 1, :],
                                    scalar=small[0:nqs, 18:19], in1=gain_bc[0:nqs, l, :], op0=ALU.mult, op1=ALU.mult)),
                                    reads=["of1", "small", "gain_bc"], writes=["OD"])
                        if stop_after == "B":
                            for j in range(nsub):
                                tj = tl[j]
                                if tj == 0:
                                    continue
                                P.op("act", ("activation", dict(out=sq_junk[:, 0:512], in_=OD[:, j, :], func=AF.Copy)),
                                     reads=["OD"], writes=["sq_junk"])
                                P.dma("sp", ("dma_start", dict(out=y_d[s, 128 * (tj - 1):128 * tj, 0:512], in_=sq_junk[:, 0:512])),
                                      reads=["sq_junk"], writes=[("y", tj)])
                        for j in range(nsub):
                            nqs = TILES[tl[j]][1]
                            for h in range(4):
                                P.op("pe", ("transpose", dict(
                                    out=pstB[:, h, 0:nqs], in_=OD[0:nqs, j, h * 128:(h + 1) * 128],
                                    identity=ident_b[0:nqs, 0:nqs])), reads=["OD", "ident_b"], writes=["pstB"])
                            P.op("act", ("activation", dict(
                                out=ODTs[:, :, j * 128:j * 128 + nqs], in_=pstB[:, :, 0:nqs], func=AF.Copy)),
                                reads=["pstB"], writes=["ODTs"])
                        P.dma("sp", ("dma_start", dict(
                            out=scr["ODT"][:, :, q0:q0 + nq].rearrange("h p t -> p h t"), in_=ODTs[:, :, 0:nq])),
                            reads=["ODTs"], writes=[("ODT", ci)])
                    P.barrier()
                if stop_after == "B":
                    break
                with ExitStack() as pc:
                    KT = sb(pc, "KTn", [128, 4, L], BF16)
                    QT = sb(pc, "QTn", [128, 4, L], BF16)
                    VN = sb(pc, "VN", [64, 33, 8, 66], BF16)
                    Et = sb(pc, "Et", [64, 8, 16, 64], BF16)
                    Pn = [sb(pc, "Pn%d" % i, [64, 4, 8, 64], BF16) for i in range(2)]
                    Pm = [sb(pc, "Pm%d" % i, [16, 4, 64], BF16) for i in range(2)]
                    ONr = [sb(pc, "ONr%d" % i, [64, 512], BF16) for i in range(2)]
                    ONT2 = [sb(pc, "ONT2_%d" % i, [128, 4, 128], BF16) for i in range(2)]
                    psN = [ps(pc, "psN%d" % i, [128, 512], F32) for i in range(4)]
                    psM = ps(pc, "psM", [128, 512], F32)[:, 0:256].rearrange("p (i q) -> p i q", i=4)
                    psO = ps(pc, "psO", [128, 512], F32)[:, 0:512].rearrange("p (i e) -> p i e", i=4)
                    pstC = ps(pc, "pstC", [128, 1024], BF16)[:, 0:256].rearrange("p (c q) -> p c q", c=4)
                    P.dma("sp", ("dma_start", dict(out=KT[:], in_=scr["KTna"].rearrange("h p t -> p h t"))),
                          reads=[("KTna", hb, q0) for hb in range(4) for (q0, _a, _b) in QCHUNKS], writes=["KTn"])
                    P.dma("sp", ("dma_start", dict(out=QT[:], in_=scr["QTna"].rearrange("h p t -> p h t"))),
                          reads=[("QTna", hb, q0) for hb in range(4) for (q0, _a, _b) in QCHUNKS], writes=["QTn"])
                    P.dma("sp", ("dma_start", dict(out=VN[0:16, 0], in_=scr["Vna"][0:16])),
                          reads=[("Vna", t) for t in range(NT)], writes=["VN"])
                    P.dma("sp", ("dma_start", dict(
                        out=VN[:, 1:33], in_=scr["Vna"][16:L].rearrange("(r k) h e -> k r h e", k=64))),
                        reads=[("Vna", t) for t in range(NT)], writes=["VN"])
                    P.dma("sp", ("dma_start", dict(out=Et[:, :, 0:15, :].rearrange("k h r q -> k h (r q)"),
                                                   in_=scr["E"][l].rearrange("k (h x) -> k h x", h=8))),
                          reads=["E_d"], writes=["Et"])
                    P.op("dve", ("memset", dict(ap=Et[:, :, 15, :], constant=0.0)), writes=["Et15"])

                    def na_block(qoff, nqr, rs, rho0, nkr, out_tile, out_name, gi):
                        for hg in range(2):
                            pn = Pn[gi[0] % 2]
                            pnn = "Pn%d" % (gi[0] % 2)
                            pm = Pm[gi[0] % 2]
                            pmn = "Pm%d" % (gi[0] % 2)
                            gi[0] += 1
                            for i in range(4):
                                h = 4 * hg + i
                                pr, bp = h // 2, 64 * (h % 2)
                                for jr in range(nkr):
                                    k0 = 16 + 64 * (rs + jr)
                                    P.op("pe", ("matmul", dict(
                                        out=psN[i][0:64, jr * 64:jr * 64 + nqr], lhsT=KT[bp:bp + 64, pr, k0:k0 + 64],
                                        rhs=QT[bp:bp + 64, pr, qoff:qoff + nqr], start=True, stop=True)),
                                        reads=["KTn", "QTn"], writes=["psN%d" % i])
                                P.op("pe", ("matmul", dict(
                                    out=psM[0:16, i, 0:nqr], lhsT=KT[bp:bp + 64, pr, 0:16],
                                    rhs=QT[bp:bp + 64, pr, qoff:qoff + nqr], start=True, stop=True)),
                                    reads=["KTn", "QTn"], writes=["psM"])
                            if nkr > 0:
                                for i in range(4):
                                    P.op("act", ("activation", dict(
                                        out=pn[:, i, 0:nkr, 0:nqr],
                                        in_=psN[i][0:64, 0:nkr * 64].rearrange("p (r q) -> p r q", q=64)[:, :, 0:nqr],
                                        func=AF.Exp, scale=0.125)), reads=["psN%d" % i], writes=[pnn])
                                P.op("dve", ("tensor_tensor", dict(
                                    out=pn[:, :, 0:nkr, 0:nqr], in0=pn[:, :, 0:nkr, 0:nqr],
                                    in1=Et[:, 4 * hg:4 * hg + 4, rho0:rho0 + nkr, 0:nqr], op=ALU.mult)),
                                    reads=[pnn, "Et", "Et15"], writes=[pnn])
                            import os
                            NSTEP = int(os.environ.get("NA_STEP", "9"))
                            if NSTEP < 2:
                                continue
                            P.op("act", ("activation", dict(
                                out=pm[0:16, :, 0:nqr], in_=psM[0:16, :, 0:nqr], func=AF.Exp, scale=0.125)),
                                reads=["psM"], writes=[pmn])
                            if NSTEP < 3:
                                continue
                            for i in range(4):
                                h = 4 * hg + i
                                for jr in range(nkr):
                                    P.op("pe", ("matmul", dict(
                                        out=psO[0:nqr, i, 0:65], lhsT=pn[:, i, jr, 0:nqr], rhs=VN[:, 1 + rs + jr, h, 0:65],
                                        start=(jr == 0), stop=False)), reads=[pnn, "VN"], writes=["psO"])
                                P.op("pe", ("matmul", dict(
                                    out=psO[0:nqr, i, 0:65], lhsT=pm[0:16, i, 0:nqr], rhs=VN[0:16, 0, h, 0:65],
                                    start=(nkr == 0), stop=True)), reads=[pmn, "VN"], writes=["psO"])
                            if NSTEP < 4:
                                continue
                            P.op("dve", ("reciprocal", dict(out=small[0:nqr, 24:28], in_=psO[0:nqr, :, 64])),
                                 reads=["psO"], writes=["small"])
                            if NSTEP < 5:
                                continue
                            P.op("dve", ("tensor_tensor", dict(
                                out=out_tile[0:nqr, hg * 256:(hg + 1) * 256].rearrange("p (h e) -> p h e", h=4),
                                in0=psO[0:nqr, :, 0:64],
                                in1=small[0:nqr, 24:28].unsqueeze(2).to_broadcast([nqr, 4, 64]), op=ALU.mult)),
                                reads=["psO", "small"], writes=[out_name])

                    gi = [0]
                    import os
                    NR = int(os.environ.get("NA_ROWS", "32"))
                    for r in range(NR):
                        rs = min(max(r - 4, 0), 24)
                        rho0 = rs - r + 7
                        onr = ONr[r % 2]
                        onrn = "ONr%d" % (r % 2)
                        na_block(16 + 64 * r, 64, rs, rho0, 8, onr, onrn, gi)
                        if stop_after == "C":
                            P.op("act", ("activation", dict(out=sq_junk[0:64, 0:512], in_=onr[0:64, :], func=AF.Copy)),
                                 reads=[onrn], writes=["sq_junk"])
                            P.dma("sp", ("dma_start", dict(out=y_d[s, 64 * r:64 * r + 64, 0:512], in_=sq_junk[0:64, 0:512])),
                                  reads=["sq_junk"], writes=[("y", r)])
                        o2 = ONT2[(r // 2) % 2]
                        o2n = "ONT2_%d" % ((r // 2) % 2)
                        for c4 in range(4):
                            P.op("pe", ("transpose", dict(
                                out=pstC[:, c4, 0:64], in_=onr[0:64, c4 * 128:(c4 + 1) * 128], identity=ident_b[0:64, 0:64])),
                                reads=[onrn, "ident_b"], writes=["pstC"])
                        P.op("act", ("activation", dict(
                            out=o2[:, :, (r % 2) * 64:(r % 2) * 64 + 64], in_=pstC[:, :, 0:64], func=AF.Copy)),
                            reads=["pstC"], writes=[o2n])
                        if r % 2 == 1:
                            t0 = 16 + 64 * (r - 1)
                            P.dma("sp", ("dma_start", dict(
                                out=scr["ONT"][:, :, t0:t0 + 128].rearrange("h p t -> p h t"), in_=o2[:])),
                                reads=[o2n], writes=[("ONT", 1 + r // 2)])
                    import os
                    NDBG = int(os.environ.get("NA_DBG", "9"))
                    NDBG = int(os.environ.get("NA_DBG", "9"))
                    if os.environ.get("NA_DUP"):
                        na_block(16 + 64 * 31, 64, 24, 0, 8, ONr[1], "ONr1", gi)
                    if NDBG >= 2:
                        na_block(0, 64, 0, 15, 1, ONr[0], "ONr0", gi)
                    for c4 in range(4 if NDBG >= 3 else 0):
                        P.op("pe", ("transpose", dict(
                            out=pstC[:, c4, 0:16], in_=ONr[0][0:16, c4 * 128:(c4 + 1) * 128], identity=ident_b[0:16, 0:16])),
                            reads=["ONr0", "ident_b"], writes=["pstC"])
                    if NDBG >= 3:
                        P.op("act", ("activation", dict(out=ONT2[0][:, :, 0:16], in_=pstC[:, :, 0:16], func=AF.Copy)),
                             reads=["pstC"], writes=["ONT2_0"])
                    else:
                        P.op("dve", ("memset", dict(ap=ONT2[0][:, :, 0:16], constant=0.0)), writes=["ONT2_0"])
                    P.dma("sp", ("dma_start", dict(out=scr["ONT"][:, :, 0:16].rearrange("h p t -> p h t"),
                                                     in_=ONT2[0][:, :, 0:16])), reads=["ONT2_0"], writes=[("ONT", 0)])
                    P.barrier()
                if stop_after == "C":
                    break
                with ExitStack() as pd:
                    alloc_wstage(pd)
                    Wa = sb(pd, "Wa", [128, 4, D], BF16)
                    Wb = sb(pd, "Wb", [128, 4, D], BF16)
                    Wo = sb(pd, "Wo", [128, 8, D], BF16)
                    odt = [sb(pd, "odt%d" % i, [128, 4, 128], BF16) for i in range(2)]
                    ont = [sb(pd, "ont%d" % i, [128, 4, 128], BF16) for i in range(2)]
                    gt = [sb(pd, "gt%d" % i, [128, 2048], BF16) for i in range(2)]
                    mf = sb(pd, "mf", [128, 2, D], F32)
                    mb = sb(pd, "mb", [128, D], BF16)
                    mT = sb(pd, "mT", [128, 8, 128], BF16)
                    pya = ps(pd, "pya", [128, 2, 512], F32)
                    pyb = ps(pd, "pyb", [128, 2, 512], F32)
                    pyo = ps(pd, "pyo", [128, 2, 512], F32)
                    pstD = ps(pd, "pstD", [128, 8, 128], BF16)
                    for half in range(2):
                        load_w_bf16(Wa, Wa[:, :, half * 512:(half + 1) * 512],
                                    dr["w_branch_a"][l].rearrange("(kc p) c -> p kc c", p=128)[:, :, half * 512:(half + 1) * 512],
                                    "Wa", (4, 512))
                        load_w_bf16(Wb, Wb[:, :, half * 512:(half + 1) * 512],
                                    dr["w_branch_b"][l].rearrange("(kc p) c -> p kc c", p=128)[:, :, half * 512:(half + 1) * 512],
                                    "Wb", (4, 512))
                        load_w_bf16(Wo, Wo[:, :, half * 512:(half + 1) * 512],
                                    dr["w_out"][l].rearrange("(kc p) c -> p kc c", p=128)[:, :, half * 512:(half + 1) * 512],
                                    "Wo", (8, 512))
                    for t in range(NT):
                        off, n = TILES[t]
                        od, on_, g_ = odt[t % 2], ont[t % 2], gt[t % 2]
                        odn, onn, gn = "odt%d" % (t % 2), "ont%d" % (t % 2), "gt%d" % (t % 2)
                        P.dma("sp", ("dma_start", dict(
                            out=od[:, :, 0:n], in_=scr["ODT"][:, :, off:off + n].rearrange("h p t -> p h t"))),
                            reads=[("ODT", ci) for ci in range(5)], writes=[odn])
                        P.dma("sp", ("dma_start", dict(
                            out=on_[:, :, 0:n], in_=scr["ONT"][:, :, off:off + n].rearrange("h p t -> p h t"))),
                            reads=[("ONT", i) for i in range(17)], writes=[onn])
                        P.dma("sp", ("dma_start", dict(out=g_[0:n, :], in_=scr["G"][off:off + n, :])),
                              reads=[("G", t, cc) for cc in range(6, 10)], writes=[gn])
                        for half in range(2):
                            for c4 in range(4):
                                P.op("pe", ("matmul", dict(
                                    out=pya[0:n, half, :], lhsT=od[:, c4, 0:n], rhs=Wa[:, c4, half * 512:(half + 1) * 512],
                                    start=(c4 == 0), stop=(c4 == 3))), reads=[odn, "Wa"], writes=["pya"])
                            for c4 in range(4):
                                P.op("pe", ("matmul", dict(
                                    out=pyb[0:n, half, :], lhsT=on_[:, c4, 0:n], rhs=Wb[:, c4, half * 512:(half + 1) * 512],
                                    start=(c4 == 0), stop=(c4 == 3))), reads=[onn, "Wb"], writes=["pyb"])
                        P.op("dve", ("tensor_tensor", dict(
                            out=mf[0:n, 0, :], in0=pya[0:n].rearrange("p a b -> p (a b)"), in1=g_[0:n, 0:1024], op=ALU.mult)),
                            reads=["pya", gn], writes=["mf0"])
                        P.op("dve", ("tensor_tensor", dict(
                            out=mf[0:n, 1, :], in0=pyb[0:n].rearrange("p a b -> p (a b)"), in1=g_[0:n, 1024:2048], op=ALU.mult)),
                            reads=["pyb", gn], writes=["mf1"])
                        P.op("dve", ("tensor_tensor", dict(
                            out=mb[0:n, :], in0=mf[0:n, 0, :], in1=mf[0:n, 1, :], op=ALU.add)),
                            reads=["mf0", "mf1"], writes=["mb"])
                        for c in range(8):
                            P.op("pe", ("transpose", dict(
                                out=pstD[:, c, 0:n], in_=mb[0:n, c * 128:(c + 1) * 128], identity=ident_b[0:n, 0:n])),
                                reads=["mb", "ident_b"], writes=["pstD"])
                        P.op("act", ("activation", dict(out=mT[:, :, 0:n], in_=pstD[:, :, 0:n], func=AF.Copy)),
                             reads=["pstD"], writes=["mT"])
                        for half in range(2):
                            for c in range(8):
                                P.op("pe", ("matmul", dict(
                                    out=pyo[0:n, half, :], lhsT=mT[:, c, 0:n], rhs=Wo[:, c, half * 512:(half + 1) * 512],
                                    start=(c == 0), stop=(c == 7))), reads=["mT", "Wo"], writes=["pyo"])
                        P.op("dve", ("tensor_tensor", dict(
                            out=H[0:n, t, :], in0=H[0:n, t, :], in1=pyo[0:n].rearrange("p a b -> p (a b)"), op=ALU.add)),
                            reads=["pyo", ("H", t)], writes=[("H", t)])
                    P.barrier()
                if stop_after == "D":
                    break
                with ExitStack() as pe_:
                    Wq = sb(pe_, "Wq", [128, 8, 2048], BF16)
                    keysT = sb(pe_, "keysT", [128, 16, 128], BF16)
                    gbc = sb(pe_, "gbcE", [128, D], F32)
                    P.dma("sp", ("dma_start", dict(out=gbc[:], in_=dr["norm_ffn"][l:l + 1, :].partition_broadcast(128))),
                          writes=["gbc"])
                    with ExitStack() as pw:
                        alloc_wstage(pw)
                        keysf = sb(pw, "keysf", [128, 16, 128], F32)
                        pkT = ps(pw, "pkT", [128, 4, 128], F32)
                        wq3 = dr["peer_wq"][l].rearrange("(kc p) c -> p kc c", p=128)
                        for c4 in range(4):
                            load_w_bf16(Wq, Wq[:, :, c4 * 512:(c4 + 1) * 512], wq3[:, :, c4 * 512:(c4 + 1) * 512], "Wq", (8, 512))
                        P.dma("sp", ("dma_start", dict(
                            out=keysf[:], in_=dr["peer_keys"][l].rearrange("h p k d -> k (h p) d"))), writes=["keysf"])
                        for g4 in range(4):
                            for i in range(4):
                                gq = 4 * g4 + i
                                P.op("pe", ("transpose", dict(
                                    out=pkT[:, i, :], in_=keysf[:, gq, :], identity=ident_f[:, :])),
                                    reads=["keysf", "ident_f"], writes=["pkT"])
                            P.op("act", ("activation", dict(out=keysT[:, 4 * g4:4 * g4 + 4, :], in_=pkT[:], func=AF.Copy)),
                                 reads=["pkT"], writes=["keysT"])
                        P.barrier()
                    X = sb(pe_, "X", [128, D], F32)
                    CT = sb(pe_, "CT", [128, 8, 128], BF16)
                    qT = sb(pe_, "qT", [128, 16, 128], BF16)
                    ssb = sb(pe_, "ssb", [128, 16, 128], F32)
                    swk = sb(pe_, "swk", [128, 256], F32)
                    vals = sb(pe_, "vals", [128, 16, 16], F32)
                    idxu = sb(pe_, "idxu", [128, 16, 16], U32)
                    idxf = sb(pe_, "idxf", [128, 16, 16], F32)
                    cs_ = sb(pe_, "cs", [128, 8, 256], F32)
                    scv = sb(pe_, "scv", [128, 8, 16], F32)
                    posu = sb(pe_, "posu", [128, 8, 16], U32)
                    posf = sb(pe_, "posf", [128, 8, 16], F32)
                    ai_ = sb(pe_, "ai", [128, 8, 16], I32)
                    af_ = sb(pe_, "af", [128, 8, 16], F32)
                    bf_ = sb(pe_, "bf", [128, 8, 16], F32)
                    msk = sb(pe_, "msk", [128, 8, 16, 16], F32)
                    i1s = sb(pe_, "i1s", [128, 8, 16], F32)
                    i2s = sb(pe_, "i2s", [128, 8, 16], F32)
                    ef = sb(pe_, "ef", [128, 128], F32)
                    ei = [sb(pe_, "ei%d" % i, [128, 128], I32) for i in range(2)]
                    gg = sb(pe_, "gg", [128, 8, 16], F32)
                    hid = sb(pe_, "hid", [128, 128], F32)
                    wgt = sb(pe_, "wgt", [128, 128], F32)
                    djunk = sb(pe_, "djunk", [128, D], BF16)
                    gat = [sb(pe_, "gat%d" % i, [128, D], F32) for i in range(6)]
                    pstE = ps(pe_, "pstE", [128, 8, 128], BF16)
                    pq = [ps(pe_, "pq%d" % i, [128, 4, 128], F32) for i in range(4)]
                    for i in range(2):
                        P.op("pool", ("memset", dict(ap=ei[i][:], constant=0)), writes=["ei%d" % i])
                    gslot = [0]
                    for t in range(NT):
                        off, n = TILES[t]
                        eit = ei[0] if n == 128 else ei[1]
                        ein = "ei0" if n == 128 else "ei1"
                        norm_transpose(t, gbc, CT[:, :, 0:n], pstE, xkeep=X, tag_out="CT")
                        for gq in range(16):
                            pqq = pq[gq // 4]
                            for kc in range(8):
                                P.op("pe", ("matmul", dict(
                                    out=pqq[:, gq % 4, 0:n], lhsT=Wq[:, kc, gq * 128:(gq + 1) * 128], rhs=CT[:, kc, 0:n],
                                    start=(kc == 0), stop=(kc == 7))), reads=["Wq", "CT"], writes=["pq%d" % (gq // 4)])
                        for g4 in range(4):
                            P.op("act", ("activation", dict(
                                out=qT[:, 4 * g4:4 * g4 + 4, 0:n], in_=pq[g4][:, :, 0:n], func=AF.Copy)),
                                reads=["pq%d" % g4], writes=["qT"])
                        for gq in range(16):
                            pqq = pq[gq // 4]
                            P.op("pe", ("matmul", dict(
                                out=pqq[0:n, gq % 4, :], lhsT=qT[:, gq, 0:n], rhs=keysT[:, gq, :], start=True, stop=True)),
                                reads=["qT", "keysT"], writes=["pq%d" % (gq // 4)])
                        for g4 in range(4):
                            P.op("act", ("activation", dict(
                                out=ssb[0:n, 4 * g4:4 * g4 + 4, :], in_=pq[g4][0:n, :, :], func=AF.Copy)),
                                reads=["pq%d" % g4], writes=["ssb"])
                        for gq in range(16):
                            P.op("dve", ("max", dict(out=vals[0:n, gq, 0:8], in_=ssb[0:n, gq, :])),
                                 reads=["ssb"], writes=["vals"])
                            P.op("dve", ("max_index", dict(out=idxu[0:n, gq, 0:8], in_max=vals[0:n, gq, 0:8],
                                                                         in_values=ssb[0:n, gq, :])),
                                 reads=["ssb", "vals"], writes=["idxu"])
                            P.op("dve", ("match_replace", dict(
                                out=swk[0:n, 0:128], in_to_replace=vals[0:n, gq, 0:8], in_values=ssb[0:n, gq, :],
                                imm_value=-1e30)), reads=["ssb", "vals"], writes=["swk"])
                            P.op("dve", ("max", dict(out=vals[0:n, gq, 8:16], in_=swk[0:n, 0:128])),
                                 reads=["swk"], writes=["vals"])
                            P.op("dve", ("max_index", dict(out=idxu[0:n, gq, 8:16], in_max=vals[0:n, gq, 8:16],
                                                                         in_values=swk[0:n, 0:128])),
                                 reads=["swk", "vals"], writes=["idxu"])
                        P.op("dve", ("tensor_copy", dict(out=idxf[0:n], in_=idxu[0:n])), reads=["idxu"], writes=["idxf"])
                        v4 = vals[:].rearrange("p (h two) k -> p h two k", two=2)
                        i4 = idxf[:].rearrange("p (h two) k -> p h two k", two=2)
                        P.op("dve", ("tensor_tensor", dict(
                            out=cs_[0:n].rearrange("p h (a b) -> p h a b", a=16),
                            in0=v4[0:n, :, 0, :].unsqueeze(3).to_broadcast([n, 8, 16, 16]),
                            in1=v4[0:n, :, 1, :].unsqueeze(2).to_broadcast([n, 8, 16, 16]), op=ALU.add)),
                            reads=["vals"], writes=["cs"])
                        for h in range(8):
                            P.op("dve", ("max", dict(out=scv[0:n, h, 0:8], in_=cs_[0:n, h, :])),
                                 reads=["cs"], writes=["scv"])
                            P.op("dve", ("max_index", dict(out=posu[0:n, h, 0:8], in_max=scv[0:n, h, 0:8],
                                                                       in_values=cs_[0:n, h, :])),
                                 reads=["cs", "scv"], writes=["posu"])
                            P.op("dve", ("match_replace", dict(
                                out=swk[0:n, :], in_to_replace=scv[0:n, h, 0:8], in_values=cs_[0:n, h, :], imm_value=-1e30)),
                                reads=["cs", "scv"], writes=["swk"])
                            P.op("dve", ("max", dict(out=scv[0:n, h, 8:16], in_=swk[0:n, :])),
                                 reads=["swk"], writes=["scv"])
                            P.op("dve", ("max_index", dict(out=posu[0:n, h, 8:16], in_max=scv[0:n, h, 8:16],
                                                                       in_values=swk[0:n, :])),
                                 reads=["swk", "scv"], writes=["posu"])
                        P.op("dve", ("tensor_copy", dict(out=posf[0:n], in_=posu[0:n])), reads=["posu"], writes=["posf"])
                        P.op("dve", ("tensor_scalar", dict(out=ai_[0:n], in0=posf[0:n], scalar1=-7.5, scalar2=1.0 / 16,
                                                                   op0=ALU.add, op1=ALU.mult)), reads=["posf"], writes=["ai"])
                        P.op("dve", ("tensor_copy", dict(out=af_[0:n], in_=ai_[0:n])), reads=["ai"], writes=["af"])
                        P.op("dve", ("scalar_tensor_tensor", dict(
                            out=bf_[0:n], in0=af_[0:n], scalar=-16.0, in1=posf[0:n], op0=ALU.mult, op1=ALU.add)),
                            reads=["af", "posf"], writes=["bf"])
                        for (src_, two, dst_, dn) in ((af_, 0, i1s, "i1s"), (bf_, 1, i2s, "i2s")):
                            P.op("dve", ("tensor_tensor", dict(
                                out=msk[0:n], in0=src_[0:n].unsqueeze(3).to_broadcast([n, 8, 16, 16]),
                                in1=iota16[0:n, :].unsqueeze(1).unsqueeze(1).to_broadcast([n, 8, 16, 16]), op=ALU.is_equal)),
                                reads=["af", "bf", "iota16"], writes=["msk"])
                            P.op("dve", ("tensor_tensor", dict(
                                out=msk[0:n], in0=msk[0:n],
                                in1=i4[0:n, :, two, :].unsqueeze(2).to_broadcast([n, 8, 16, 16]), op=ALU.mult)),
                                reads=["msk", "idxf"], writes=["msk"])
                            P.op("dve", ("tensor_reduce", dict(
                                out=dst_[0:n], in_=msk[0:n], axis=AX.X, op=ALU.add)), reads=["msk"], writes=[dn])
                        P.op("dve", ("scalar_tensor_tensor", dict(
                            out=ef[0:n, :].rearrange("p (h k) -> p h k", h=8), in0=i1s[0:n], scalar=128.0, in1=i2s[0:n],
                            op0=ALU.mult, op1=ALU.add)), reads=["i1s", "i2s"], writes=["ef"])
                        P.op("dve", ("tensor_scalar", dict(out=eit[0:n, :], in0=ef[0:n, :], scalar1=float(l * NEXP), scalar2=None, op0=ALU.add)),
                             reads=["ef"], writes=[ein])
                        P.op("dve", ("tensor_tensor", dict(
                            out=gg[0:n], in0=scv[0:n], in1=scv[0:n, :, 0:1].to_broadcast([n, 8, 16]), op=ALU.subtract)),
                            reads=["scv"], writes=["gg"])
                        P.op("act", ("activation", dict(out=gg[0:n], in_=gg[0:n], func=AF.Exp)),
                             reads=["gg"], writes=["gg"])
                        P.op("dve", ("tensor_reduce", dict(out=small[0:n, 32:40], in_=gg[0:n], axis=AX.X, op=ALU.add)),
                             reads=["gg"], writes=["small"])
                        P.op("dve", ("reciprocal", dict(out=small[0:n, 32:40], in_=small[0:n, 32:40])),
                             reads=["small"], writes=["small"])
                        P.op("dve", ("tensor_tensor", dict(
                            out=gg[0:n], in0=gg[0:n], in1=small[0:n, 32:40].unsqueeze(2).to_broadcast([n, 8, 16]), op=ALU.mult)),
                            reads=["gg", "small"], writes=["gg"])
                        for j in range(128):
                            sl = gslot[0] % 6
                            gslot[0] += 1
                            P.dma("pool", ("indirect_dma_start", dict(
                                out=gat[sl][:], out_offset=None, in_=dr["peer_u"].rearrange("l e d -> (l e) d"),
                                in_offset=bass.IndirectOffsetOnAxis(ap=eit[:, j:j + 1], axis=0))),
                                reads=[ein], writes=["gat%d" % sl], semkey=gsem[sl])
                            P.op("dve", ("scalar_tensor_tensor", dict(
                                out=djunk[0:n, :], in0=gat[sl][0:n, :], scalar=1.0, in1=X[0:n, :], op0=ALU.mult, op1=ALU.mult,
                                accum_out=hid[0:n, j:j + 1])), reads=["gat%d" % sl, "abuf"], writes=["djunk", ("hid", j)])
                        P.op("act", ("activation", dict(out=wgt[0:n, :], in_=hid[0:n, :], func=AF.Gelu)),
                             reads=[("hid", j) for j in range(128)], writes=["wgt"])
                        P.op("dve", ("tensor_tensor", dict(
                            out=wgt[0:n, :], in0=wgt[0:n, :], in1=gg[0:n].rearrange("p h k -> p (h k)"), op=ALU.mult)),
                            reads=["wgt", "gg"], writes=["wgt"])
                        for j in range(128):
                            sl = gslot[0] % 6
                            gslot[0] += 1
                            P.dma("pool", ("indirect_dma_start", dict(
                                out=gat[sl][:], out_offset=None, in_=dr["peer_v"].rearrange("l e d -> (l e) d"),
                                in_offset=bass.IndirectOffsetOnAxis(ap=eit[:, j:j + 1], axis=0))),
                                reads=[ein], writes=["gat%d" % sl], semkey=gsem[sl])
                            P.op("dve", ("scalar_tensor_tensor", dict(
                                out=H[0:n, t, :], in0=gat[sl][0:n, :], scalar=wgt[0:n, j:j + 1], in1=H[0:n, t, :],
                                op0=ALU.mult, op1=ALU.add)), reads=["gat%d" % sl, "wgt", ("H", t)], writes=[("H", t)])
                    P.barrier()
            if stop_after in ("B", "C"):
                continue
            if stop_after is not None:
                for t in range(1, NT):
                    P.dma("sp", ("dma_start", dict(out=y_d[s, 128 * (t - 1):128 * t, :], in_=H[:, t, :])),
                          reads=[("H", t)], writes=[("y", s, t)])
                continue
            for t in range(1, NT):
                rms_rstd(H[:, t, :], 128, 8, ("H", t))
                P.op("dve", ("scalar_tensor_tensor", dict(
                    out=sq_junk[:, :], in0=H[:, t, :], scalar=small[:, 8:9], in1=gfin_bc[:, :],
                    op0=ALU.mult, op1=ALU.mult)), reads=[("H", t), "small", "gfin_bc"], writes=["sq_junk"])
                P.dma("sp", ("dma_start", dict(out=y_d[s, 128 * (t - 1):128 * t, :], in_=sq_junk[:, :])),
                      reads=["sq_junk"], writes=[("y", s, t)])
            P.barrier()
        P.barrier()
        P.emit()
    return nc, P


def make_in_maps(inputs, nseq=SEQ_PER_CORE, ncores=NCORES):
    consts = host_consts()
    xp = np.asarray(inputs["x_prompt"], dtype=np.float32)
    xs = np.asarray(inputs["x_sample"], dtype=np.float32)
    allx = np.concatenate([xp, xs], axis=0)
    maps = []
    for c in range(ncores):
        m = {"x": np.ascontiguousarray(allx[c * nseq:(c + 1) * nseq])}
        for k, shp in WEIGHT_SHAPES.items():
            m[k] = np.ascontiguousarray(np.asarray(inputs[k], dtype=np.float32).reshape(shp))
        m.update(consts)
        maps.append(m)
    return maps


def kernel(**inputs):
    nc, _ = build()
    maps = make_in_maps(inputs)
    res = run_bass_kernel_spmd(nc, maps, core_ids=list(range(NCORES)))
    ys = np.concatenate([r["y"] for r in res.results], axis=0)
    nb = np.asarray(inputs["x_prompt"]).shape[0]
    return (np.ascontiguousarray(ys[:nb]), np.ascontiguousarray(ys[nb:]))
```

```python
import math
import numpy as np
from contextlib import ExitStack
import concourse.bass as bass
import concourse.mybir as mybir
from concourse.bass_utils import run_bass_kernel_spmd

F32 = mybir.dt.float32
BF16 = mybir.dt.bfloat16
I32 = mybir.dt.int32
U32 = mybir.dt.uint32
ALU = mybir.AluOpType
AF = mybir.ActivationFunctionType
AX = mybir.AxisListType

D = 1024
NMETA = 16
SEQ = 2048
L = SEQ + NMETA
DEPTH = 4
NCORES = 8
SEQ_PER_CORE = 5
EPS = 1e-6
TILES = [(0, 16)] + [(16 + 128 * i, 128) for i in range(16)]
NT = len(TILES)
QCHUNKS = [(16 + 512 * c, 512, [1 + 4 * c + j for j in range(4)]) for c in range(4)] + [(0, 16, [0])]
NEXP = 16384
ENGS = ("pe", "act", "dve", "pool", "sp")


class Prog:
    def __init__(self, nc, es, n_dma_sems=24):
        self.nc = nc
        self.es = es
        self.q = {e: [] for e in ENGS}
        self.cnt = {e: 0 for e in ENGS}
        self.waited = {e: {} for e in ENGS}
        self.res_w = {}
        self.res_r = {}
        self.sem = {}
        for e in ENGS:
            self.sem[e] = es.enter_context(nc.semaphore("s_" + e))
        self.dma_sems = []
        self.dma_val = {}
        for i in range(n_dma_sems):
            self.dma_sems.append(self.new_dma_sem("dma%d" % i))
        self.dma_rr = 0
        self.ninstr = 0

    def new_dma_sem(self, name):
        self.sem[name] = self.es.enter_context(self.nc.semaphore("s_" + name))
        self.dma_val[name] = 0
        return name

    def _deps(self, eng, reads, writes):
        ev = {}
        for r in reads:
            e = self.res_w.get(r)
            if e is not None and ev.get(e[0], 0) < e[1]:
                ev[e[0]] = e[1]
        for w in writes:
            e = self.res_w.get(w)
            if e is not None and ev.get(e[0], 0) < e[1]:
                ev[e[0]] = e[1]
            rr = self.res_r.get(w)
            if rr:
                for k, v in rr.items():
                    if ev.get(k, 0) < v:
                        ev[k] = v
        waits = []
        wd = self.waited[eng]
        for k, v in ev.items():
            if k == "pe" and eng == "pe":
                continue
            if wd.get(k, 0) < v:
                wd[k] = v
                waits.append((k, v))
        return waits

    def _commit(self, event, reads, writes):
        for w in writes:
            self.res_w[w] = event
            self.res_r[w] = {}
        k, v = event
        for r in reads:
            d = self.res_r.setdefault(r, {})
            if d.get(k, 0) < v:
                d[k] = v

    def op(self, eng, fn, reads=(), writes=()):
        waits = self._deps(eng, reads, writes)
        self.cnt[eng] += 1
        event = (eng, self.cnt[eng])
        self.q[eng].append((waits, fn, (eng, 1)))
        self._commit(event, reads, writes)
        self.ninstr += 1

    def dma(self, eng, fn, reads=(), writes=(), semkey=None):
        if semkey is None:
            semkey = self.dma_sems[self.dma_rr % len(self.dma_sems)]
            self.dma_rr += 1
        waits = self._deps(eng, reads, writes)
        pv = self.dma_val[semkey]
        wd = self.waited[eng]
        if pv > 0 and wd.get(semkey, 0) < pv:
            wd[semkey] = pv
            waits.append((semkey, pv))
        self.dma_val[semkey] = pv + 16
        event = (semkey, pv + 16)
        self.q[eng].append((waits, fn, (semkey, 16)))
        self._commit(event, reads, writes)
        self.ninstr += 1
        return event

    def barrier(self):
        tot = dict(self.dma_val)
        for e in ENGS:
            tot[e] = self.cnt[e]
        for eng in ENGS:
            waits = []
            wd = self.waited[eng]
            for k, v in tot.items():
                if v > 0 and k != eng and wd.get(k, 0) < v:
                    wd[k] = v
                    waits.append((k, v))
            if waits:
                self.q[eng].append((waits, None, None))
        self.res_w = {}
        self.res_r = {}

    def emit(self):
        nc = self.nc
        sem = self.sem
        q = self.q
        with nc.Block() as block:
            def run(engobj, lst):
                for waits, fn, inc in lst:
                    for k, v in waits:
                        engobj.wait_ge(sem[k], v)
                    if fn is not None:
                        getattr(engobj, fn[0])(**fn[1]).then_inc(sem[inc[0]], inc[1])

            @block.tensor
            def _(e):
                run(e, q["pe"])

            @block.scalar
            def _(e):
                run(e, q["act"])

            @block.vector
            def _(e):
                run(e, q["dve"])

            @block.gpsimd
            def _(e):
                run(e, q["pool"])

            @block.sync
            def _(e):
                run(e, q["sp"])


def host_consts():
    c = {}
    c["c_ident"] = np.eye(128, dtype=np.float32)
    c["c_iota16"] = np.tile(np.arange(16, dtype=np.float32)[None, :], (128, 1))
    half = 32
    inv = 1.0 / (10000.0 ** (np.arange(half, dtype=np.float64) * 2.0 / 64.0))
    c["c_invf"] = np.tile(inv, 4).reshape(128, 1).astype(np.float32) / np.float32(2 * math.pi)
    cc = np.arange(64)
    cs = np.clip(cc - 8, 0, 48)
    k = np.arange(64)
    c["c_colmask"] = ((k[:, None] >= cs[None, :]) & (k[:, None] < cs[None, :] + 16)).astype(np.float32)
    c["c_antiI"] = np.ascontiguousarray(np.eye(31, dtype=np.float32)[::-1])
    c["c_pos"] = np.tile(np.arange(L, dtype=np.float32)[None, :], (128, 1))
    return c


CONST_SHAPES = {"c_ident": [128, 128], "c_iota16": [128, 16], "c_invf": [128, 1],
                "c_colmask": [64, 64], "c_antiI": [31, 31], "c_pos": [128, L]}

WEIGHT_SHAPES = {
    "meta_tokens": [16, D], "norm_mix": [4, D], "w_in": [4, D, 5120],
    "lambda_q1": [4, 64], "lambda_k1": [4, 64], "lambda_q2": [4, 64], "lambda_k2": [4, 64],
    "subln_gain": [4, 128], "na_rpb": [4, 8, 15, 31], "w_branch_a": [4, 512, D],
    "w_branch_b": [4, 512, D], "w_out": [4, D, D], "norm_ffn": [4, D],
    "peer_wq": [4, D, 2048], "peer_keys": [4, 8, 2, 128, 128],
    "peer_u": [4, NEXP, D], "peer_v": [4, NEXP, D], "norm_final": [1, D],
}


def build(nseq=SEQ_PER_CORE, depth=DEPTH, dbg=False, stop_after=None):
    nc = bass.Bass("TRN2", target_bir_lowering=False)
    dr = {}
    dr["x"] = nc.dram_tensor("x", [nseq, SEQ, D], F32, kind="ExternalInput").ap()
    for k, shp in WEIGHT_SHAPES.items():
        dr[k] = nc.dram_tensor(k, shp, F32, kind="ExternalInput").ap()
    for k, shp in CONST_SHAPES.items():
        dr[k] = nc.dram_tensor(k, shp, F32, kind="ExternalInput").ap()
    y_d = nc.dram_tensor("y", [nseq, SEQ, D], F32, kind="ExternalOutput").ap()
    scr = {}
    for nm in ("QTda", "KTda", "QTna", "KTna", "ODT", "ONT"):
        scr[nm] = nc.dram_tensor("scr_" + nm, [4, 128, L], BF16).ap()
    scr["Vda"] = nc.dram_tensor("scr_Vda", [L, 4, 130], BF16).ap()
    scr["Vna"] = nc.dram_tensor("scr_Vna", [L, 8, 66], BF16).ap()
    scr["G"] = nc.dram_tensor("scr_G", [L, 2048], BF16).ap()
    PW = 160
    Dsk_t = nc.dram_tensor("scr_Dsk", [4, 120, 64, PW], F32)
    scr["Dsk"] = Dsk_t.ap()
    scr["E"] = nc.dram_tensor("scr_E", [4, 64, 7680], BF16).ap()
    scr["UV"] = nc.dram_tensor("scr_UV", [4 * NEXP, 2048], BF16).ap()

    with ExitStack() as es:
        P = Prog(nc, es)
        gsem = [P.new_dma_sem("gat%d" % i) for i in range(8)]

        uid = [0]

        def sb(es_, name, shape, dt):
            uid[0] += 1
            if shape[0] < 128:
                t_ = es_.enter_context(nc.sbuf_tensor("sb_%s_%d" % (name, uid[0]), [128] + list(shape[1:]), dt))
                return t_[0:shape[0]]
            return es_.enter_context(nc.sbuf_tensor("sb_%s_%d" % (name, uid[0]), shape, dt))

        def ps(es_, name, shape, dt):
            uid[0] += 1
            return es_.enter_context(nc.psum_tensor("ps_%s_%d" % (name, uid[0]), shape, dt))

        H = sb(es, "H", [128, NT, D], F32)
        ident_f = sb(es, "ident_f", [128, 128], F32)
        ident_b = sb(es, "ident_b", [128, 128], BF16)
        iota16 = sb(es, "iota16", [128, 16], F32)
        cosT = sb(es, "cosT", [128, L], BF16)
        sinT = sb(es, "sinT", [128, L], BF16)
        lam = sb(es, "lam", [128, 8], F32)
        gain_bc = sb(es, "gain_bc", [128, 4, 128], F32)
        gfin_bc = sb(es, "gfin_bc", [128, D], F32)
        small = sb(es, "small", [128, 64], F32)

        P.dma("sp", ("dma_start", dict(out=ident_f[:], in_=dr["c_ident"])), writes=["ident_f"])
        P.dma("sp", ("dma_start", dict(out=iota16[:], in_=dr["c_iota16"])), writes=["iota16"])
        P.op("dve", ("tensor_copy", dict(out=ident_b[:], in_=ident_f[:])), reads=["ident_f"], writes=["ident_b"])
        P.dma("sp", ("dma_start", dict(out=gfin_bc[:], in_=dr["norm_final"].partition_broadcast(128))),
              writes=["gfin_bc"])
        with ExitStack() as s1:
            invf = sb(s1, "invf", [128, 1], F32)
            pos = sb(s1, "pos", [128, L], F32)
            yy = sb(s1, "yy", [128, L], F32)
            yi = sb(s1, "yi", [128, L], I32)
            yf = sb(s1, "yf", [128, L], F32)
            lq = sb(s1, "lq", [128, 4, 4, 64], F32)
            junk = sb(s1, "junk", [128, 64], F32)
            P.dma("sp", ("dma_start", dict(out=invf[:], in_=dr["c_invf"])), writes=["invf"])
            P.dma("sp", ("dma_start", dict(out=pos[:], in_=dr["c_pos"])), writes=["pos"])
            for which, tab in ((0, sinT), (1, cosT)):
                P.op("dve", ("tensor_scalar", dict(
                    out=yy[:], in0=pos[:], scalar1=invf[:, 0:1], scalar2=0.25 * which,
                    op0=ALU.mult, op1=ALU.add)), reads=["pos", "invf"], writes=["yy"])
                P.op("dve", ("tensor_copy", dict(out=yi[:], in_=yy[:])), reads=["yy"], writes=["yi"])
                P.op("dve", ("tensor_copy", dict(out=yf[:], in_=yi[:])), reads=["yi"], writes=["yf"])
                P.op("dve", ("tensor_tensor", dict(out=yy[:], in0=yy[:], in1=yf[:], op=ALU.subtract)),
                     reads=["yy", "yf"], writes=["yy"])
                P.op("dve", ("tensor_scalar", dict(out=yy[:], in0=yy[:], scalar1=0.5, scalar2=-0.5,
                                                      op0=ALU.min, op1=ALU.max)), reads=["yy"], writes=["yy"])
                P.op("act", ("activation", dict(out=tab[:], in_=yy[:], func=AF.Sin,
                                                            scale=2 * math.pi)),
                     reads=["yy"], writes=["tab%d" % which])
            for i, nm in enumerate(("lambda_q1", "lambda_k1", "lambda_q2", "lambda_k2")):
                for l in range(4):
                    P.dma("sp", ("dma_start", dict(
                        out=lq[:, i, l, :], in_=dr[nm][l:l + 1, :].partition_broadcast(128))),
                        writes=["lq"])
            for l in range(4):
                lam_init = 0.8 - 0.6 * math.exp(-0.3 * l)
                P.op("dve", ("scalar_tensor_tensor", dict(
                    out=junk[:], in0=lq[:, 0, l, :], scalar=1.0, in1=lq[:, 1, l, :],
                    op0=ALU.mult, op1=ALU.mult, accum_out=small[:, 0:1])),
                    reads=["lq"], writes=["junk", "small"])
                P.op("dve", ("scalar_tensor_tensor", dict(
                    out=junk[:], in0=lq[:, 2, l, :], scalar=1.0, in1=lq[:, 3, l, :],
                    op0=ALU.mult, op1=ALU.mult, accum_out=small[:, 1:2])),
                    reads=["lq", "small"], writes=["junk", "small"])
                P.op("act", ("activation", dict(out=small[:, 2:4], in_=small[:, 0:2], func=AF.Exp)),
                     reads=["small"], writes=["small"])
                P.op("dve", ("scalar_tensor_tensor", dict(
                    out=lam[:, l:l + 1], in0=small[:, 2:3], scalar=lam_init, in1=small[:, 3:4],
                    op0=ALU.add, op1=ALU.subtract)), reads=["small"], writes=["lam"])
                P.op("dve", ("tensor_scalar", dict(
                    out=lam[:, 4 + l:5 + l], in0=lam[:, l:l + 1], scalar1=-1.0, scalar2=None, op0=ALU.mult)),
                    reads=["lam"], writes=["lam"])
                P.dma("sp", ("dma_start", dict(
                    out=gain_bc[:, l, :], in_=dr["subln_gain"][l:l + 1, :].partition_broadcast(128))),
                    writes=["gain_bc"])
                P.op("dve", ("tensor_scalar", dict(
                    out=gain_bc[:, l, :], in0=gain_bc[:, l, :], scalar1=1.0 - lam_init, scalar2=None, op0=ALU.mult)),
                    reads=["gain_bc"], writes=["gain_bc"])
            P.barrier()
        with ExitStack() as s1:
          if stop_after != "S0":
              rT = sb(s1, "rT", [31, 120], F32)
              antiI = sb(s1, "antiI", [31, 31], F32)
              colmask = sb(s1, "colmask", [64, 64], F32)
              Ppad = sb(s1, "Ppad", [120, PW], F32)
              Esk = sb(s1, "Esk", [64, 120, 64], F32)
              Ebf = sb(s1, "Ebf", [64, 120, 64], BF16)
              prev = ps(s1, "prev", [128, 512], F32)
              P.dma("sp", ("dma_start", dict(out=antiI[:], in_=dr["c_antiI"])), writes=["antiI"])
              P.dma("sp", ("dma_start", dict(out=colmask[:], in_=dr["c_colmask"])), writes=["colmask"])
              for l in range(depth):
                  P.dma("sp", ("dma_start", dict(
                      out=rT[:], in_=dr["na_rpb"][l].rearrange("h r j -> j (h r)"), allow_slow_non_contiguous=True)),
                      writes=["rT"])
                  P.op("pe", ("matmul", dict(out=prev[0:120, 0:31], lhsT=rT[:, :], rhs=antiI[:, :],
                                                start=True, stop=True)), reads=["rT", "antiI"], writes=["prev"])
                  P.op("dve", ("memset", dict(ap=Ppad[:], constant=-30000.0)), writes=["Ppad"])
                  P.op("dve", ("tensor_copy", dict(out=Ppad[:, 64:95], in_=prev[0:120, 0:31])),
                       reads=["prev"], writes=["Ppad"])
                  P.op("act", ("activation", dict(out=Ppad[:], in_=Ppad[:], func=AF.Exp)),
                       reads=["Ppad"], writes=["Ppad"])
                  P.dma("sp", ("dma_start", dict(
                      out=scr["Dsk"][l], in_=Ppad[:].unsqueeze(1).to_broadcast([120, 64, PW]))),
                      reads=["Ppad"], writes=["Dsk"])
                  src = bass.AP(Dsk_t, l * 120 * 64 * PW + 79, [[PW - 1, 64], [64 * PW, 120], [1, 64]])
                  P.dma("sp", ("dma_start", dict(out=Esk[:], in_=src)), reads=["Dsk"], writes=["Esk"])
                  P.op("dve", ("tensor_tensor", dict(
                      out=Ebf[:], in0=Esk[:], in1=colmask[:].unsqueeze(1).to_broadcast([64, 120, 64]),
                      op=ALU.mult)), reads=["Esk", "colmask"], writes=["Ebf"])
                  P.dma("sp", ("dma_start", dict(
                      out=scr["E"][l], in_=Ebf[:].rearrange("k a q -> k (a q)"))), reads=["Ebf"], writes=["E_d"])
              P.barrier()

        with ExitStack() as s1:
          if stop_after not in ("S0", "S"):
            u32 = [sb(s1, "u32_%d" % i, [128, 4, D], F32) for i in range(2)]
            v32 = [sb(s1, "v32_%d" % i, [128, 4, D], F32) for i in range(2)]
            uv16 = [sb(s1, "uv16_%d" % i, [128, 4, 2, D], BF16) for i in range(2)]
            it = 0
            for l in range(depth):
                for blk in range(NEXP // 512):
                    i = it % 2
                    it += 1
                    e0 = blk * 512
                    P.dma("sp", ("dma_start", dict(
                        out=u32[i][:], in_=dr["peer_u"][l, e0:e0 + 512, :].rearrange("(p k) d -> p k d", k=4))),
                        writes=["u32_%d" % i])
                    P.dma("sp", ("dma_start", dict(
                        out=v32[i][:], in_=dr["peer_v"][l, e0:e0 + 512, :].rearrange("(p k) d -> p k d", k=4))),
                        writes=["v32_%d" % i])
                    P.op("dve", ("tensor_copy", dict(out=uv16[i][:, :, 0, :], in_=u32[i][:])),
                         reads=["u32_%d" % i], writes=["uv16u_%d" % i])
                    P.op("act", ("activation", dict(out=uv16[i][:, 0:2, 1, :], in_=v32[i][:, 0:2, :], func=AF.Copy)),
                         reads=["v32_%d" % i], writes=["uv16va_%d" % i])
                    P.op("pool", ("tensor_copy", dict(out=uv16[i][:, 2:4, 1, :], in_=v32[i][:, 2:4, :])),
                         reads=["v32_%d" % i], writes=["uv16vb_%d" % i])
                    r0 = l * NEXP + e0
                    P.dma("sp", ("dma_start", dict(
                        out=scr["UV"][r0:r0 + 512, :].rearrange("(p k) c -> p k c", k=4),
                        in_=uv16[i][:].rearrange("p k two d -> p k (two d)"))),
                        reads=["uv16u_%d" % i, "uv16va_%d" % i, "uv16vb_%d" % i], writes=[("UV", l, blk)])
            P.barrier()
        def rms_rstd(Xap, n, col, tag):
            width = Xap.shape[-1]
            P.op("act", ("activation", dict(out=sq_junk[0:n, 0:width], in_=Xap, func=AF.Square,
                                               accum_out=small[0:n, col:col + 1])),
                 reads=[tag], writes=["sq_junk", "small"])
            P.op("dve", ("tensor_scalar", dict(out=small[0:n, col:col + 1], in0=small[0:n, col:col + 1],
                                                  scalar1=1.0 / width, scalar2=EPS, op0=ALU.mult, op1=ALU.add)),
                 reads=["small"], writes=["small"])
            P.op("act", ("activation", dict(out=small[0:n, col:col + 1], in_=small[0:n, col:col + 1],
                                               func=AF.Sqrt)), reads=["small"], writes=["small"])
            P.op("dve", ("reciprocal", dict(out=small[0:n, col:col + 1], in_=small[0:n, col:col + 1])),
                 reads=["small"], writes=["small"])

        sq_junk = sb(es, "sq_junk", [128, D], F32)
        wstage = [None, None]
        wst_i = [0]

        def alloc_wstage(es_):
            for i in range(2):
                wstage[i] = sb(es_, "wstage%d" % i, [128, 4096], F32)

        def load_w_bf16(dst_tile, dst_ap, src_ap, dst_name, shape3):
            a, b = shape3
            i = wst_i[0] % 2
            wst_i[0] += 1
            st = wstage[i]
            stv = st[:, 0:a * b].rearrange("p (a b) -> p a b", a=a)
            P.dma("sp", ("dma_start", dict(out=stv, in_=src_ap)), writes=["wstage%d" % i])
            P.op("pool", ("tensor_copy", dict(out=dst_ap, in_=stv)), reads=["wstage%d" % i], writes=[dst_name])
            return stv, i

        def norm_transpose(t, gbc, AT_dst, pst, xkeep=None, tag_out="AT"):
            off, n = TILES[t]
            rms_rstd(H[0:n, t, :], n, 8, ("H", t))
            dst = xkeep if xkeep is not None else abuf
            P.op("dve", ("scalar_tensor_tensor", dict(
                out=dst[0:n, :], in0=H[0:n, t, :], scalar=small[0:n, 8:9], in1=gbc[0:n, :],
                op0=ALU.mult, op1=ALU.mult)), reads=[("H", t), "small", "gbc"], writes=["abuf"])
            if xkeep is not None:
                P.op("act", ("activation", dict(out=abuf[0:n, :], in_=xkeep[0:n, :], func=AF.Copy)),
                     reads=["abuf"], writes=["abuf_b"])
                rtag = "abuf_b"
            else:
                rtag = "abuf"
            for c in range(8):
                P.op("pe", ("transpose", dict(out=pst[:, c, 0:n], in_=abuf[0:n, c * 128:(c + 1) * 128],
                                                      identity=ident_b[0:n, 0:n])),
                     reads=[rtag, "ident_b"], writes=["pst"])
            P.op("act", ("activation", dict(out=AT_dst, in_=pst[:, :, 0:n], func=AF.Copy)),
                 reads=["pst"], writes=[tag_out])

        abuf = sb(es, "abuf", [128, D], BF16)

        for s in range(nseq if stop_after not in ("S0", "S") else 0):
            P.dma("sp", ("dma_start", dict(out=H[0:16, 0, :], in_=dr["meta_tokens"])), writes=[("H", 0)])
            for t in range(1, NT):
                P.dma("sp", ("dma_start", dict(out=H[:, t, :], in_=dr["x"][s, 128 * (t - 1):128 * t, :])),
                      writes=[("H", t)])
            for l in range(depth):
                with ExitStack() as pa:
                    alloc_wstage(pa)
                    AT = sb(pa, "AT", [128, 8, L], BF16)
                    gbc = sb(pa, "gbc", [128, D], F32)
                    abuf_f = None
                    wbf = [sb(pa, "wbf%d" % i, [128, 8, 512], BF16) for i in range(2)]
                    wrot = sb(pa, "wrot", [128, 8, 512], BF16)
                    ev_f = sb(pa, "ev_f", [128, 2, 512], F32)
                    ev_b = [sb(pa, "ev_b%d" % i, [128, 520], BF16) for i in range(2)]
                    vda_b = [sb(pa, "vda_b%d" % i, [128, 4, 130], BF16) for i in range(2)]
                    vna_b = [sb(pa, "vna_b%d" % i, [128, 8, 66], BF16) for i in range(2)]
                    pst = ps(pa, "pstA", [128, 8, 128], BF16)
                    pz = [ps(pa, "pzA%d" % i, [128, 512], F32) for i in range(4)]
                    P.dma("sp", ("dma_start", dict(out=gbc[:], in_=dr["norm_mix"][l:l + 1, :].partition_broadcast(128))),
                          writes=["gbc"])
                    for i in range(2):
                        P.op("pool", ("memset", dict(ap=vda_b[i][:], constant=1.0)), writes=["vda_b%d" % i])
                        P.op("pool", ("memset", dict(ap=vna_b[i][:], constant=1.0)), writes=["vna_b%d" % i])
                    for t in range(NT):
                        off, n = TILES[t]
                        norm_transpose(t, gbc, AT[:, :, off:off + n], pst, tag_out=("AT", t))
                    AT_all = [("AT", t) for t in range(NT)]
                    w3 = dr["w_in"][l].rearrange("(kc p) c -> p kc c", p=128)
                    ecnt = [0]
                    for cc in range(10):
                        wb = wbf[cc % 2]
                        wname = "wbf%d" % (cc % 2)
                        stv, sti = load_w_bf16(wb, wb[:], w3[:, :, cc * 512:(cc + 1) * 512], wname, (8, 512))
                        if cc in (0, 1):
                            st5 = stv.rearrange("p a (g two h) -> p a g two h", two=2, h=32)
                            wr5 = wrot[:].rearrange("p a (g two h) -> p a g two h", two=2, h=32)
                            for kc in range(8):
                                P.op("pool", ("tensor_scalar", dict(
                                    out=wr5[:, kc, :, 0, :], in0=st5[:, kc, :, 1, :], scalar1=-1.0, scalar2=None,
                                    op0=ALU.mult)), reads=["wstage%d" % sti], writes=["wrot"])
                                P.op("pool", ("tensor_copy", dict(
                                    out=wr5[:, kc, :, 1, :], in_=st5[:, kc, :, 0, :])),
                                    reads=["wstage%d" % sti], writes=["wrot"])
                        if cc in (0, 1, 3, 4):
                            dstn = {0: "QTda", 1: "KTda", 3: "QTna", 4: "KTna"}[cc]
                            for hb in range(4):
                                for (q0, nq, _tl) in QCHUNKS:
                                    k = ecnt[0]
                                    ecnt[0] += 1
                                    pa_ = pz[(2 * k) % 4]
                                    pb_ = pz[(2 * k + 1) % 4]
                                    pan = "pzA%d" % ((2 * k) % 4)
                                    pbn = "pzA%d" % ((2 * k + 1) % 4)
                                    evb = ev_b[k % 2]
                                    evn = "ev_b%d" % (k % 2)
                                    for kc in range(8):
                                        P.op("pe", ("matmul", dict(
                                            out=pa_[:, 0:nq], lhsT=wb[:, kc, hb * 128:(hb + 1) * 128],
                                            rhs=AT[:, kc, q0:q0 + nq], start=(kc == 0), stop=(kc == 7))),
                                            reads=[wname] + AT_all, writes=[pan])
                                    if cc in (0, 1):
                                        for kc in range(8):
                                            P.op("pe", ("matmul", dict(
                                                out=pb_[:, 0:nq], lhsT=wrot[:, kc, hb * 128:(hb + 1) * 128],
                                                rhs=AT[:, kc, q0:q0 + nq], start=(kc == 0), stop=(kc == 7))),
                                                reads=["wrot"] + AT_all, writes=[pbn])
                                        P.op("dve", ("tensor_tensor", dict(
                                            out=ev_f[:, 0, 0:nq], in0=pa_[:, 0:nq], in1=cosT[:, q0:q0 + nq], op=ALU.mult)),
                                            reads=[pan, "tab1"], writes=["ev_f0"])
                                        P.op("dve", ("tensor_tensor", dict(
                                            out=ev_f[:, 1, 0:nq], in0=pb_[:, 0:nq], in1=sinT[:, q0:q0 + nq], op=ALU.mult)),
                                            reads=[pbn, "tab0"], writes=["ev_f1"])
                                        P.op("dve", ("tensor_tensor", dict(
                                            out=evb[:, 0:nq], in0=ev_f[:, 0, 0:nq], in1=ev_f[:, 1, 0:nq], op=ALU.add)),
                                            reads=["ev_f0", "ev_f1"], writes=[evn])
                                    else:
                                        P.op("act", ("activation", dict(
                                            out=evb[:, 0:nq], in_=pa_[:, 0:nq], func=AF.Copy)),
                                            reads=[pan], writes=[evn])
                                    P.dma("sp", ("dma_start", dict(
                                        out=scr[dstn][hb, :, q0:q0 + nq], in_=evb[:, 0:nq])),
                                        reads=[evn], writes=[(dstn, hb, q0)])
                        else:
                            for t in range(NT):
                                off, n = TILES[t]
                                k = ecnt[0]
                                ecnt[0] += 1
                                pa_ = pz[k % 4]
                                pan = "pzA%d" % (k % 4)
                                for kc in range(8):
                                    P.op("pe", ("matmul", dict(
                                        out=pa_[0:n, :], lhsT=AT[:, kc, off:off + n], rhs=wb[:, kc, :],
                                        start=(kc == 0), stop=(kc == 7))), reads=[wname, ("AT", t)], writes=[pan])
                                if cc == 2:
                                    vb = vda_b[k % 2]
                                    vn = "vda_b%d" % (k % 2)
                                    P.op("act", ("activation", dict(
                                        out=vb[0:n, :, 0:128], in_=pa_[0:n, :].rearrange("p (h e) -> p h e", h=4),
                                        func=AF.Copy)), reads=[pan], writes=[vn])
                                    P.dma("sp", ("dma_start", dict(
                                        out=scr["Vda"][off:off + n], in_=vb[0:n])), reads=[vn], writes=[("Vda", t)])
                                elif cc == 5:
                                    vb = vna_b[k % 2]
                                    vn = "vna_b%d" % (k % 2)
                                    P.op("act", ("activation", dict(
                                        out=vb[0:n, :, 0:64], in_=pa_[0:n, :].rearrange("p (h e) -> p h e", h=8),
                                        func=AF.Copy)), reads=[pan], writes=[vn])
                                    P.dma("sp", ("dma_start", dict(
                                        out=scr["Vna"][off:off + n], in_=vb[0:n])), reads=[vn], writes=[("Vna", t)])
                                else:
                                    evb = ev_b[k % 2]
                                    evn = "ev_b%d" % (k % 2)
                                    P.op("act", ("activation", dict(
                                        out=evb[0:n, 0:512], in_=pa_[0:n, :], func=AF.Sigmoid)),
                                        reads=[pan], writes=[evn])
                                    gc0 = (cc - 6) * 512
                                    P.dma("sp", ("dma_start", dict(
                                        out=scr["G"][off:off + n, gc0:gc0 + 512], in_=evb[0:n, 0:512])),
                                        reads=[evn], writes=[("G", t, cc)])
                    P.barrier()
                if stop_after == "A":
                    break
                with ExitStack() as pb:
                    KT = sb(pb, "KT", [128, 4, L], BF16)
                    VA = sb(pb, "VA", [128, NT, 4, 130], BF16)
                    QT = [sb(pb, "QT%d" % i, [128, 4, 512], BF16) for i in range(2)]
                    PT = [sb(pb, "PT%d" % i, [128, 512], BF16) for i in range(3)]
                    OD = sb(pb, "OD", [128, 4, 512], BF16)
                    ODTs = sb(pb, "ODTs", [128, 4, 512], BF16)
                    of = sb(pb, "of", [128, 2, 128], F32)
                    psS = [ps(pb, "psS%d" % i, [128, 512], F32) for i in range(2)]
                    acc = [[ps(pb, "acc%d_%d" % (m, i), [128, 2, 256], F32) for i in range(2)] for m in range(2)]
                    pstB = ps(pb, "pstB", [128, 1024], BF16)[:, 0:512].rearrange("p (h t) -> p h t", h=4)
                    P.dma("sp", ("dma_start", dict(out=KT[:], in_=scr["KTda"].rearrange("h p t -> p h t"))),
                          reads=[("KTda", hb, q0) for hb in range(4) for (q0, _a, _b) in QCHUNKS], writes=["KT"])
                    for t in range(NT):
                        off, n = TILES[t]
                        P.dma("sp", ("dma_start", dict(out=VA[0:n, t], in_=scr["Vda"][off:off + n])),
                              reads=[("Vda", t)], writes=["VA"])
                    sc_i = 0
                    for ci, (q0, nq, tl) in enumerate(QCHUNKS):
                        qt = QT[ci % 2]
                        qtn = "QT%d" % (ci % 2)
                        P.dma("sp", ("dma_start", dict(
                            out=qt[:, :, 0:nq], in_=scr["QTda"][:, :, q0:q0 + nq].rearrange("h p t -> p h t"))),
                            reads=[("QTda", hb, q0) for hb in range(4)], writes=[qtn])
                        nsub = len(tl)
                        for h in range(4):
                            for m in range(2):
                                bp = 64 * m
                                for kt in range(NT):
                                    koff, kn = TILES[kt]
                                    pS = psS[sc_i % 2]
                                    pSn = "psS%d" % (sc_i % 2)
                                    pt = PT[sc_i % 3]
                                    ptn = "PT%d" % (sc_i % 3)
                                    sc_i += 1
                                    P.op("pe", ("matmul", dict(
                                        out=pS[0:kn, 0:nq], lhsT=KT[bp:bp + 64, h, koff:koff + kn],
                                        rhs=qt[bp:bp + 64, h, 0:nq], start=True, stop=True)),
                                        reads=["KT", qtn], writes=[pSn])
                                    P.op("act", ("activation", dict(
                                        out=pt[0:kn, 0:nq], in_=pS[0:kn, 0:nq], func=AF.Exp, scale=0.125)),
                                        reads=[pSn], writes=[ptn])
                                    for j in range(nsub):
                                        nqs = TILES[tl[j]][1]
                                        a_ = acc[m][j // 2]
                                        an = "acc%d_%d" % (m, j // 2)
                                        P.op("pe", ("matmul", dict(
                                            out=a_[0:nqs, j % 2, 0:129], lhsT=pt[0:kn, j * 128:j * 128 + nqs],
                                            rhs=VA[0:kn, kt, h, 0:129], start=(kt == 0 and j % 2 == 0), stop=(kt == NT - 1))),
                                            reads=[ptn, "VA"], writes=[an])
                            for j in range(nsub):
                                nqs = TILES[tl[j]][1]
                                a0 = acc[0][j // 2]
                                a1 = acc[1][j // 2]
                                a0n = "acc0_%d" % (j // 2)
                                a1n = "acc1_%d" % (j // 2)
                                P.op("dve", ("reciprocal", dict(
                                    out=small[0:nqs, 16:17], in_=a0[0:nqs, j % 2, 128:129])), reads=[a0n], writes=["small"])
                                P.op("dve", ("reciprocal", dict(
                                    out=small[0:nqs, 17:18], in_=a1[0:nqs, j % 2, 128:129])), reads=[a1n, "small"], writes=["small"])
                                P.op("dve", ("tensor_scalar", dict(
                                    out=small[0:nqs, 17:18], in0=small[0:nqs, 17:18], scalar1=lam[0:nqs, 4 + l:5 + l],
                                    scalar2=None, op0=ALU.mult)), reads=["small", "lam"], writes=["small"])
                                P.op("dve", ("tensor_scalar", dict(
                                    out=of[0:nqs, 0, :], in0=a1[0:nqs, j % 2, 0:128], scalar1=small[0:nqs, 17:18],
                                    scalar2=None, op0=ALU.mult)), reads=[a1n, "small"], writes=["of0"])
                                P.op("dve", ("scalar_tensor_tensor", dict(
                                    out=of[0:nqs, 1, :], in0=a0[0:nqs, j % 2, 0:128], scalar=small[0:nqs, 16:17],
                                    in1=of[0:nqs, 0, :], op0=ALU.mult, op1=ALU.add)),
                                    reads=[a0n, "small", "of0"], writes=["of1"])
                                rms_rstd(of[0:nqs, 1, :], nqs, 18, "of1")
                                P.op("dve", ("scalar_tensor_tensor", dict(
                                    out=OD[0:nqs, j, h * 128:(h + 1) * 128], in0=of[0:nqs, 1, :],
                                    scalar=small[0:nqs, 18:19], in1=gain_bc[0:nqs, l, :], op0=ALU.mult, op1=ALU.mult)),
                                    reads=["of1", "small", "gain_bc"], writes=["OD"])
                        if stop_after == "B":
                            for j in range(nsub):
                                tj = tl[j]
                                if tj == 0:
                                    continue
                                P.op("act", ("activation", dict(out=sq_junk[:, 0:512], in_=OD[:, j, :], func=AF.Copy)),
                                     reads=["OD"], writes=["sq_junk"])
                                P.dma("sp", ("dma_start", dict(out=y_d[s, 128 * (tj - 1):128 * tj, 0:512], in_=sq_junk[:, 0:512])),
                                      reads=["sq_junk"], writes=[("y", tj)])
                        for j in range(nsub):
                            nqs = TILES[tl[j]][1]
                            for h in range(4):
                                P.op("pe", ("transpose", dict(
                                    out=pstB[:, h, 0:nqs], in_=OD[0:nqs, j, h * 128:(h + 1) * 128],
                                    identity=ident_b[0:nqs, 0:nqs])), reads=["OD", "ident_b"], writes=["pstB"])
                            P.op("act", ("activation", dict(
                                out=ODTs[:, :, j * 128:j * 128 + nqs], in_=pstB[:, :, 0:nqs], func=AF.Copy)),
                                reads=["pstB"], writes=["ODTs"])
                        P.dma("sp", ("dma_start", dict(
                            out=scr["ODT"][:, :, q0:q0 + nq].rearrange("h p t -> p h t"), in_=ODTs[:, :, 0:nq])),
                            reads=["ODTs"], writes=[("ODT", ci)])
                    P.barrier()
                if stop_after == "B":
                    break
                with ExitStack() as pc:
                    KT = sb(pc, "KTn", [128, 4, L], BF16)
                    QT = sb(pc, "QTn", [128, 4, L], BF16)
                    VN = sb(pc, "VN", [64, 33, 8, 66], BF16)
                    Et = sb(pc, "Et", [64, 8, 16, 64], BF16)
                    Pn = [sb(pc, "Pn%d" % i, [64, 4, 8, 64], BF16) for i in range(2)]
                    Pm = [sb(pc, "Pm%d" % i, [16, 4, 64], BF16) for i in range(2)]
                    ONr = [sb(pc, "ONr%d" % i, [64, 512], BF16) for i in range(2)]
                    ONT2 = [sb(pc, "ONT2_%d" % i, [128, 4, 128], BF16) for i in range(2)]
                    psN = [ps(pc, "psN%d" % i, [128, 512], F32) for i in range(4)]
                    psM = ps(pc, "psM", [128, 512], F32)[:, 0:256].rearrange("p (i q) -> p i q", i=4)
                    psO = ps(pc, "psO", [128, 512], F32)[:, 0:512].rearrange("p (i e) -> p i e", i=4)
                    pstC = ps(pc, "pstC", [128, 1024], BF16)[:, 0:256].rearrange("p (c q) -> p c q", c=4)
                    P.dma("sp", ("dma_start", dict(out=KT[:], in_=scr["KTna"].rearrange("h p t -> p h t"))),
                          reads=[("KTna", hb, q0) for hb in range(4) for (q0, _a, _b) in QCHUNKS], writes=["KTn"])
                    P.dma("sp", ("dma_start", dict(out=QT[:], in_=scr["QTna"].rearrange("h p t -> p h t"))),
                          reads=[("QTna", hb, q0) for hb in range(4) for (q0, _a, _b) in QCHUNKS], writes=["QTn"])
                    P.dma("sp", ("dma_start", dict(out=VN[0:16, 0], in_=scr["Vna"][0:16])),
                          reads=[("Vna", t) for t in range(NT)], writes=["VN"])
                    P.dma("sp", ("dma_start", dict(
                        out=VN[:, 1:33], in_=scr["Vna"][16:L].rearrange("(r k) h e -> k r h e", k=64))),
                        reads=[("Vna", t) for t in range(NT)], writes=["VN"])
                    P.dma("sp", ("dma_start", dict(out=Et[:, :, 0:15, :].rearrange("k h r q -> k h (r q)"),
                                                   in_=scr["E"][l].rearrange("k (h x) -> k h x", h=8))),
                          reads=["E_d"], writes=["Et"])
                    P.op("dve", ("memset", dict(ap=Et[:, :, 15, :], constant=0.0)), writes=["Et15"])

                    def na_block(qoff, nqr, rs, rho0, nkr, out_tile, out_name, gi):
                        for hg in range(2):
                            pn = Pn[gi[0] % 2]
                            pnn = "Pn%d" % (gi[0] % 2)
                            pm = Pm[gi[0] % 2]
                            pmn = "Pm%d" % (gi[0] % 2)
                            gi[0] += 1
                            for i in range(4):
                                h = 4 * hg + i
                                pr, bp = h // 2, 64 * (h % 2)
                                for jr in range(nkr):
                                    k0 = 16 + 64 * (rs + jr)
                                    P.op("pe", ("matmul", dict(
                                        out=psN[i][0:64, jr * 64:jr * 64 + nqr], lhsT=KT[bp:bp + 64, pr, k0:k0 + 64],
                                        rhs=QT[bp:bp + 64, pr, qoff:qoff + nqr], start=True, stop=True)),
                                        reads=["KTn", "QTn"], writes=["psN%d" % i])
                                P.op("pe", ("matmul", dict(
                                    out=psM[0:16, i, 0:nqr], lhsT=KT[bp:bp + 64, pr, 0:16],
                                    rhs=QT[bp:bp + 64, pr, qoff:qoff + nqr], start=True, stop=True)),
                                    reads=["KTn", "QTn"], writes=["psM"])
                            if nkr > 0:
                                for i in range(4):
                                    P.op("act", ("activation", dict(
                                        out=pn[:, i, 0:nkr, 0:nqr],
                                        in_=psN[i][0:64, 0:nkr * 64].rearrange("p (r q) -> p r q", q=64)[:, :, 0:nqr],
                                        func=AF.Exp, scale=0.125)), reads=["psN%d" % i], writes=[pnn])
                                P.op("dve", ("tensor_tensor", dict(
                                    out=pn[:, :, 0:nkr, 0:nqr], in0=pn[:, :, 0:nkr, 0:nqr],
                                    in1=Et[:, 4 * hg:4 * hg + 4, rho0:rho0 + nkr, 0:nqr], op=ALU.mult)),
                                    reads=[pnn, "Et", "Et15"], writes=[pnn])
                            import os
                            NSTEP = int(os.environ.get("NA_STEP", "9"))
                            if NSTEP < 2:
                                continue
                            P.op("act", ("activation", dict(
                                out=pm[0:16, :, 0:nqr], in_=psM[0:16, :, 0:nqr], func=AF.Exp, scale=0.125)),
                                reads=["psM"], writes=[pmn])
                            if NSTEP < 3:
                                continue
                            for i in range(4):
                                h = 4 * hg + i
                                for jr in range(nkr):
                                    P.op("pe", ("matmul", dict(
                                        out=psO[0:nqr, i, 0:65], lhsT=pn[:, i, jr, 0:nqr], rhs=VN[:, 1 + rs + jr, h, 0:65],
                                        start=(jr == 0), stop=False)), reads=[pnn, "VN"], writes=["psO"])
                                P.op("pe", ("matmul", dict(
                                    out=psO[0:nqr, i, 0:65], lhsT=pm[0:16, i, 0:nqr], rhs=VN[0:16, 0, h, 0:65],
                                    start=(nkr == 0), stop=True)), reads=[pmn, "VN"], writes=["psO"])
                            if NSTEP < 4:
                                continue
                            P.op("dve", ("reciprocal", dict(out=small[0:nqr, 24:28], in_=psO[0:nqr, :, 64])),
                                 reads=["psO"], writes=["small"])
                            if NSTEP < 5:
                                continue
                            P.op("dve", ("tensor_tensor", dict(
                                out=out_tile[0:nqr, hg * 256:(hg + 1) * 256].rearrange("p (h e) -> p h e", h=4),
                                in0=psO[0:nqr, :, 0:64],
                                in1=small[0:nqr, 24:28].unsqueeze(2).to_broadcast([nqr, 4, 64]), op=ALU.mult)),
                                reads=["psO", "small"], writes=[out_name])

                    gi = [0]
                    import os
                    NR = int(os.environ.get("NA_ROWS", "32"))
                    for r in range(NR):
                        rs = min(max(r - 4, 0), 24)
                        rho0 = rs - r + 7
                        onr = ONr[r % 2]
                        onrn = "ONr%d" % (r % 2)
                        na_block(16 + 64 * r, 64, rs, rho0, 8, onr, onrn, gi)
                        if stop_after == "C":
                            P.op("act", ("activation", dict(out=sq_junk[0:64, 0:512], in_=onr[0:64, :], func=AF.Copy)),
                                 reads=[onrn], writes=["sq_junk"])
                            P.dma("sp", ("dma_start", dict(out=y_d[s, 64 * r:64 * r + 64, 0:512], in_=sq_junk[0:64, 0:512])),
                                  reads=["sq_junk"], writes=[("y", r)])
                        o2 = ONT2[(r // 2) % 2]
                        o2n = "ONT2_%d" % ((r // 2) % 2)
                        for c4 in range(4):
                            P.op("pe", ("transpose", dict(
                                out=pstC[:, c4, 0:64], in_=onr[0:64, c4 * 128:(c4 + 1) * 128], identity=ident_b[0:64, 0:64])),
                                reads=[onrn, "ident_b"], writes=["pstC"])
                        P.op("act", ("activation", dict(
                            out=o2[:, :, (r % 2) * 64:(r % 2) * 64 + 64], in_=pstC[:, :, 0:64], func=AF.Copy)),
                            reads=["pstC"], writes=[o2n])
                        if r % 2 == 1:
                            t0 = 16 + 64 * (r - 1)
                            P.dma("sp", ("dma_start", dict(
                                out=scr["ONT"][:, :, t0:t0 + 128].rearrange("h p t -> p h t"), in_=o2[:])),
                                reads=[o2n], writes=[("ONT", 1 + r // 2)])
                    import os
                    NDBG = int(os.environ.get("NA_DBG", "9"))
                    NDBG = int(os.environ.get("NA_DBG", "9"))
                    if os.environ.get("NA_DUP"):
                        na_block(16 + 64 * 31, 64, 24, 0, 8, ONr[1], "ONr1", gi)
                    if NDBG >= 2:
                        na_block(0, 64, 0, 15, 1, ONr[0], "ONr0", gi)
                    for c4 in range(4 if NDBG >= 3 else 0):
                        P.op("pe", ("transpose", dict(
                            out=pstC[:, c4, 0:16], in_=ONr[0][0:16, c4 * 128:(c4 + 1) * 128], identity=ident_b[0:16, 0:16])),
                            reads=["ONr0", "ident_b"], writes=["pstC"])
                    if NDBG >= 3:
                        P.op("act", ("activation", dict(out=ONT2[0][:, :, 0:16], in_=pstC[:, :, 0:16], func=AF.Copy)),
                             reads=["pstC"], writes=["ONT2_0"])
                    else:
                        P.op("dve", ("memset", dict(ap=ONT2[0][:, :, 0:16], constant=0.0)), writes=["ONT2_0"])
                    P.dma("sp", ("dma_start", dict(out=scr["ONT"][:, :, 0:16].rearrange("h p t -> p h t"),
                                                     in_=ONT2[0][:, :, 0:16])), reads=["ONT2_0"], writes=[("ONT", 0)])
                    P.barrier()
                if stop_after == "C":
                    break
                with ExitStack() as pd:
                    alloc_wstage(pd)
                    Wa = sb(pd, "Wa", [128, 4, D], BF16)
                    Wb = sb(pd, "Wb", [128, 4, D], BF16)
                    Wo = sb(pd, "Wo", [128, 8, D], BF16)
                    odt = [sb(pd, "odt%d" % i, [128, 4, 128], BF16) for i in range(2)]
                    ont = [sb(pd, "ont%d" % i, [128, 4, 128], BF16) for i in range(2)]
                    gt = [sb(pd, "gt%d" % i, [128, 2048], BF16) for i in range(2)]
                    mf = sb(pd, "mf", [128, 2, D], F32)
                    mb = sb(pd, "mb", [128, D], BF16)
                    mT = sb(pd, "mT", [128, 8, 128], BF16)
                    pya = ps(pd, "pya", [128, 2, 512], F32)
                    pyb = ps(pd, "pyb", [128, 2, 512], F32)
                    pyo = ps(pd, "pyo", [128, 2, 512], F32)
                    pstD = ps(pd, "pstD", [128, 8, 128], BF16)
                    for half in range(2):
                        load_w_bf16(Wa, Wa[:, :, half * 512:(half + 1) * 512],
                                    dr["w_branch_a"][l].rearrange("(kc p) c -> p kc c", p=128)[:, :, half * 512:(half + 1) * 512],
                                    "Wa", (4, 512))
                        load_w_bf16(Wb, Wb[:, :, half * 512:(half + 1) * 512],
                                    dr["w_branch_b"][l].rearrange("(kc p) c -> p kc c", p=128)[:, :, half * 512:(half + 1) * 512],
                                    "Wb", (4, 512))
                        load_w_bf16(Wo, Wo[:, :, half * 512:(half + 1) * 512],
                                    dr["w_out"][l].rearrange("(kc p) c -> p kc c", p=128)[:, :, half * 512:(half + 1) * 512],
                                    "Wo", (8, 512))
                    for t in range(NT):
                        off, n = TILES[t]
                        od, on_, g_ = odt[t % 2], ont[t % 2], gt[t % 2]
                        odn, onn, gn = "odt%d" % (t % 2), "ont%d" % (t % 2), "gt%d" % (t % 2)
                        P.dma("sp", ("dma_start", dict(
                            out=od[:, :, 0:n], in_=scr["ODT"][:, :, off:off + n].rearrange("h p t -> p h t"))),
                            reads=[("ODT", ci) for ci in range(5)], writes=[odn])
                        P.dma("sp", ("dma_start", dict(
                            out=on_[:, :, 0:n], in_=scr["ONT"][:, :, off:off + n].rearrange("h p t -> p h t"))),
                            reads=[("ONT", i) for i in range(17)], writes=[onn])
                        P.dma("sp", ("dma_start", dict(out=g_[0:n, :], in_=scr["G"][off:off + n, :])),
                              reads=[("G", t, cc) for cc in range(6, 10)], writes=[gn])
                        for half in range(2):
                            for c4 in range(4):
                                P.op("pe", ("matmul", dict(
                                    out=pya[0:n, half, :], lhsT=od[:, c4, 0:n], rhs=Wa[:, c4, half * 512:(half + 1) * 512],
                                    start=(c4 == 0), stop=(c4 == 3))), reads=[odn, "Wa"], writes=["pya"])
                            for c4 in range(4):
                                P.op("pe", ("matmul", dict(
                                    out=pyb[0:n, half, :], lhsT=on_[:, c4, 0:n], rhs=Wb[:, c4, half * 512:(half + 1) * 512],
                                    start=(c4 == 0), stop=(c4 == 3))), reads=[onn, "Wb"], writes=["pyb"])
                        P.op("dve", ("tensor_tensor", dict(
                            out=mf[0:n, 0, :], in0=pya[0:n].rearrange("p a b -> p (a b)"), in1=g_[0:n, 0:1024], op=ALU.mult)),
                            reads=["pya", gn], writes=["mf0"])
                        P.op("dve", ("tensor_tensor", dict(
                            out=mf[0:n, 1, :], in0=pyb[0:n].rearrange("p a b -> p (a b)"), in1=g_[0:n, 1024:2048], op=ALU.mult)),
                            reads=["pyb", gn], writes=["mf1"])
                        P.op("dve", ("tensor_tensor", dict(
                            out=mb[0:n, :], in0=mf[0:n, 0, :], in1=mf[0:n, 1, :], op=ALU.add)),
                            reads=["mf0", "mf1"], writes=["mb"])
                        for c in range(8):
                            P.op("pe", ("transpose", dict(
                                out=pstD[:, c, 0:n], in_=mb[0:n, c * 128:(c + 1) * 128], identity=ident_b[0:n, 0:n])),
                                reads=["mb", "ident_b"], writes=["pstD"])
                        P.op("act", ("activation", dict(out=mT[:, :, 0:n], in_=pstD[:, :, 0:n], func=AF.Copy)),
                             reads=["pstD"], writes=["mT"])
                        for half in range(2):
                            for c in range(8):
                                P.op("pe", ("matmul", dict(
                                    out=pyo[0:n, half, :], lhsT=mT[:, c, 0:n], rhs=Wo[:, c, half * 512:(half + 1) * 512],
                                    start=(c == 0), stop=(c == 7))), reads=["mT", "Wo"], writes=["pyo"])
                        P.op("dve", ("tensor_tensor", dict(
                            out=H[0:n, t, :], in0=H[0:n, t, :], in1=pyo[0:n].rearrange("p a b -> p (a b)"), op=ALU.add)),
                            reads=["pyo", ("H", t)], writes=[("H", t)])
                    P.barrier()
                if stop_after == "D":
                    break
                with ExitStack() as pe_:
                    Wq = sb(pe_, "Wq", [128, 8, 2048], BF16)
                    keysT = sb(pe_, "keysT", [128, 16, 128], BF16)
                    gbc = sb(pe_, "gbcE", [128, D], F32)
                    P.dma("sp", ("dma_start", dict(out=gbc[:], in_=dr["norm_ffn"][l:l + 1, :].partition_broadcast(128))),
                          writes=["gbc"])
                    with ExitStack() as pw:
                        alloc_wstage(pw)
                        keysf = sb(pw, "keysf", [128, 16, 128], F32)
                        pkT = ps(pw, "pkT", [128, 4, 128], F32)
                        wq3 = dr["peer_wq"][l].rearrange("(kc p) c -> p kc c", p=128)
                        for c4 in range(4):
                            load_w_bf16(Wq, Wq[:, :, c4 * 512:(c4 + 1) * 512], wq3[:, :, c4 * 512:(c4 + 1) * 512], "Wq", (8, 512))
                        P.dma("sp", ("dma_start", dict(
                            out=keysf[:], in_=dr["peer_keys"][l].rearrange("h p k d -> k (h p) d"))), writes=["keysf"])
                        for g4 in range(4):
                            for i in range(4):
                                gq = 4 * g4 + i
                                P.op("pe", ("transpose", dict(
                                    out=pkT[:, i, :], in_=keysf[:, gq, :], identity=ident_f[:, :])),
                                    reads=["keysf", "ident_f"], writes=["pkT"])
                            P.op("act", ("activation", dict(out=keysT[:, 4 * g4:4 * g4 + 4, :], in_=pkT[:], func=AF.Copy)),
                                 reads=["pkT"], writes=["keysT"])
                        P.barrier()
                    CT = sb(pe_, "CT", [128, 8, 128], BF16)
                    qT = sb(pe_, "qT", [128, 16, 128], BF16)
                    ssb = sb(pe_, "ssb", [128, 16, 128], F32)
                    swk = sb(pe_, "swk", [128, 256], F32)
                    vals = sb(pe_, "vals", [128, 16, 16], F32)
                    idxu = sb(pe_, "idxu", [128, 16, 16], U32)
                    idxf = sb(pe_, "idxf", [128, 16, 16], F32)
                    cs_ = sb(pe_, "cs", [128, 8, 256], F32)
                    scv = sb(pe_, "scv", [128, 8, 16], F32)
                    posu = sb(pe_, "posu", [128, 8, 16], U32)
                    posf = sb(pe_, "posf", [128, 8, 16], F32)
                    ai_ = sb(pe_, "ai", [128, 8, 16], I32)
                    af_ = sb(pe_, "af", [128, 8, 16], F32)
                    bf_ = sb(pe_, "bf", [128, 8, 16], F32)
                    msk = sb(pe_, "msk", [128, 8, 16, 16], F32)
                    i1s = sb(pe_, "i1s", [128, 8, 16], F32)
                    i2s = sb(pe_, "i2s", [128, 8, 16], F32)
                    ef = sb(pe_, "ef", [128, 128], F32)
                    ei = [sb(pe_, "ei%d" % i, [128, 128], I32) for i in range(2)]
                    gg = sb(pe_, "gg", [128, 8, 16], F32)
                    hid = sb(pe_, "hid", [128, 128], F32)
                    gl = sb(pe_, "gl", [128, 128], F32)
                    djunk = sb(pe_, "djunk", [128, D], BF16)
                    NG = 8
                    gat = [sb(pe_, "gat%d" % i, [128, 2 * D], BF16) for i in range(NG)]
                    dg = [sb(pe_, "dg%d" % i, [128, 128], BF16) for i in range(4)]
                    pacc = ps(pe_, "pacc", [128, 2, 512], F32)
                    pstE = ps(pe_, "pstE", [128, 8, 128], BF16)
                    pq = [ps(pe_, "pq%d" % i, [128, 4, 128], F32) for i in range(4)]
                    for i in range(2):
                        P.op("pool", ("memset", dict(ap=ei[i][:], constant=0)), writes=["ei%d" % i])
                    gslot = [0]
                    for t in range(NT):
                        off, n = TILES[t]
                        eit = ei[0] if n == 128 else ei[1]
                        ein = "ei0" if n == 128 else "ei1"
                        norm_transpose(t, gbc, CT[:, :, 0:n], pstE, tag_out="CT")
                        for gq in range(16):
                            pqq = pq[gq // 4]
                            for kc in range(8):
                                P.op("pe", ("matmul", dict(
                                    out=pqq[:, gq % 4, 0:n], lhsT=Wq[:, kc, gq * 128:(gq + 1) * 128], rhs=CT[:, kc, 0:n],
                                    start=(kc == 0), stop=(kc == 7))), reads=["Wq", "CT"], writes=["pq%d" % (gq // 4)])
                        for g4 in range(4):
                            P.op("act", ("activation", dict(
                                out=qT[:, 4 * g4:4 * g4 + 4, 0:n], in_=pq[g4][:, :, 0:n], func=AF.Copy)),
                                reads=["pq%d" % g4], writes=["qT"])
                        for gq in range(16):
                            pqq = pq[gq // 4]
                            P.op("pe", ("matmul", dict(
                                out=pqq[0:n, gq % 4, :], lhsT=qT[:, gq, 0:n], rhs=keysT[:, gq, :], start=True, stop=True)),
                                reads=["qT", "keysT"], writes=["pq%d" % (gq // 4)])
                        for g4 in range(4):
                            P.op("act", ("activation", dict(
                                out=ssb[0:n, 4 * g4:4 * g4 + 4, :], in_=pq[g4][0:n, :, :], func=AF.Copy)),
                                reads=["pq%d" % g4], writes=["ssb"])
                        for gq in range(16):
                            P.op("dve", ("max", dict(out=vals[0:n, gq, 0:8], in_=ssb[0:n, gq, :])),
                                 reads=["ssb"], writes=["vals"])
                            P.op("dve", ("max_index", dict(out=idxu[0:n, gq, 0:8], in_max=vals[0:n, gq, 0:8],
                                                                         in_values=ssb[0:n, gq, :])),
                                 reads=["ssb", "vals"], writes=["idxu"])
                            P.op("dve", ("match_replace", dict(
                                out=swk[0:n, 0:128], in_to_replace=vals[0:n, gq, 0:8], in_values=ssb[0:n, gq, :],
                                imm_value=-1e30)), reads=["ssb", "vals"], writes=["swk"])
                            P.op("dve", ("max", dict(out=vals[0:n, gq, 8:16], in_=swk[0:n, 0:128])),
                                 reads=["swk"], writes=["vals"])
                            P.op("dve", ("max_index", dict(out=idxu[0:n, gq, 8:16], in_max=vals[0:n, gq, 8:16],
                                                                         in_values=swk[0:n, 0:128])),
                                 reads=["swk", "vals"], writes=["idxu"])
                        P.op("dve", ("tensor_copy", dict(out=idxf[0:n], in_=idxu[0:n])), reads=["idxu"], writes=["idxf"])
                        v4 = vals[:].rearrange("p (h two) k -> p h two k", two=2)
                        i4 = idxf[:].rearrange("p (h two) k -> p h two k", two=2)
                        P.op("dve", ("tensor_tensor", dict(
                            out=cs_[0:n].rearrange("p h (a b) -> p h a b", a=16),
                            in0=v4[0:n, :, 0, :].unsqueeze(3).to_broadcast([n, 8, 16, 16]),
                            in1=v4[0:n, :, 1, :].unsqueeze(2).to_broadcast([n, 8, 16, 16]), op=ALU.add)),
                            reads=["vals"], writes=["cs"])
                        for h in range(8):
                            P.op("dve", ("max", dict(out=scv[0:n, h, 0:8], in_=cs_[0:n, h, :])),
                                 reads=["cs"], writes=["scv"])
                            P.op("dve", ("max_index", dict(out=posu[0:n, h, 0:8], in_max=scv[0:n, h, 0:8],
                                                                       in_values=cs_[0:n, h, :])),
                                 reads=["cs", "scv"], writes=["posu"])
                            P.op("dve", ("match_replace", dict(
                                out=swk[0:n, :], in_to_replace=scv[0:n, h, 0:8], in_values=cs_[0:n, h, :], imm_value=-1e30)),
                                reads=["cs", "scv"], writes=["swk"])
                            P.op("dve", ("max", dict(out=scv[0:n, h, 8:16], in_=swk[0:n, :])),
                                 reads=["swk"], writes=["scv"])
                            P.op("dve", ("max_index", dict(out=posu[0:n, h, 8:16], in_max=scv[0:n, h, 8:16],
                                                                       in_values=swk[0:n, :])),
                                 reads=["swk", "scv"], writes=["posu"])
                        P.op("dve", ("tensor_copy", dict(out=posf[0:n], in_=posu[0:n])), reads=["posu"], writes=["posf"])
                        P.op("dve", ("tensor_scalar", dict(out=ai_[0:n], in0=posf[0:n], scalar1=-7.5, scalar2=1.0 / 16,
                                                                   op0=ALU.add, op1=ALU.mult)), reads=["posf"], writes=["ai"])
                        P.op("dve", ("tensor_copy", dict(out=af_[0:n], in_=ai_[0:n])), reads=["ai"], writes=["af"])
                        P.op("dve", ("scalar_tensor_tensor", dict(
                            out=bf_[0:n], in0=af_[0:n], scalar=-16.0, in1=posf[0:n], op0=ALU.mult, op1=ALU.add)),
                            reads=["af", "posf"], writes=["bf"])
                        for (src_, two, dst_, dn) in ((af_, 0, i1s, "i1s"), (bf_, 1, i2s, "i2s")):
                            P.op("dve", ("tensor_tensor", dict(
                                out=msk[0:n], in0=src_[0:n].unsqueeze(3).to_broadcast([n, 8, 16, 16]),
                                in1=iota16[0:n, :].unsqueeze(1).unsqueeze(1).to_broadcast([n, 8, 16, 16]), op=ALU.is_equal)),
                                reads=["af", "bf", "iota16"], writes=["msk"])
                            P.op("dve", ("tensor_tensor", dict(
                                out=msk[0:n], in0=msk[0:n],
                                in1=i4[0:n, :, two, :].unsqueeze(2).to_broadcast([n, 8, 16, 16]), op=ALU.mult)),
                                reads=["msk", "idxf"], writes=["msk"])
                            P.op("dve", ("tensor_reduce", dict(
                                out=dst_[0:n], in_=msk[0:n], axis=AX.X, op=ALU.add)), reads=["msk"], writes=[dn])
                        P.op("dve", ("scalar_tensor_tensor", dict(
                            out=ef[0:n, :].rearrange("p (h k) -> p h k", h=8), in0=i1s[0:n], scalar=128.0, in1=i2s[0:n],
                            op0=ALU.mult, op1=ALU.add)), reads=["i1s", "i2s"], writes=["ef"])
                        P.op("dve", ("tensor_scalar", dict(out=eit[0:n, :], in0=ef[0:n, :], scalar1=float(l * NEXP), scalar2=None, op0=ALU.add)),
                             reads=["ef"], writes=[ein])
                        P.op("dve", ("tensor_tensor", dict(
                            out=gg[0:n], in0=scv[0:n], in1=scv[0:n, :, 0:1].to_broadcast([n, 8, 16]), op=ALU.subtract)),
                            reads=["scv"], writes=["gg"])
                        P.op("act", ("activation", dict(out=gg[0:n], in_=gg[0:n], func=AF.Exp)),
                             reads=["gg"], writes=["gg"])
                        P.op("dve", ("tensor_reduce", dict(out=small[0:n, 32:40], in_=gg[0:n], axis=AX.X, op=ALU.add)),
                             reads=["gg"], writes=["small"])
                        P.op("dve", ("reciprocal", dict(out=small[0:n, 32:40], in_=small[0:n, 32:40])),
                             reads=["small"], writes=["small"])
                        P.op("dve", ("tensor_tensor", dict(
                            out=gg[0:n], in0=gg[0:n], in1=small[0:n, 32:40].unsqueeze(2).to_broadcast([n, 8, 16]), op=ALU.mult)),
                            reads=["gg", "small"], writes=["gg"])
                        gflat = gg[:].rearrange("p h k -> p (h k)")
                        slots = []
                        def slot_tail(j, sl):
                            dgi = j % 4
                            P.op("dve", ("tensor_scalar", dict(
                                out=dg[dgi][0:n, 0:n], in0=ident_b[0:n, 0:n], scalar1=gl[0:n, j:j + 1],
                                scalar2=gflat[0:n, j:j + 1], op0=ALU.mult, op1=ALU.mult)),
                                reads=["ident_b", ("gl", j), "gg"], writes=["dg%d" % dgi])
                            for half in range(2):
                                P.op("pe", ("matmul", dict(
                                    out=pacc[0:n, half, :], lhsT=dg[dgi][0:n, 0:n],
                                    rhs=gat[sl][0:n, D + half * 512:D + (half + 1) * 512],
                                    start=(j == 0), stop=(j == 127))),
                                    reads=["dg%d" % dgi, "gat%d" % sl], writes=["pacc"])
                        for j in range(128):
                            sl = gslot[0] % NG
                            gslot[0] += 1
                            P.dma("pool", ("indirect_dma_start", dict(
                                out=gat[sl][:], out_offset=None, in_=scr["UV"],
                                in_offset=bass.IndirectOffsetOnAxis(ap=eit[:, j:j + 1], axis=0))),
                                reads=[ein] + ([("UV", l, b_) for b_ in range(NEXP // 512)] if (t == 0 and j == 0) else []),
                                writes=["gat%d" % sl], semkey=gsem[sl])
                            P.op("dve", ("scalar_tensor_tensor", dict(
                                out=djunk[0:n, :], in0=gat[sl][0:n, 0:D], scalar=1.0, in1=abuf[0:n, :], op0=ALU.mult, op1=ALU.mult,
                                accum_out=hid[0:n, j:j + 1])), reads=["gat%d" % sl, "abuf"], writes=["djunk", ("hid", j)])
                            P.op("act", ("activation", dict(out=gl[0:n, j:j + 1], in_=hid[0:n, j:j + 1], func=AF.Gelu)),
                                 reads=[("hid", j)], writes=[("gl", j)])
                            if j >= 1:
                                slot_tail(j - 1, slots[j - 1])
                            slots.append(sl)
                        slot_tail(127, slots[127])
                        P.op("dve", ("tensor_tensor", dict(
                            out=H[0:n, t, :], in0=H[0:n, t, :], in1=pacc[0:n].rearrange("p a b -> p (a b)"), op=ALU.add)),
                            reads=["pacc", ("H", t)], writes=[("H", t)])
                    P.barrier()
            if stop_after in ("B", "C"):
                continue
            if stop_after is not None:
                for t in range(1, NT):
                    P.dma("sp", ("dma_start", dict(out=y_d[s, 128 * (t - 1):128 * t, :], in_=H[:, t, :])),
                          reads=[("H", t)], writes=[("y", s, t)])
                continue
            for t in range(1, NT):
                rms_rstd(H[:, t, :], 128, 8, ("H", t))
                P.op("dve", ("scalar_tensor_tensor", dict(
                    out=sq_junk[:, :], in0=H[:, t, :], scalar=small[:, 8:9], in1=gfin_bc[:, :],
                    op0=ALU.mult, op1=ALU.mult)), reads=[("H", t), "small", "gfin_bc"], writes=["sq_junk"])
                P.dma("sp", ("dma_start", dict(out=y_d[s, 128 * (t - 1):128 * t, :], in_=sq_junk[:, :])),
                      reads=["sq_junk"], writes=[("y", s, t)])
            P.barrier()
        P.barrier()
        P.emit()
    return nc, P


def make_in_maps(inputs, nseq=SEQ_PER_CORE, ncores=NCORES):
    consts = host_consts()
    xp = np.asarray(inputs["x_prompt"], dtype=np.float32)
    xs = np.asarray(inputs["x_sample"], dtype=np.float32)
    allx = np.concatenate([xp, xs], axis=0)
    maps = []
    for c in range(ncores):
        m = {"x": np.ascontiguousarray(allx[c * nseq:(c + 1) * nseq])}
        for k, shp in WEIGHT_SHAPES.items():
            m[k] = np.ascontiguousarray(np.asarray(inputs[k], dtype=np.float32).reshape(shp))
        m.update(consts)
        maps.append(m)
    return maps


def kernel(**inputs):
    nc, _ = build()
    maps = make_in_maps(inputs)
    res = run_bass_kernel_spmd(nc, maps, core_ids=list(range(NCORES)))
    ys = np.concatenate([r["y"] for r in res.results], axis=0)
    nb = np.asarray(inputs["x_prompt"]).shape[0]
    return (np.ascontiguousarray(ys[:nb]), np.ascontiguousarray(ys[nb:]))
```

```python
import math
import numpy as np
from contextlib import ExitStack
import concourse.bass as bass
import concourse.mybir as mybir
from concourse.bass_utils import run_bass_kernel_spmd

F32 = mybir.dt.float32
BF16 = mybir.dt.bfloat16
I32 = mybir.dt.int32
U32 = mybir.dt.uint32
ALU = mybir.AluOpType
AF = mybir.ActivationFunctionType
AX = mybir.AxisListType

D = 1024
NMETA = 16
SEQ = 2048
L = SEQ + NMETA
DEPTH = 4
NCORES = 8
SEQ_PER_CORE = 5
EPS = 1e-6
TILES = [(0, 16)] + [(16 + 128 * i, 128) for i in range(16)]
NT = len(TILES)
QCHUNKS = [(16 + 512 * c, 512, [1 + 4 * c + j for j in range(4)]) for c in range(4)] + [(0, 16, [0])]
NEXP = 16384
ENGS = ("pe", "act", "dve", "pool", "sp")


class Prog:
    def __init__(self, nc, es, n_dma_sems=24):
        self.nc = nc
        self.es = es
        self.q = {e: [] for e in ENGS}
        self.cnt = {e: 0 for e in ENGS}
        self.waited = {e: {} for e in ENGS}
        self.res_w = {}
        self.res_r = {}
        self.sem = {}
        for e in ENGS:
            self.sem[e] = es.enter_context(nc.semaphore("s_" + e))
        self.dma_sems = []
        self.dma_val = {}
        for i in range(n_dma_sems):
            self.dma_sems.append(self.new_dma_sem("dma%d" % i))
        self.dma_rr = 0
        self.ninstr = 0

    def new_dma_sem(self, name):
        self.sem[name] = self.es.enter_context(self.nc.semaphore("s_" + name))
        self.dma_val[name] = 0
        return name

    def _deps(self, eng, reads, writes):
        ev = {}
        for r in reads:
            e = self.res_w.get(r)
            if e is not None and ev.get(e[0], 0) < e[1]:
                ev[e[0]] = e[1]
        for w in writes:
            e = self.res_w.get(w)
            if e is not None and ev.get(e[0], 0) < e[1]:
                ev[e[0]] = e[1]
            rr = self.res_r.get(w)
            if rr:
                for k, v in rr.items():
                    if ev.get(k, 0) < v:
                        ev[k] = v
        waits = []
        wd = self.waited[eng]
        for k, v in ev.items():
            if k == "pe" and eng == "pe":
                continue
            if wd.get(k, 0) < v:
                wd[k] = v
                waits.append((k, v))
        return waits

    def _commit(self, event, reads, writes):
        for w in writes:
            self.res_w[w] = event
            self.res_r[w] = {}
        k, v = event
        for r in reads:
            d = self.res_r.setdefault(r, {})
            if d.get(k, 0) < v:
                d[k] = v

    def op(self, eng, fn, reads=(), writes=()):
        waits = self._deps(eng, reads, writes)
        self.cnt[eng] += 1
        event = (eng, self.cnt[eng])
        self.q[eng].append((waits, fn, (eng, 1)))
        self._commit(event, reads, writes)
        self.ninstr += 1

    def dma(self, eng, fn, reads=(), writes=(), semkey=None):
        if semkey is None:
            semkey = self.dma_sems[self.dma_rr % len(self.dma_sems)]
            self.dma_rr += 1
        waits = self._deps(eng, reads, writes)
        pv = self.dma_val[semkey]
        wd = self.waited[eng]
        if pv > 0 and wd.get(semkey, 0) < pv:
            wd[semkey] = pv
            waits.append((semkey, pv))
        self.dma_val[semkey] = pv + 16
        event = (semkey, pv + 16)
        self.q[eng].append((waits, fn, (semkey, 16)))
        self._commit(event, reads, writes)
        self.ninstr += 1
        return event

    def barrier(self):
        tot = dict(self.dma_val)
        for e in ENGS:
            tot[e] = self.cnt[e]
        for eng in ENGS:
            waits = []
            wd = self.waited[eng]
            for k, v in tot.items():
                if v > 0 and k != eng and wd.get(k, 0) < v:
                    wd[k] = v
                    waits.append((k, v))
            if waits:
                self.q[eng].append((waits, None, None))
        self.res_w = {}
        self.res_r = {}

    def emit(self):
        nc = self.nc
        sem = self.sem
        q = self.q
        with nc.Block() as block:
            def run(engobj, lst):
                for waits, fn, inc in lst:
                    for k, v in waits:
                        engobj.wait_ge(sem[k], v)
                    if fn is not None:
                        getattr(engobj, fn[0])(**fn[1]).then_inc(sem[inc[0]], inc[1])

            @block.tensor
            def _(e):
                run(e, q["pe"])

            @block.scalar
            def _(e):
                run(e, q["act"])

            @block.vector
            def _(e):
                run(e, q["dve"])

            @block.gpsimd
            def _(e):
                run(e, q["pool"])

            @block.sync
            def _(e):
                run(e, q["sp"])


def host_consts():
    c = {}
    c["c_ident"] = np.eye(128, dtype=np.float32)
    c["c_iota16"] = np.tile(np.arange(16, dtype=np.float32)[None, :], (128, 1))
    half = 32
    inv = 1.0 / (10000.0 ** (np.arange(half, dtype=np.float64) * 2.0 / 64.0))
    c["c_invf"] = np.tile(inv, 4).reshape(128, 1).astype(np.float32) / np.float32(2 * math.pi)
    cc = np.arange(64)
    cs = np.clip(cc - 8, 0, 48)
    k = np.arange(64)
    c["c_colmask"] = ((k[:, None] >= cs[None, :]) & (k[:, None] < cs[None, :] + 16)).astype(np.float32)
    c["c_antiI"] = np.ascontiguousarray(np.eye(31, dtype=np.float32)[::-1])
    c["c_pos"] = np.tile(np.arange(L, dtype=np.float32)[None, :], (128, 1))
    return c


CONST_SHAPES = {"c_ident": [128, 128], "c_iota16": [128, 16], "c_invf": [128, 1],
                "c_colmask": [64, 64], "c_antiI": [31, 31], "c_pos": [128, L]}

WEIGHT_SHAPES = {
    "meta_tokens": [16, D], "norm_mix": [4, D], "w_in": [4, D, 5120],
    "lambda_q1": [4, 64], "lambda_k1": [4, 64], "lambda_q2": [4, 64], "lambda_k2": [4, 64],
    "subln_gain": [4, 128], "na_rpb": [4, 8, 15, 31], "w_branch_a": [4, 512, D],
    "w_branch_b": [4, 512, D], "w_out": [4, D, D], "norm_ffn": [4, D],
    "peer_wq": [4, D, 2048], "peer_keys": [4, 8, 2, 128, 128],
    "peer_u": [4, NEXP, D], "peer_v": [4, NEXP, D], "norm_final": [1, D],
}


def build(nseq=SEQ_PER_CORE, depth=DEPTH, dbg=False, stop_after=None):
    nc = bass.Bass("TRN2", target_bir_lowering=False)
    dr = {}
    dr["x"] = nc.dram_tensor("x", [nseq, SEQ, D], F32, kind="ExternalInput").ap()
    for k, shp in WEIGHT_SHAPES.items():
        dr[k] = nc.dram_tensor(k, shp, F32, kind="ExternalInput").ap()
    for k, shp in CONST_SHAPES.items():
        dr[k] = nc.dram_tensor(k, shp, F32, kind="ExternalInput").ap()
    y_d = nc.dram_tensor("y", [nseq, SEQ, D], F32, kind="ExternalOutput").ap()
    scr = {}
    for nm in ("QTda", "KTda", "QTna", "KTna", "ODT", "ONT"):
        scr[nm] = nc.dram_tensor("scr_" + nm, [4, 128, L], BF16).ap()
    scr["Vda"] = nc.dram_tensor("scr_Vda", [L, 4, 130], BF16).ap()
    scr["Vna"] = nc.dram_tensor("scr_Vna", [L, 8, 66], BF16).ap()
    scr["G"] = nc.dram_tensor("scr_G", [L, 2048], BF16).ap()
    PW = 160
    Dsk_t = nc.dram_tensor("scr_Dsk", [4, 120, 64, PW], F32)
    scr["Dsk"] = Dsk_t.ap()
    scr["E"] = nc.dram_tensor("scr_E", [4, 64, 7680], BF16).ap()
    scr["UV"] = nc.dram_tensor("scr_UV", [4 * NEXP, 2048], BF16).ap()

    with ExitStack() as es:
        P = Prog(nc, es)
        gsem = [P.new_dma_sem("gat%d" % i) for i in range(8)]

        uid = [0]

        def sb(es_, name, shape, dt):
            uid[0] += 1
            if shape[0] < 128:
                t_ = es_.enter_context(nc.sbuf_tensor("sb_%s_%d" % (name, uid[0]), [128] + list(shape[1:]), dt))
                return t_[0:shape[0]]
            return es_.enter_context(nc.sbuf_tensor("sb_%s_%d" % (name, uid[0]), shape, dt))

        def ps(es_, name, shape, dt):
            uid[0] += 1
            return es_.enter_context(nc.psum_tensor("ps_%s_%d" % (name, uid[0]), shape, dt))

        H = sb(es, "H", [128, NT, D], F32)
        ident_f = sb(es, "ident_f", [128, 128], F32)
        ident_b = sb(es, "ident_b", [128, 128], BF16)
        iota16 = sb(es, "iota16", [128, 16], F32)
        cosT = sb(es, "cosT", [128, L], BF16)
        sinT = sb(es, "sinT", [128, L], BF16)
        lam = sb(es, "lam", [128, 8], F32)
        gain_bc = sb(es, "gain_bc", [128, 4, 128], F32)
        gfin_bc = sb(es, "gfin_bc", [128, D], F32)
        small = sb(es, "small", [128, 64], F32)

        P.dma("sp", ("dma_start", dict(out=ident_f[:], in_=dr["c_ident"])), writes=["ident_f"])
        P.dma("sp", ("dma_start", dict(out=iota16[:], in_=dr["c_iota16"])), writes=["iota16"])
        P.op("dve", ("tensor_copy", dict(out=ident_b[:], in_=ident_f[:])), reads=["ident_f"], writes=["ident_b"])
        P.dma("sp", ("dma_start", dict(out=gfin_bc[:], in_=dr["norm_final"].partition_broadcast(128))),
              writes=["gfin_bc"])
        with ExitStack() as s1:
            invf = sb(s1, "invf", [128, 1], F32)
            pos = sb(s1, "pos", [128, L], F32)
            yy = sb(s1, "yy", [128, L], F32)
            yi = sb(s1, "yi", [128, L], I32)
            yf = sb(s1, "yf", [128, L], F32)
            lq = sb(s1, "lq", [128, 4, 4, 64], F32)
            junk = sb(s1, "junk", [128, 64], F32)
            P.dma("sp", ("dma_start", dict(out=invf[:], in_=dr["c_invf"])), writes=["invf"])
            P.dma("sp", ("dma_start", dict(out=pos[:], in_=dr["c_pos"])), writes=["pos"])
            for which, tab in ((0, sinT), (1, cosT)):
                P.op("dve", ("tensor_scalar", dict(
                    out=yy[:], in0=pos[:], scalar1=invf[:, 0:1], scalar2=0.25 * which,
                    op0=ALU.mult, op1=ALU.add)), reads=["pos", "invf"], writes=["yy"])
                P.op("dve", ("tensor_copy", dict(out=yi[:], in_=yy[:])), reads=["yy"], writes=["yi"])
                P.op("dve", ("tensor_copy", dict(out=yf[:], in_=yi[:])), reads=["yi"], writes=["yf"])
                P.op("dve", ("tensor_tensor", dict(out=yy[:], in0=yy[:], in1=yf[:], op=ALU.subtract)),
                     reads=["yy", "yf"], writes=["yy"])
                P.op("dve", ("tensor_scalar", dict(out=yy[:], in0=yy[:], scalar1=0.5, scalar2=-0.5,
                                                      op0=ALU.min, op1=ALU.max)), reads=["yy"], writes=["yy"])
                P.op("act", ("activation", dict(out=tab[:], in_=yy[:], func=AF.Sin,
                                                            scale=2 * math.pi)),
                     reads=["yy"], writes=["tab%d" % which])
            for i, nm in enumerate(("lambda_q1", "lambda_k1", "lambda_q2", "lambda_k2")):
                for l in range(4):
                    P.dma("sp", ("dma_start", dict(
                        out=lq[:, i, l, :], in_=dr[nm][l:l + 1, :].partition_broadcast(128))),
                        writes=["lq"])
            for l in range(4):
                lam_init = 0.8 - 0.6 * math.exp(-0.3 * l)
                P.op("dve", ("scalar_tensor_tensor", dict(
                    out=junk[:], in0=lq[:, 0, l, :], scalar=1.0, in1=lq[:, 1, l, :],
                    op0=ALU.mult, op1=ALU.mult, accum_out=small[:, 0:1])),
                    reads=["lq"], writes=["junk", "small"])
                P.op("dve", ("scalar_tensor_tensor", dict(
                    out=junk[:], in0=lq[:, 2, l, :], scalar=1.0, in1=lq[:, 3, l, :],
                    op0=ALU.mult, op1=ALU.mult, accum_out=small[:, 1:2])),
                    reads=["lq", "small"], writes=["junk", "small"])
                P.op("act", ("activation", dict(out=small[:, 2:4], in_=small[:, 0:2], func=AF.Exp)),
                     reads=["small"], writes=["small"])
                P.op("dve", ("scalar_tensor_tensor", dict(
                    out=lam[:, l:l + 1], in0=small[:, 2:3], scalar=lam_init, in1=small[:, 3:4],
                    op0=ALU.add, op1=ALU.subtract)), reads=["small"], writes=["lam"])
                P.op("dve", ("tensor_scalar", dict(
                    out=lam[:, 4 + l:5 + l], in0=lam[:, l:l + 1], scalar1=-1.0, scalar2=None, op0=ALU.mult)),
                    reads=["lam"], writes=["lam"])
                P.dma("sp", ("dma_start", dict(
                    out=gain_bc[:, l, :], in_=dr["subln_gain"][l:l + 1, :].partition_broadcast(128))),
                    writes=["gain_bc"])
                P.op("dve", ("tensor_scalar", dict(
                    out=gain_bc[:, l, :], in0=gain_bc[:, l, :], scalar1=1.0 - lam_init, scalar2=None, op0=ALU.mult)),
                    reads=["gain_bc"], writes=["gain_bc"])
            P.barrier()
        with ExitStack() as s1:
          if stop_after != "S0":
              rT = sb(s1, "rT", [31, 120], F32)
              antiI = sb(s1, "antiI", [31, 31], F32)
              colmask = sb(s1, "colmask", [64, 64], F32)
              Ppad = sb(s1, "Ppad", [120, PW], F32)
              Esk = sb(s1, "Esk", [64, 120, 64], F32)
              Ebf = sb(s1, "Ebf", [64, 120, 64], BF16)
              prev = ps(s1, "prev", [128, 512], F32)
              P.dma("sp", ("dma_start", dict(out=antiI[:], in_=dr["c_antiI"])), writes=["antiI"])
              P.dma("sp", ("dma_start", dict(out=colmask[:], in_=dr["c_colmask"])), writes=["colmask"])
              for l in range(depth):
                  P.dma("sp", ("dma_start", dict(
                      out=rT[:], in_=dr["na_rpb"][l].rearrange("h r j -> j (h r)"), allow_slow_non_contiguous=True)),
                      writes=["rT"])
                  P.op("pe", ("matmul", dict(out=prev[0:120, 0:31], lhsT=rT[:, :], rhs=antiI[:, :],
                                                start=True, stop=True)), reads=["rT", "antiI"], writes=["prev"])
                  P.op("dve", ("memset", dict(ap=Ppad[:], constant=-30000.0)), writes=["Ppad"])
                  P.op("dve", ("tensor_copy", dict(out=Ppad[:, 64:95], in_=prev[0:120, 0:31])),
                       reads=["prev"], writes=["Ppad"])
                  P.op("act", ("activation", dict(out=Ppad[:], in_=Ppad[:], func=AF.Exp)),
                       reads=["Ppad"], writes=["Ppad"])
                  P.dma("sp", ("dma_start", dict(
                      out=scr["Dsk"][l], in_=Ppad[:].unsqueeze(1).to_broadcast([120, 64, PW]))),
                      reads=["Ppad"], writes=["Dsk"])
                  src = bass.AP(Dsk_t, l * 120 * 64 * PW + 79, [[PW - 1, 64], [64 * PW, 120], [1, 64]])
                  P.dma("sp", ("dma_start", dict(out=Esk[:], in_=src)), reads=["Dsk"], writes=["Esk"])
                  P.op("dve", ("tensor_tensor", dict(
                      out=Ebf[:], in0=Esk[:], in1=colmask[:].unsqueeze(1).to_broadcast([64, 120, 64]),
                      op=ALU.mult)), reads=["Esk", "colmask"], writes=["Ebf"])
                  P.dma("sp", ("dma_start", dict(
                      out=scr["E"][l], in_=Ebf[:].rearrange("k a q -> k (a q)"))), reads=["Ebf"], writes=["E_d"])
              P.barrier()

        with ExitStack() as s1:
          if stop_after not in ("S0", "S"):
            u32 = [sb(s1, "u32_%d" % i, [128, 4, D], F32) for i in range(2)]
            v32 = [sb(s1, "v32_%d" % i, [128, 4, D], F32) for i in range(2)]
            uv16 = [sb(s1, "uv16_%d" % i, [128, 4, 2, D], BF16) for i in range(2)]
            it = 0
            for l in range(depth):
                for blk in range(NEXP // 512):
                    i = it % 2
                    it += 1
                    e0 = blk * 512
                    P.dma("sp", ("dma_start", dict(
                        out=u32[i][:], in_=dr["peer_u"][l, e0:e0 + 512, :].rearrange("(p k) d -> p k d", k=4))),
                        writes=["u32_%d" % i])
                    P.dma("sp", ("dma_start", dict(
                        out=v32[i][:], in_=dr["peer_v"][l, e0:e0 + 512, :].rearrange("(p k) d -> p k d", k=4))),
                        writes=["v32_%d" % i])
                    P.op("dve", ("tensor_copy", dict(out=uv16[i][:, :, 0, :], in_=u32[i][:])),
                         reads=["u32_%d" % i], writes=["uv16u_%d" % i])
                    P.op("act", ("activation", dict(out=uv16[i][:, 0:2, 1, :], in_=v32[i][:, 0:2, :], func=AF.Copy)),
                         reads=["v32_%d" % i], writes=["uv16va_%d" % i])
                    P.op("pool", ("tensor_copy", dict(out=uv16[i][:, 2:4, 1, :], in_=v32[i][:, 2:4, :])),
                         reads=["v32_%d" % i], writes=["uv16vb_%d" % i])
                    r0 = l * NEXP + e0
                    P.dma("sp", ("dma_start", dict(
                        out=scr["UV"][r0:r0 + 512, :].rearrange("(p k) c -> p k c", k=4),
                        in_=uv16[i][:].rearrange("p k two d -> p k (two d)"))),
                        reads=["uv16u_%d" % i, "uv16va_%d" % i, "uv16vb_%d" % i], writes=[("UV", l, blk)])
            P.barrier()
        def rms_rstd(Xap, n, col, tag):
            width = Xap.shape[-1]
            P.op("act", ("activation", dict(out=sq_junk[0:n, 0:width], in_=Xap, func=AF.Square,
                                               accum_out=small[0:n, col:col + 1])),
                 reads=[tag], writes=["sq_junk", "small"])
            P.op("dve", ("tensor_scalar", dict(out=small[0:n, col:col + 1], in0=small[0:n, col:col + 1],
                                                  scalar1=1.0 / width, scalar2=EPS, op0=ALU.mult, op1=ALU.add)),
                 reads=["small"], writes=["small"])
            P.op("act", ("activation", dict(out=small[0:n, col:col + 1], in_=small[0:n, col:col + 1],
                                               func=AF.Sqrt)), reads=["small"], writes=["small"])
            P.op("dve", ("reciprocal", dict(out=small[0:n, col:col + 1], in_=small[0:n, col:col + 1])),
                 reads=["small"], writes=["small"])

        sq_junk = sb(es, "sq_junk", [128, D], F32)
        wstage = [None, None]
        wst_i = [0]

        def alloc_wstage(es_):
            for i in range(2):
                wstage[i] = sb(es_, "wstage%d" % i, [128, 4096], F32)

        def load_w_bf16(dst_tile, dst_ap, src_ap, dst_name, shape3):
            a, b = shape3
            i = wst_i[0] % 2
            wst_i[0] += 1
            st = wstage[i]
            stv = st[:, 0:a * b].rearrange("p (a b) -> p a b", a=a)
            P.dma("sp", ("dma_start", dict(out=stv, in_=src_ap)), writes=["wstage%d" % i])
            P.op("pool", ("tensor_copy", dict(out=dst_ap, in_=stv)), reads=["wstage%d" % i], writes=[dst_name])
            return stv, i

        def norm_transpose(t, gbc, AT_dst, pst, xkeep=None, tag_out="AT"):
            off, n = TILES[t]
            rms_rstd(H[0:n, t, :], n, 8, ("H", t))
            dst = xkeep if xkeep is not None else abuf
            P.op("dve", ("scalar_tensor_tensor", dict(
                out=dst[0:n, :], in0=H[0:n, t, :], scalar=small[0:n, 8:9], in1=gbc[0:n, :],
                op0=ALU.mult, op1=ALU.mult)), reads=[("H", t), "small", "gbc"], writes=["abuf"])
            if xkeep is not None:
                P.op("act", ("activation", dict(out=abuf[0:n, :], in_=xkeep[0:n, :], func=AF.Copy)),
                     reads=["abuf"], writes=["abuf_b"])
                rtag = "abuf_b"
            else:
                rtag = "abuf"
            for c in range(8):
                P.op("pe", ("transpose", dict(out=pst[:, c, 0:n], in_=abuf[0:n, c * 128:(c + 1) * 128],
                                                      identity=ident_b[0:n, 0:n])),
                     reads=[rtag, "ident_b"], writes=["pst"])
            P.op("act", ("activation", dict(out=AT_dst, in_=pst[:, :, 0:n], func=AF.Copy)),
                 reads=["pst"], writes=[tag_out])

        abuf = sb(es, "abuf", [128, D], BF16)

        for s in range(nseq if stop_after not in ("S0", "S") else 0):
            P.dma("sp", ("dma_start", dict(out=H[0:16, 0, :], in_=dr["meta_tokens"])), writes=[("H", 0)])
            for t in range(1, NT):
                P.dma("sp", ("dma_start", dict(out=H[:, t, :], in_=dr["x"][s, 128 * (t - 1):128 * t, :])),
                      writes=[("H", t)])
            for l in range(depth):
                with ExitStack() as pa:
                    alloc_wstage(pa)
                    AT = sb(pa, "AT", [128, 8, L], BF16)
                    gbc = sb(pa, "gbc", [128, D], F32)
                    abuf_f = None
                    wbf = [sb(pa, "wbf%d" % i, [128, 8, 512], BF16) for i in range(2)]
                    wrot = sb(pa, "wrot", [128, 8, 512], BF16)
                    ev_f = sb(pa, "ev_f", [128, 2, 512], F32)
                    ev_b = [sb(pa, "ev_b%d" % i, [128, 520], BF16) for i in range(2)]
                    vda_b = [sb(pa, "vda_b%d" % i, [128, 4, 130], BF16) for i in range(2)]
                    vna_b = [sb(pa, "vna_b%d" % i, [128, 8, 66], BF16) for i in range(2)]
                    pst = ps(pa, "pstA", [128, 8, 128], BF16)
                    pz = [ps(pa, "pzA%d" % i, [128, 512], F32) for i in range(4)]
                    P.dma("sp", ("dma_start", dict(out=gbc[:], in_=dr["norm_mix"][l:l + 1, :].partition_broadcast(128))),
                          writes=["gbc"])
                    for i in range(2):
                        P.op("pool", ("memset", dict(ap=vda_b[i][:], constant=1.0)), writes=["vda_b%d" % i])
                        P.op("pool", ("memset", dict(ap=vna_b[i][:], constant=1.0)), writes=["vna_b%d" % i])
                    for t in range(NT):
                        off, n = TILES[t]
                        norm_transpose(t, gbc, AT[:, :, off:off + n], pst, tag_out=("AT", t))
                    AT_all = [("AT", t) for t in range(NT)]
                    w3 = dr["w_in"][l].rearrange("(kc p) c -> p kc c", p=128)
                    ecnt = [0]
                    for cc in range(10):
                        wb = wbf[cc % 2]
                        wname = "wbf%d" % (cc % 2)
                        stv, sti = load_w_bf16(wb, wb[:], w3[:, :, cc * 512:(cc + 1) * 512], wname, (8, 512))
                        if cc in (0, 1):
                            st5 = stv.rearrange("p a (g two h) -> p a g two h", two=2, h=32)
                            wr5 = wrot[:].rearrange("p a (g two h) -> p a g two h", two=2, h=32)
                            for kc in range(8):
                                P.op("pool", ("tensor_scalar", dict(
                                    out=wr5[:, kc, :, 0, :], in0=st5[:, kc, :, 1, :], scalar1=-1.0, scalar2=None,
                                    op0=ALU.mult)), reads=["wstage%d" % sti], writes=["wrot"])
                                P.op("pool", ("tensor_copy", dict(
                                    out=wr5[:, kc, :, 1, :], in_=st5[:, kc, :, 0, :])),
                                    reads=["wstage%d" % sti], writes=["wrot"])
                        if cc in (0, 1, 3, 4):
                            dstn = {0: "QTda", 1: "KTda", 3: "QTna", 4: "KTna"}[cc]
                            for hb in range(4):
                                for (q0, nq, _tl) in QCHUNKS:
                                    k = ecnt[0]
                                    ecnt[0] += 1
                                    pa_ = pz[(2 * k) % 4]
                                    pb_ = pz[(2 * k + 1) % 4]
                                    pan = "pzA%d" % ((2 * k) % 4)
                                    pbn = "pzA%d" % ((2 * k + 1) % 4)
                                    evb = ev_b[k % 2]
                                    evn = "ev_b%d" % (k % 2)
                                    for kc in range(8):
                                        P.op("pe", ("matmul", dict(
                                            out=pa_[:, 0:nq], lhsT=wb[:, kc, hb * 128:(hb + 1) * 128],
                                            rhs=AT[:, kc, q0:q0 + nq], start=(kc == 0), stop=(kc == 7))),
                                            reads=[wname] + AT_all, writes=[pan])
                                    if cc in (0, 1):
                                        for kc in range(8):
                                            P.op("pe", ("matmul", dict(
                                                out=pb_[:, 0:nq], lhsT=wrot[:, kc, hb * 128:(hb + 1) * 128],
                                                rhs=AT[:, kc, q0:q0 + nq], start=(kc == 0), stop=(kc == 7))),
                                                reads=["wrot"] + AT_all, writes=[pbn])
                                        P.op("dve", ("tensor_tensor", dict(
                                            out=ev_f[:, 0, 0:nq], in0=pa_[:, 0:nq], in1=cosT[:, q0:q0 + nq], op=ALU.mult)),
                                            reads=[pan, "tab1"], writes=["ev_f0"])
                                        P.op("dve", ("tensor_tensor", dict(
                                            out=ev_f[:, 1, 0:nq], in0=pb_[:, 0:nq], in1=sinT[:, q0:q0 + nq], op=ALU.mult)),
                                            reads=[pbn, "tab0"], writes=["ev_f1"])
                                        P.op("dve", ("tensor_tensor", dict(
                                            out=evb[:, 0:nq], in0=ev_f[:, 0, 0:nq], in1=ev_f[:, 1, 0:nq], op=ALU.add)),
                                            reads=["ev_f0", "ev_f1"], writes=[evn])
                                    else:
                                        P.op("act", ("activation", dict(
                                            out=evb[:, 0:nq], in_=pa_[:, 0:nq], func=AF.Copy)),
                                            reads=[pan], writes=[evn])
                                    P.dma("sp", ("dma_start", dict(
                                        out=scr[dstn][hb, :, q0:q0 + nq], in_=evb[:, 0:nq])),
                                        reads=[evn], writes=[(dstn, hb, q0)])
                        else:
                            for t in range(NT):
                                off, n = TILES[t]
                                k = ecnt[0]
                                ecnt[0] += 1
                                pa_ = pz[k % 4]
                                pan = "pzA%d" % (k % 4)
                                for kc in range(8):
                                    P.op("pe", ("matmul", dict(
                                        out=pa_[0:n, :], lhsT=AT[:, kc, off:off + n], rhs=wb[:, kc, :],
                                        start=(kc == 0), stop=(kc == 7))), reads=[wname, ("AT", t)], writes=[pan])
                                if cc == 2:
                                    vb = vda_b[k % 2]
                                    vn = "vda_b%d" % (k % 2)
                                    P.op("act", ("activation", dict(
                                        out=vb[0:n, :, 0:128], in_=pa_[0:n, :].rearrange("p (h e) -> p h e", h=4),
                                        func=AF.Copy)), reads=[pan], writes=[vn])
                                    P.dma("sp", ("dma_start", dict(
                                        out=scr["Vda"][off:off + n], in_=vb[0:n])), reads=[vn], writes=[("Vda", t)])
                                elif cc == 5:
                                    vb = vna_b[k % 2]
                                    vn = "vna_b%d" % (k % 2)
                                    P.op("act", ("activation", dict(
                                        out=vb[0:n, :, 0:64], in_=pa_[0:n, :].rearrange("p (h e) -> p h e", h=8),
                                        func=AF.Copy)), reads=[pan], writes=[vn])
                                    P.dma("sp", ("dma_start", dict(
                                        out=scr["Vna"][off:off + n], in_=vb[0:n])), reads=[vn], writes=[("Vna", t)])
                                else:
                                    evb = ev_b[k % 2]
                                    evn = "ev_b%d" % (k % 2)
                                    P.op("act", ("activation", dict(
                                        out=evb[0:n, 0:512], in_=pa_[0:n, :], func=AF.Sigmoid)),
                                        reads=[pan], writes=[evn])
                                    gc0 = (cc - 6) * 512
                                    P.dma("sp", ("dma_start", dict(
                                        out=scr["G"][off:off + n, gc0:gc0 + 512], in_=evb[0:n, 0:512])),
                                        reads=[evn], writes=[("G", t, cc)])
                    P.barrier()
                if stop_after == "A":
                    break
                with ExitStack() as pb:
                    KT = sb(pb, "KT", [128, 4, L], BF16)
                    VA = sb(pb, "VA", [128, NT, 4, 130], BF16)
                    QT = [sb(pb, "QT%d" % i, [128, 4, 512], BF16) for i in range(2)]
                    PT = [sb(pb, "PT%d" % i, [128, 512], BF16) for i in range(3)]
                    OD = sb(pb, "OD", [128, 4, 512], BF16)
                    ODTs = sb(pb, "ODTs", [128, 4, 512], BF16)
                    of = sb(pb, "of", [128, 2, 128], F32)
                    psS = [ps(pb, "psS%d" % i, [128, 512], F32) for i in range(2)]
                    acc = [[ps(pb, "acc%d_%d" % (m, i), [128, 2, 256], F32) for i in range(2)] for m in range(2)]
                    pstB = ps(pb, "pstB", [128, 1024], BF16)[:, 0:512].rearrange("p (h t) -> p h t", h=4)
                    P.dma("sp", ("dma_start", dict(out=KT[:], in_=scr["KTda"].rearrange("h p t -> p h t"))),
                          reads=[("KTda", hb, q0) for hb in range(4) for (q0, _a, _b) in QCHUNKS], writes=["KT"])
                    for t in range(NT):
                        off, n = TILES[t]
                        P.dma("sp", ("dma_start", dict(out=VA[0:n, t], in_=scr["Vda"][off:off + n])),
                              reads=[("Vda", t)], writes=["VA"])
                    sc_cnt = [0]
                    for ci, (q0, nq, tl) in enumerate(QCHUNKS):
                        qt = QT[ci % 2]
                        qtn = "QT%d" % (ci % 2)
                        P.dma("sp", ("dma_start", dict(
                            out=qt[:, :, 0:nq], in_=scr["QTda"][:, :, q0:q0 + nq].rearrange("h p t -> p h t"))),
                            reads=[("QTda", hb, q0) for hb in range(4)], writes=[qtn])
                        nsub = len(tl)
                        for h in range(4):
                            items = [(m, kt) for m in range(2) for kt in range(NT)]
                            bufs = []

                            def emit_score(m, kt):
                                nonlocal_sc = sc_cnt[0]
                                sc_cnt[0] += 1
                                bp = 64 * m
                                koff, kn = TILES[kt]
                                pS = psS[nonlocal_sc % 2]
                                pSn = "psS%d" % (nonlocal_sc % 2)
                                pt = PT[nonlocal_sc % 3]
                                ptn = "PT%d" % (nonlocal_sc % 3)
                                P.op("pe", ("matmul", dict(
                                    out=pS[0:kn, 0:nq], lhsT=KT[bp:bp + 64, h, koff:koff + kn],
                                    rhs=qt[bp:bp + 64, h, 0:nq], start=True, stop=True)),
                                    reads=["KT", qtn], writes=[pSn])
                                P.op("act", ("activation", dict(
                                    out=pt[0:kn, 0:nq], in_=pS[0:kn, 0:nq], func=AF.Exp, scale=0.125)),
                                    reads=[pSn], writes=[ptn])
                                return pt, ptn

                            def emit_pv(m, kt, pt, ptn):
                                koff, kn = TILES[kt]
                                for j in range(nsub):
                                    nqs = TILES[tl[j]][1]
                                    a_ = acc[m][j // 2]
                                    an = "acc%d_%d" % (m, j // 2)
                                    P.op("pe", ("matmul", dict(
                                        out=a_[0:nqs, j % 2, 0:129], lhsT=pt[0:kn, j * 128:j * 128 + nqs],
                                        rhs=VA[0:kn, kt, h, 0:129], start=(kt == 0 and j % 2 == 0), stop=(kt == NT - 1))),
                                        reads=[ptn, "VA"], writes=[an])

                            cur = emit_score(*items[0])
                            for ii, (m, kt) in enumerate(items):
                                nxt = emit_score(*items[ii + 1]) if ii + 1 < len(items) else None
                                emit_pv(m, kt, cur[0], cur[1])
                                cur = nxt
                            for j in range(nsub):
                                nqs = TILES[tl[j]][1]
                                a0 = acc[0][j // 2]
                                a1 = acc[1][j // 2]
                                a0n = "acc0_%d" % (j // 2)
                                a1n = "acc1_%d" % (j // 2)
                                P.op("dve", ("reciprocal", dict(
                                    out=small[0:nqs, 16:17], in_=a0[0:nqs, j % 2, 128:129])), reads=[a0n], writes=["small"])
                                P.op("dve", ("reciprocal", dict(
                                    out=small[0:nqs, 17:18], in_=a1[0:nqs, j % 2, 128:129])), reads=[a1n, "small"], writes=["small"])
                                P.op("dve", ("tensor_scalar", dict(
                                    out=small[0:nqs, 17:18], in0=small[0:nqs, 17:18], scalar1=lam[0:nqs, 4 + l:5 + l],
                                    scalar2=None, op0=ALU.mult)), reads=["small", "lam"], writes=["small"])
                                P.op("dve", ("tensor_scalar", dict(
                                    out=of[0:nqs, 0, :], in0=a1[0:nqs, j % 2, 0:128], scalar1=small[0:nqs, 17:18],
                                    scalar2=None, op0=ALU.mult)), reads=[a1n, "small"], writes=["of0"])
                                P.op("dve", ("scalar_tensor_tensor", dict(
                                    out=of[0:nqs, 1, :], in0=a0[0:nqs, j % 2, 0:128], scalar=small[0:nqs, 16:17],
                                    in1=of[0:nqs, 0, :], op0=ALU.mult, op1=ALU.add)),
                                    reads=[a0n, "small", "of0"], writes=["of1"])
                                rms_rstd(of[0:nqs, 1, :], nqs, 18, "of1")
                                P.op("dve", ("scalar_tensor_tensor", dict(
                                    out=OD[0:nqs, j, h * 128:(h + 1) * 128], in0=of[0:nqs, 1, :],
                                    scalar=small[0:nqs, 18:19], in1=gain_bc[0:nqs, l, :], op0=ALU.mult, op1=ALU.mult)),
                                    reads=["of1", "small", "gain_bc"], writes=["OD"])
                        if stop_after == "B":
                            for j in range(nsub):
                                tj = tl[j]
                                if tj == 0:
                                    continue
                                P.op("act", ("activation", dict(out=sq_junk[:, 0:512], in_=OD[:, j, :], func=AF.Copy)),
                                     reads=["OD"], writes=["sq_junk"])
                                P.dma("sp", ("dma_start", dict(out=y_d[s, 128 * (tj - 1):128 * tj, 0:512], in_=sq_junk[:, 0:512])),
                                      reads=["sq_junk"], writes=[("y", tj)])
                        for j in range(nsub):
                            nqs = TILES[tl[j]][1]
                            for h in range(4):
                                P.op("pe", ("transpose", dict(
                                    out=pstB[:, h, 0:nqs], in_=OD[0:nqs, j, h * 128:(h + 1) * 128],
                                    identity=ident_b[0:nqs, 0:nqs])), reads=["OD", "ident_b"], writes=["pstB"])
                            P.op("act", ("activation", dict(
                                out=ODTs[:, :, j * 128:j * 128 + nqs], in_=pstB[:, :, 0:nqs], func=AF.Copy)),
                                reads=["pstB"], writes=["ODTs"])
                        P.dma("sp", ("dma_start", dict(
                            out=scr["ODT"][:, :, q0:q0 + nq].rearrange("h p t -> p h t"), in_=ODTs[:, :, 0:nq])),
                            reads=["ODTs"], writes=[("ODT", ci)])
                    P.barrier()
                if stop_after == "B":
                    break
                with ExitStack() as pc:
                    KT = sb(pc, "KTn", [128, 4, L], BF16)
                    QT = sb(pc, "QTn", [128, 4, L], BF16)
                    VN = sb(pc, "VN", [64, 33, 8, 66], BF16)
                    Et = sb(pc, "Et", [64, 8, 16, 64], BF16)
                    Pn = [sb(pc, "Pn%d" % i, [64, 4, 8, 64], BF16) for i in range(2)]
                    Pm = [sb(pc, "Pm%d" % i, [16, 4, 64], BF16) for i in range(2)]
                    ONr = [sb(pc, "ONr%d" % i, [64, 512], BF16) for i in range(2)]
                    ONT2 = [sb(pc, "ONT2_%d" % i, [128, 4, 128], BF16) for i in range(2)]
                    psN = [ps(pc, "psN%d" % i, [128, 512], F32) for i in range(4)]
                    psM = ps(pc, "psM", [128, 512], F32)[:, 0:256].rearrange("p (i q) -> p i q", i=4)
                    psO = ps(pc, "psO", [128, 512], F32)[:, 0:512].rearrange("p (i e) -> p i e", i=4)
                    pstC = ps(pc, "pstC", [128, 1024], BF16)[:, 0:256].rearrange("p (c q) -> p c q", c=4)
                    P.dma("sp", ("dma_start", dict(out=KT[:], in_=scr["KTna"].rearrange("h p t -> p h t"))),
                          reads=[("KTna", hb, q0) for hb in range(4) for (q0, _a, _b) in QCHUNKS], writes=["KTn"])
                    P.dma("sp", ("dma_start", dict(out=QT[:], in_=scr["QTna"].rearrange("h p t -> p h t"))),
                          reads=[("QTna", hb, q0) for hb in range(4) for (q0, _a, _b) in QCHUNKS], writes=["QTn"])
                    P.dma("sp", ("dma_start", dict(out=VN[0:16, 0], in_=scr["Vna"][0:16])),
                          reads=[("Vna", t) for t in range(NT)], writes=["VN"])
                    P.dma("sp", ("dma_start", dict(
                        out=VN[:, 1:33], in_=scr["Vna"][16:L].rearrange("(r k) h e -> k r h e", k=64))),
                        reads=[("Vna", t) for t in range(NT)], writes=["VN"])
                    P.dma("sp", ("dma_start", dict(out=Et[:, :, 0:15, :].rearrange("k h r q -> k h (r q)"),
                                                   in_=scr["E"][l].rearrange("k (h x) -> k h x", h=8))),
                          reads=["E_d"], writes=["Et"])
                    P.op("dve", ("memset", dict(ap=Et[:, :, 15, :], constant=0.0)), writes=["Et15"])

                    def na_block(qoff, nqr, rs, rho0, nkr, out_tile, out_name, gi):
                        for hg in range(2):
                            pn = Pn[gi[0] % 2]
                            pnn = "Pn%d" % (gi[0] % 2)
                            pm = Pm[gi[0] % 2]
                            pmn = "Pm%d" % (gi[0] % 2)
                            gi[0] += 1
                            for i in range(4):
                                h = 4 * hg + i
                                pr, bp = h // 2, 64 * (h % 2)
                                for jr in range(nkr):
                                    k0 = 16 + 64 * (rs + jr)
                                    P.op("pe", ("matmul", dict(
                                        out=psN[i][0:64, jr * 64:jr * 64 + nqr], lhsT=KT[bp:bp + 64, pr, k0:k0 + 64],
                                        rhs=QT[bp:bp + 64, pr, qoff:qoff + nqr], start=True, stop=True)),
                                        reads=["KTn", "QTn"], writes=["psN%d" % i])
                                P.op("pe", ("matmul", dict(
                                    out=psM[0:16, i, 0:nqr], lhsT=KT[bp:bp + 64, pr, 0:16],
                                    rhs=QT[bp:bp + 64, pr, qoff:qoff + nqr], start=True, stop=True)),
                                    reads=["KTn", "QTn"], writes=["psM"])
                            if nkr > 0:
                                for i in range(4):
                                    P.op("act", ("activation", dict(
                                        out=pn[:, i, 0:nkr, 0:nqr],
                                        in_=psN[i][0:64, 0:nkr * 64].rearrange("p (r q) -> p r q", q=64)[:, :, 0:nqr],
                                        func=AF.Exp, scale=0.125)), reads=["psN%d" % i], writes=[pnn])
                                P.op("dve", ("tensor_tensor", dict(
                                    out=pn[:, :, 0:nkr, 0:nqr], in0=pn[:, :, 0:nkr, 0:nqr],
                                    in1=Et[:, 4 * hg:4 * hg + 4, rho0:rho0 + nkr, 0:nqr], op=ALU.mult)),
                                    reads=[pnn, "Et", "Et15"], writes=[pnn])
                            import os
                            NSTEP = int(os.environ.get("NA_STEP", "9"))
                            if NSTEP < 2:
                                continue
                            P.op("act", ("activation", dict(
                                out=pm[0:16, :, 0:nqr], in_=psM[0:16, :, 0:nqr], func=AF.Exp, scale=0.125)),
                                reads=["psM"], writes=[pmn])
                            if NSTEP < 3:
                                continue
                            for i in range(4):
                                h = 4 * hg + i
                                for jr in range(nkr):
                                    P.op("pe", ("matmul", dict(
                                        out=psO[0:nqr, i, 0:65], lhsT=pn[:, i, jr, 0:nqr], rhs=VN[:, 1 + rs + jr, h, 0:65],
                                        start=(jr == 0), stop=False)), reads=[pnn, "VN"], writes=["psO"])
                                P.op("pe", ("matmul", dict(
                                    out=psO[0:nqr, i, 0:65], lhsT=pm[0:16, i, 0:nqr], rhs=VN[0:16, 0, h, 0:65],
                                    start=(nkr == 0), stop=True)), reads=[pmn, "VN"], writes=["psO"])
                            if NSTEP < 4:
                                continue
                            P.op("dve", ("reciprocal", dict(out=small[0:nqr, 24:28], in_=psO[0:nqr, :, 64])),
                                 reads=["psO"], writes=["small"])
                            if NSTEP < 5:
                                continue
                            P.op("dve", ("tensor_tensor", dict(
                                out=out_tile[0:nqr, hg * 256:(hg + 1) * 256].rearrange("p (h e) -> p h e", h=4),
                                in0=psO[0:nqr, :, 0:64],
                                in1=small[0:nqr, 24:28].unsqueeze(2).to_broadcast([nqr, 4, 64]), op=ALU.mult)),
                                reads=["psO", "small"], writes=[out_name])

                    gi = [0]
                    import os
                    NR = int(os.environ.get("NA_ROWS", "32"))
                    for r in range(NR):
                        rs = min(max(r - 4, 0), 24)
                        rho0 = rs - r + 7
                        onr = ONr[r % 2]
                        onrn = "ONr%d" % (r % 2)
                        na_block(16 + 64 * r, 64, rs, rho0, 8, onr, onrn, gi)
                        if stop_after == "C":
                            P.op("act", ("activation", dict(out=sq_junk[0:64, 0:512], in_=onr[0:64, :], func=AF.Copy)),
                                 reads=[onrn], writes=["sq_junk"])
                            P.dma("sp", ("dma_start", dict(out=y_d[s, 64 * r:64 * r + 64, 0:512], in_=sq_junk[0:64, 0:512])),
                                  reads=["sq_junk"], writes=[("y", r)])
                        o2 = ONT2[(r // 2) % 2]
                        o2n = "ONT2_%d" % ((r // 2) % 2)
                        for c4 in range(4):
                            P.op("pe", ("transpose", dict(
                                out=pstC[:, c4, 0:64], in_=onr[0:64, c4 * 128:(c4 + 1) * 128], identity=ident_b[0:64, 0:64])),
                                reads=[onrn, "ident_b"], writes=["pstC"])
                        P.op("act", ("activation", dict(
                            out=o2[:, :, (r % 2) * 64:(r % 2) * 64 + 64], in_=pstC[:, :, 0:64], func=AF.Copy)),
                            reads=["pstC"], writes=[o2n])
                        if r % 2 == 1:
                            t0 = 16 + 64 * (r - 1)
                            P.dma("sp", ("dma_start", dict(
                                out=scr["ONT"][:, :, t0:t0 + 128].rearrange("h p t -> p h t"), in_=o2[:])),
                                reads=[o2n], writes=[("ONT", 1 + r // 2)])
                    import os
                    NDBG = int(os.environ.get("NA_DBG", "9"))
                    NDBG = int(os.environ.get("NA_DBG", "9"))
                    if os.environ.get("NA_DUP"):
                        na_block(16 + 64 * 31, 64, 24, 0, 8, ONr[1], "ONr1", gi)
                    if NDBG >= 2:
                        na_block(0, 64, 0, 15, 1, ONr[0], "ONr0", gi)
                    for c4 in range(4 if NDBG >= 3 else 0):
                        P.op("pe", ("transpose", dict(
                            out=pstC[:, c4, 0:16], in_=ONr[0][0:16, c4 * 128:(c4 + 1) * 128], identity=ident_b[0:16, 0:16])),
                            reads=["ONr0", "ident_b"], writes=["pstC"])
                    if NDBG >= 3:
                        P.op("act", ("activation", dict(out=ONT2[0][:, :, 0:16], in_=pstC[:, :, 0:16], func=AF.Copy)),
                             reads=["pstC"], writes=["ONT2_0"])
                    else:
                        P.op("dve", ("memset", dict(ap=ONT2[0][:, :, 0:16], constant=0.0)), writes=["ONT2_0"])
                    P.dma("sp", ("dma_start", dict(out=scr["ONT"][:, :, 0:16].rearrange("h p t -> p h t"),
                                                     in_=ONT2[0][:, :, 0:16])), reads=["ONT2_0"], writes=[("ONT", 0)])
                    P.barrier()
                if stop_after == "C":
                    break
                with ExitStack() as pd:
                    alloc_wstage(pd)
                    Wa = sb(pd, "Wa", [128, 4, D], BF16)
                    Wb = sb(pd, "Wb", [128, 4, D], BF16)
                    Wo = sb(pd, "Wo", [128, 8, D], BF16)
                    odt = [sb(pd, "odt%d" % i, [128, 4, 128], BF16) for i in range(2)]
                    ont = [sb(pd, "ont%d" % i, [128, 4, 128], BF16) for i in range(2)]
                    gt = [sb(pd, "gt%d" % i, [128, 2048], BF16) for i in range(2)]
                    mf = sb(pd, "mf", [128, 2, D], F32)
                    mb = sb(pd, "mb", [128, D], BF16)
                    mT = sb(pd, "mT", [128, 8, 128], BF16)
                    pya = ps(pd, "pya", [128, 2, 512], F32)
                    pyb = ps(pd, "pyb", [128, 2, 512], F32)
                    pyo = ps(pd, "pyo", [128, 2, 512], F32)
                    pstD = ps(pd, "pstD", [128, 8, 128], BF16)
                    for half in range(2):
                        load_w_bf16(Wa, Wa[:, :, half * 512:(half + 1) * 512],
                                    dr["w_branch_a"][l].rearrange("(kc p) c -> p kc c", p=128)[:, :, half * 512:(half + 1) * 512],
                                    "Wa", (4, 512))
                        load_w_bf16(Wb, Wb[:, :, half * 512:(half + 1) * 512],
                                    dr["w_branch_b"][l].rearrange("(kc p) c -> p kc c", p=128)[:, :, half * 512:(half + 1) * 512],
                                    "Wb", (4, 512))
                        load_w_bf16(Wo, Wo[:, :, half * 512:(half + 1) * 512],
                                    dr["w_out"][l].rearrange("(kc p) c -> p kc c", p=128)[:, :, half * 512:(half + 1) * 512],
                                    "Wo", (8, 512))
                    for t in range(NT):
                        off, n = TILES[t]
                        od, on_, g_ = odt[t % 2], ont[t % 2], gt[t % 2]
                        odn, onn, gn = "odt%d" % (t % 2), "ont%d" % (t % 2), "gt%d" % (t % 2)
                        P.dma("sp", ("dma_start", dict(
                            out=od[:, :, 0:n], in_=scr["ODT"][:, :, off:off + n].rearrange("h p t -> p h t"))),
                            reads=[("ODT", ci) for ci in range(5)], writes=[odn])
                        P.dma("sp", ("dma_start", dict(
                            out=on_[:, :, 0:n], in_=scr["ONT"][:, :, off:off + n].rearrange("h p t -> p h t"))),
                            reads=[("ONT", i) for i in range(17)], writes=[onn])
                        P.dma("sp", ("dma_start", dict(out=g_[0:n, :], in_=scr["G"][off:off + n, :])),
                              reads=[("G", t, cc) for cc in range(6, 10)], writes=[gn])
                        for half in range(2):
                            for c4 in range(4):
                                P.op("pe", ("matmul", dict(
                                    out=pya[0:n, half, :], lhsT=od[:, c4, 0:n], rhs=Wa[:, c4, half * 512:(half + 1) * 512],
                                    start=(c4 == 0), stop=(c4 == 3))), reads=[odn, "Wa"], writes=["pya"])
                            for c4 in range(4):
                                P.op("pe", ("matmul", dict(
                                    out=pyb[0:n, half, :], lhsT=on_[:, c4, 0:n], rhs=Wb[:, c4, half * 512:(half + 1) * 512],
                                    start=(c4 == 0), stop=(c4 == 3))), reads=[onn, "Wb"], writes=["pyb"])
                        P.op("dve", ("tensor_tensor", dict(
                            out=mf[0:n, 0, :], in0=pya[0:n].rearrange("p a b -> p (a b)"), in1=g_[0:n, 0:1024], op=ALU.mult)),
                            reads=["pya", gn], writes=["mf0"])
                        P.op("dve", ("tensor_tensor", dict(
                            out=mf[0:n, 1, :], in0=pyb[0:n].rearrange("p a b -> p (a b)"), in1=g_[0:n, 1024:2048], op=ALU.mult)),
                            reads=["pyb", gn], writes=["mf1"])
                        P.op("dve", ("tensor_tensor", dict(
                            out=mb[0:n, :], in0=mf[0:n, 0, :], in1=mf[0:n, 1, :], op=ALU.add)),
                            reads=["mf0", "mf1"], writes=["mb"])
                        for c in range(8):
                            P.op("pe", ("transpose", dict(
                                out=pstD[:, c, 0:n], in_=mb[0:n, c * 128:(c + 1) * 128], identity=ident_b[0:n, 0:n])),
                                reads=["mb", "ident_b"], writes=["pstD"])
                        P.op("act", ("activation", dict(out=mT[:, :, 0:n], in_=pstD[:, :, 0:n], func=AF.Copy)),
                             reads=["pstD"], writes=["mT"])
                        for half in range(2):
                            for c in range(8):
                                P.op("pe", ("matmul", dict(
                                    out=pyo[0:n, half, :], lhsT=mT[:, c, 0:n], rhs=Wo[:, c, half * 512:(half + 1) * 512],
                                    start=(c == 0), stop=(c == 7))), reads=["mT", "Wo"], writes=["pyo"])
                        P.op("dve", ("tensor_tensor", dict(
                            out=H[0:n, t, :], in0=H[0:n, t, :], in1=pyo[0:n].rearrange("p a b -> p (a b)"), op=ALU.add)),
                            reads=["pyo", ("H", t)], writes=[("H", t)])
                    P.barrier()
                if stop_after == "D":
                    break
                with ExitStack() as pe_:
                    Wq = sb(pe_, "Wq", [128, 8, 2048], BF16)
                    keysT = sb(pe_, "keysT", [128, 16, 128], BF16)
                    gbc = sb(pe_, "gbcE", [128, D], F32)
                    P.dma("sp", ("dma_start", dict(out=gbc[:], in_=dr["norm_ffn"][l:l + 1, :].partition_broadcast(128))),
                          writes=["gbc"])
                    with ExitStack() as pw:
                        alloc_wstage(pw)
                        keysf = sb(pw, "keysf", [128, 16, 128], F32)
                        pkT = ps(pw, "pkT", [128, 4, 128], F32)
                        wq3 = dr["peer_wq"][l].rearrange("(kc p) c -> p kc c", p=128)
                        for c4 in range(4):
                            load_w_bf16(Wq, Wq[:, :, c4 * 512:(c4 + 1) * 512], wq3[:, :, c4 * 512:(c4 + 1) * 512], "Wq", (8, 512))
                        P.dma("sp", ("dma_start", dict(
                            out=keysf[:], in_=dr["peer_keys"][l].rearrange("h p k d -> k (h p) d"))), writes=["keysf"])
                        for g4 in range(4):
                            for i in range(4):
                                gq = 4 * g4 + i
                                P.op("pe", ("transpose", dict(
                                    out=pkT[:, i, :], in_=keysf[:, gq, :], identity=ident_f[:, :])),
                                    reads=["keysf", "ident_f"], writes=["pkT"])
                            P.op("act", ("activation", dict(out=keysT[:, 4 * g4:4 * g4 + 4, :], in_=pkT[:], func=AF.Copy)),
                                 reads=["pkT"], writes=["keysT"])
                        P.barrier()
                    CT = sb(pe_, "CT", [128, 8, 128], BF16)
                    qT = sb(pe_, "qT", [128, 16, 128], BF16)
                    ssb = sb(pe_, "ssb", [128, 16, 128], F32)
                    swk = sb(pe_, "swk", [128, 256], F32)
                    vals = sb(pe_, "vals", [128, 16, 16], F32)
                    idxu = sb(pe_, "idxu", [128, 16, 16], U32)
                    idxf = sb(pe_, "idxf", [128, 16, 16], F32)
                    cs_ = sb(pe_, "cs", [128, 8, 256], F32)
                    scv = sb(pe_, "scv", [128, 8, 16], F32)
                    posu = sb(pe_, "posu", [128, 8, 16], U32)
                    posf = sb(pe_, "posf", [128, 8, 16], F32)
                    ai_ = sb(pe_, "ai", [128, 8, 16], I32)
                    af_ = sb(pe_, "af", [128, 8, 16], F32)
                    bf_ = sb(pe_, "bf", [128, 8, 16], F32)
                    msk = sb(pe_, "msk", [128, 8, 16, 16], F32)
                    i1s = sb(pe_, "i1s", [128, 8, 16], F32)
                    i2s = sb(pe_, "i2s", [128, 8, 16], F32)
                    ef = sb(pe_, "ef", [128, 128], F32)
                    ei = [sb(pe_, "ei%d" % i, [128, 128], I32) for i in range(2)]
                    gg = sb(pe_, "gg", [128, 8, 16], F32)
                    gg2 = sb(pe_, "gg2", [128, 8, 16], F32)
                    abuf2 = sb(pe_, "abuf2", [128, D], BF16)
                    ei2 = sb(pe_, "ei2", [128, 128], I32)
                    hid = sb(pe_, "hid", [128, 128], F32)
                    gl = sb(pe_, "gl", [128, 128], F32)
                    wv = sb(pe_, "wv", [128, 128], F32)
                    djunk = sq_junk[:].bitcast(BF16)[:, 0:D]
                    NG = 8
                    gat = [sb(pe_, "gat%d" % i, [128, 2 * D], BF16) for i in range(NG)]
                    dg = [sb(pe_, "dg%d" % i, [128, 128], BF16) for i in range(4)]
                    pacc = ps(pe_, "pacc", [128, 2, 512], F32)
                    pstE = ps(pe_, "pstE", [128, 8, 128], BF16)
                    pq = [ps(pe_, "pq%d" % i, [128, 4, 128], F32) for i in range(4)]
                    for i in range(2):
                        P.op("pool", ("memset", dict(ap=ei[i][:], constant=0)), writes=["ei%d" % i])
                    gslot = [0]
                    abufs = [abuf, abuf2]
                    ggs = [gg, gg2]
                    eis = [ei[0], ei2]

                    def prologue(t):
                        off, n = TILES[t]
                        pb = t % 2
                        ab, abn = abufs[pb], "abuf%d" % pb
                        g_, gn = ggs[pb], "gg%d" % pb
                        if n == 128:
                            eit, ein = eis[pb], "eiN%d" % pb
                        else:
                            eit, ein = ei[1], "ei1"
                        st = []

                        def s_norm():
                            rms_rstd(H[0:n, t, :], n, 8, ("H", t))
                            P.op("dve", ("scalar_tensor_tensor", dict(
                                out=ab[0:n, :], in0=H[0:n, t, :], scalar=small[0:n, 8:9], in1=gbc[0:n, :],
                                op0=ALU.mult, op1=ALU.mult)), reads=[("H", t), "small", "gbc"], writes=[abn])
                            for c in range(8):
                                P.op("pe", ("transpose", dict(out=pstE[:, c, 0:n], in_=ab[0:n, c * 128:(c + 1) * 128],
                                                              identity=ident_b[0:n, 0:n])),
                                     reads=[abn, "ident_b"], writes=["pstE"])
                            P.op("act", ("activation", dict(out=CT[:, :, 0:n], in_=pstE[:, :, 0:n], func=AF.Copy)),
                                 reads=["pstE"], writes=["CT"])
                        st.append(s_norm)

                        def s_q(g4):
                            for gq in range(4 * g4, 4 * g4 + 4):
                                pqq = pq[gq // 4]
                                for kc in range(8):
                                    P.op("pe", ("matmul", dict(
                                        out=pqq[:, gq % 4, 0:n], lhsT=Wq[:, kc, gq * 128:(gq + 1) * 128], rhs=CT[:, kc, 0:n],
                                        start=(kc == 0), stop=(kc == 7))), reads=["Wq", "CT"], writes=["pq%d" % (gq // 4)])
                            P.op("act", ("activation", dict(
                                out=qT[:, 4 * g4:4 * g4 + 4, 0:n], in_=pq[g4][:, :, 0:n], func=AF.Copy)),
                                reads=["pq%d" % g4], writes=[("qT", g4)])
                        for g4 in range(4):
                            st.append(lambda g4=g4: s_q(g4))

                        def s_sc():
                            for gq in range(16):
                                pqq = pq[gq // 4]
                                P.op("pe", ("matmul", dict(
                                    out=pqq[0:n, gq % 4, :], lhsT=qT[:, gq, 0:n], rhs=keysT[:, gq, :], start=True, stop=True)),
                                    reads=[("qT", gq // 4), "keysT"], writes=["pq%d" % (gq // 4)])
                            for g4 in range(4):
                                P.op("act", ("activation", dict(
                                    out=ssb[0:n, 4 * g4:4 * g4 + 4, :], in_=pq[g4][0:n, :, :], func=AF.Copy)),
                                    reads=["pq%d" % g4], writes=[("ssb", g4)])
                        st.append(s_sc)

                        def s_top(g4):
                            for gq in range(4 * g4, 4 * g4 + 4):
                                P.op("dve", ("max", dict(out=vals[0:n, gq, 0:8], in_=ssb[0:n, gq, :])),
                                     reads=[("ssb", g4)], writes=["vals"])
                                P.op("dve", ("max_index", dict(out=idxu[0:n, gq, 0:8], in_max=vals[0:n, gq, 0:8],
                                                               in_values=ssb[0:n, gq, :])),
                                     reads=[("ssb", g4), "vals"], writes=["idxu"])
                                P.op("dve", ("match_replace", dict(
                                    out=swk[0:n, 0:128], in_to_replace=vals[0:n, gq, 0:8], in_values=ssb[0:n, gq, :],
                                    imm_value=-1e30)), reads=[("ssb", g4), "vals"], writes=["swk"])
                                P.op("dve", ("max", dict(out=vals[0:n, gq, 8:16], in_=swk[0:n, 0:128])),
                                     reads=["swk"], writes=["vals"])
                                P.op("dve", ("max_index", dict(out=idxu[0:n, gq, 8:16], in_max=vals[0:n, gq, 8:16],
                                                               in_values=swk[0:n, 0:128])),
                                     reads=["swk", "vals"], writes=["idxu"])
                        for g4 in range(4):
                            st.append(lambda g4=g4: s_top(g4))

                        v4 = vals[:].rearrange("p (h two) k -> p h two k", two=2)
                        i4 = idxf[:].rearrange("p (h two) k -> p h two k", two=2)

                        def s_cs():
                            P.op("dve", ("tensor_copy", dict(out=idxf[0:n], in_=idxu[0:n])), reads=["idxu"], writes=["idxf"])
                            P.op("dve", ("tensor_tensor", dict(
                                out=cs_[0:n].rearrange("p h (a b) -> p h a b", a=16),
                                in0=v4[0:n, :, 0, :].unsqueeze(3).to_broadcast([n, 8, 16, 16]),
                                in1=v4[0:n, :, 1, :].unsqueeze(2).to_broadcast([n, 8, 16, 16]), op=ALU.add)),
                                reads=["vals"], writes=["cs"])
                        st.append(s_cs)

                        def s_top2(h0):
                            for h in range(h0, h0 + 4):
                                P.op("dve", ("max", dict(out=scv[0:n, h, 0:8], in_=cs_[0:n, h, :])),
                                     reads=["cs"], writes=["scv"])
                                P.op("dve", ("max_index", dict(out=posu[0:n, h, 0:8], in_max=scv[0:n, h, 0:8],
                                                               in_values=cs_[0:n, h, :])),
                                     reads=["cs", "scv"], writes=["posu"])
                                P.op("dve", ("match_replace", dict(
                                    out=swk[0:n, :], in_to_replace=scv[0:n, h, 0:8], in_values=cs_[0:n, h, :], imm_value=-1e30)),
                                    reads=["cs", "scv"], writes=["swk"])
                                P.op("dve", ("max", dict(out=scv[0:n, h, 8:16], in_=swk[0:n, :])),
                                     reads=["swk"], writes=["scv"])
                                P.op("dve", ("max_index", dict(out=posu[0:n, h, 8:16], in_max=scv[0:n, h, 8:16],
                                                               in_values=swk[0:n, :])),
                                     reads=["swk", "scv"], writes=["posu"])
                        st.append(lambda: s_top2(0))
                        st.append(lambda: s_top2(4))

                        def s_dec():
                            P.op("dve", ("tensor_copy", dict(out=posf[0:n], in_=posu[0:n])), reads=["posu"], writes=["posf"])
                            P.op("dve", ("tensor_scalar", dict(out=ai_[0:n], in0=posf[0:n], scalar1=-7.5, scalar2=1.0 / 16,
                                                               op0=ALU.add, op1=ALU.mult)), reads=["posf"], writes=["ai"])
                            P.op("dve", ("tensor_copy", dict(out=af_[0:n], in_=ai_[0:n])), reads=["ai"], writes=["af"])
                            P.op("dve", ("scalar_tensor_tensor", dict(
                                out=bf_[0:n], in0=af_[0:n], scalar=-16.0, in1=posf[0:n], op0=ALU.mult, op1=ALU.add)),
                                reads=["af", "posf"], writes=["bf"])
                            for (src_, two, dst_, dn) in ((af_, 0, i1s, "i1s"), (bf_, 1, i2s, "i2s")):
                                P.op("dve", ("tensor_tensor", dict(
                                    out=msk[0:n], in0=src_[0:n].unsqueeze(3).to_broadcast([n, 8, 16, 16]),
                                    in1=iota16[0:n, :].unsqueeze(1).unsqueeze(1).to_broadcast([n, 8, 16, 16]), op=ALU.is_equal)),
                                    reads=["af", "bf", "iota16"], writes=["msk"])
                                P.op("dve", ("tensor_tensor", dict(
                                    out=msk[0:n], in0=msk[0:n],
                                    in1=i4[0:n, :, two, :].unsqueeze(2).to_broadcast([n, 8, 16, 16]), op=ALU.mult)),
                                    reads=["msk", "idxf"], writes=["msk"])
                                P.op("dve", ("tensor_reduce", dict(
                                    out=dst_[0:n], in_=msk[0:n], axis=AX.X, op=ALU.add)), reads=["msk"], writes=[dn])
                            P.op("dve", ("scalar_tensor_tensor", dict(
                                out=ef[0:n, :].rearrange("p (h k) -> p h k", h=8), in0=i1s[0:n], scalar=128.0, in1=i2s[0:n],
                                op0=ALU.mult, op1=ALU.add)), reads=["i1s", "i2s"], writes=["ef"])
                            P.op("dve", ("tensor_scalar", dict(out=eit[0:n, :], in0=ef[0:n, :], scalar1=float(l * NEXP),
                                                               scalar2=None, op0=ALU.add)), reads=["ef"], writes=[ein])
                        st.append(s_dec)

                        def s_gate():
                            P.op("dve", ("tensor_tensor", dict(
                                out=g_[0:n], in0=scv[0:n], in1=scv[0:n, :, 0:1].to_broadcast([n, 8, 16]), op=ALU.subtract)),
                                reads=["scv"], writes=[gn])
                            P.op("act", ("activation", dict(out=g_[0:n], in_=g_[0:n], func=AF.Exp)),
                                 reads=[gn], writes=[gn])
                            P.op("dve", ("tensor_reduce", dict(out=small[0:n, 32:40], in_=g_[0:n], axis=AX.X, op=ALU.add)),
                                 reads=[gn], writes=["small"])
                            P.op("dve", ("reciprocal", dict(out=small[0:n, 32:40], in_=small[0:n, 32:40])),
                                 reads=["small"], writes=["small"])
                            P.op("dve", ("tensor_tensor", dict(
                                out=g_[0:n], in0=g_[0:n], in1=small[0:n, 32:40].unsqueeze(2).to_broadcast([n, 8, 16]), op=ALU.mult)),
                                reads=[gn, "small"], writes=[gn])
                        st.append(s_gate)
                        return st

                    def run_slots(t, nxt_stages):
                        off, n = TILES[t]
                        pb = t % 2
                        ab, abn = abufs[pb], "abuf%d" % pb
                        g_, gn = ggs[pb], "gg%d" % pb
                        if n == 128:
                            eit, ein = eis[pb], "eiN%d" % pb
                        else:
                            eit, ein = ei[1], "ei1"
                        gflat = g_[:].rearrange("p h k -> p (h k)")
                        slots = []
                        every = 7

                        def slot_tail(j, sl):
                            dgi = j % 4
                            P.op("act", ("activation", dict(
                                out=wv[0:n, j:j + 1], in_=gl[0:n, j:j + 1], func=AF.Copy, scale=gflat[0:n, j:j + 1])),
                                reads=[("gl", j), gn], writes=[("wv", j)])
                            P.op("act", ("activation", dict(
                                out=dg[dgi][0:n, 0:n], in_=ident_b[0:n, 0:n], func=AF.Copy, scale=wv[0:n, j:j + 1])),
                                reads=["ident_b", ("wv", j)], writes=["dg%d" % dgi])
                            for half in range(2):
                                P.op("pe", ("matmul", dict(
                                    out=pacc[0:n, half, :], lhsT=dg[dgi][0:n, 0:n],
                                    rhs=gat[sl][0:n, D + half * 512:D + (half + 1) * 512],
                                    start=(j == 0), stop=(j == 127))),
                                    reads=["dg%d" % dgi, "gat%d" % sl], writes=["pacc"])
                        for j in range(128):
                            sl = gslot[0] % NG
                            gslot[0] += 1
                            P.dma("pool", ("indirect_dma_start", dict(
                                out=gat[sl][:], out_offset=None, in_=scr["UV"],
                                in_offset=bass.IndirectOffsetOnAxis(ap=eit[:, j:j + 1], axis=0))),
                                reads=[ein] + ([("UV", l, b_) for b_ in range(NEXP // 512)] if (t == 0 and j == 0) else []),
                                writes=["gat%d" % sl], semkey=gsem[sl])
                            P.op("dve", ("scalar_tensor_tensor", dict(
                                out=djunk[0:n, :], in0=gat[sl][0:n, 0:D], scalar=1.0, in1=ab[0:n, :], op0=ALU.mult, op1=ALU.mult,
                                accum_out=hid[0:n, j:j + 1])), reads=["gat%d" % sl, abn], writes=["sq_junk", ("hid", j)])
                            P.op("act", ("activation", dict(out=gl[0:n, j:j + 1], in_=hid[0:n, j:j + 1], func=AF.Gelu)),
                                 reads=[("hid", j)], writes=[("gl", j)])
                            if j >= 1:
                                slot_tail(j - 1, slots[j - 1])
                            slots.append(sl)
                            if nxt_stages and j % every == every - 1:
                                nxt_stages.pop(0)()
                        slot_tail(127, slots[127])
                        while nxt_stages:
                            nxt_stages.pop(0)()
                        P.op("dve", ("tensor_tensor", dict(
                            out=H[0:n, t, :], in0=H[0:n, t, :], in1=pacc[0:n].rearrange("p a b -> p (a b)"), op=ALU.add)),
                            reads=["pacc", ("H", t)], writes=[("H", t)])

                    for st_ in prologue(0):
                        st_()
                    for t in range(NT):
                        run_slots(t, prologue(t + 1) if t + 1 < NT else [])
                    P.barrier()
            if stop_after in ("B", "C"):
                continue
            if stop_after is not None:
                for t in range(1, NT):
                    P.dma("sp", ("dma_start", dict(out=y_d[s, 128 * (t - 1):128 * t, :], in_=H[:, t, :])),
                          reads=[("H", t)], writes=[("y", s, t)])
                continue
            for t in range(1, NT):
                rms_rstd(H[:, t, :], 128, 8, ("H", t))
                P.op("dve", ("scalar_tensor_tensor", dict(
                    out=sq_junk[:, :], in0=H[:, t, :], scalar=small[:, 8:9], in1=gfin_bc[:, :],
                    op0=ALU.mult, op1=ALU.mult)), reads=[("H", t), "small", "gfin_bc"], writes=["sq_junk"])
                P.dma("sp", ("dma_start", dict(out=y_d[s, 128 * (t - 1):128 * t, :], in_=sq_junk[:, :])),
                      reads=["sq_junk"], writes=[("y", s, t)])
            P.barrier()
        P.barrier()
        P.emit()
    return nc, P


def make_in_maps(inputs, nseq=SEQ_PER_CORE, ncores=NCORES):
    consts = host_consts()
    xp = np.asarray(inputs["x_prompt"], dtype=np.float32)
    xs = np.asarray(inputs["x_sample"], dtype=np.float32)
    allx = np.concatenate([xp, xs], axis=0)
    maps = []
    for c in range(ncores):
        m = {"x": np.ascontiguousarray(allx[c * nseq:(c + 1) * nseq])}
        for k, shp in WEIGHT_SHAPES.items():
            m[k] = np.ascontiguousarray(np.asarray(inputs[k], dtype=np.float32).reshape(shp))
        m.update(consts)
        maps.append(m)
    return maps


def kernel(**inputs):
    nc, _ = build()
    maps = make_in_maps(inputs)
    res = run_bass_kernel_spmd(nc, maps, core_ids=list(range(NCORES)))
    ys = np.concatenate([r["y"] for r in res.results], axis=0)
    nb = np.asarray(inputs["x_prompt"]).shape[0]
    return (np.ascontiguousarray(ys[:nb]), np.ascontiguousarray(ys[nb:]))
```

```python
import math
import numpy as np
from contextlib import ExitStack
import concourse.bass as bass
import concourse.mybir as mybir
from concourse.bass_utils import run_bass_kernel_spmd

F32 = mybir.dt.float32
BF16 = mybir.dt.bfloat16
I32 = mybir.dt.int32
U32 = mybir.dt.uint32
ALU = mybir.AluOpType
AF = mybir.ActivationFunctionType
AX = mybir.AxisListType

D = 1024
NMETA = 16
SEQ = 2048
L = SEQ + NMETA
DEPTH = 4
NCORES = 8
SEQ_PER_CORE = 5
EPS = 1e-6
TILES = [(0, 16)] + [(16 + 128 * i, 128) for i in range(16)]
NT = len(TILES)
QCHUNKS = [(16 + 512 * c, 512, [1 + 4 * c + j for j in range(4)]) for c in range(4)] + [(0, 16, [0])]
NEXP = 16384
ENGS = ("pe", "act", "dve", "pool", "sp")


class Prog:
    def __init__(self, nc, es, n_dma_sems=24):
        self.nc = nc
        self.es = es
        self.q = {e: [] for e in ENGS}
        self.cnt = {e: 0 for e in ENGS}
        self.waited = {e: {} for e in ENGS}
        self.res_w = {}
        self.res_r = {}
        self.sem = {}
        for e in ENGS:
            self.sem[e] = es.enter_context(nc.semaphore("s_" + e))
        self.dma_sems = []
        self.dma_val = {}
        for i in range(n_dma_sems):
            self.dma_sems.append(self.new_dma_sem("dma%d" % i))
        self.dma_rr = 0
        self.ninstr = 0

    def new_dma_sem(self, name):
        self.sem[name] = self.es.enter_context(self.nc.semaphore("s_" + name))
        self.dma_val[name] = 0
        return name

    def _deps(self, eng, reads, writes):
        ev = {}
        for r in reads:
            e = self.res_w.get(r)
            if e is not None and ev.get(e[0], 0) < e[1]:
                ev[e[0]] = e[1]
        for w in writes:
            e = self.res_w.get(w)
            if e is not None and ev.get(e[0], 0) < e[1]:
                ev[e[0]] = e[1]
            rr = self.res_r.get(w)
            if rr:
                for k, v in rr.items():
                    if ev.get(k, 0) < v:
                        ev[k] = v
        waits = []
        wd = self.waited[eng]
        for k, v in ev.items():
            if k == "pe" and eng == "pe":
                continue
            if wd.get(k, 0) < v:
                wd[k] = v
                waits.append((k, v))
        return waits

    def _commit(self, event, reads, writes):
        for w in writes:
            self.res_w[w] = event
            self.res_r[w] = {}
        k, v = event
        for r in reads:
            d = self.res_r.setdefault(r, {})
            if d.get(k, 0) < v:
                d[k] = v

    def op(self, eng, fn, reads=(), writes=()):
        waits = self._deps(eng, reads, writes)
        self.cnt[eng] += 1
        event = (eng, self.cnt[eng])
        self.q[eng].append((waits, fn, (eng, 1)))
        self._commit(event, reads, writes)
        self.ninstr += 1

    def dma(self, eng, fn, reads=(), writes=(), semkey=None):
        if semkey is None:
            semkey = self.dma_sems[self.dma_rr % len(self.dma_sems)]
            self.dma_rr += 1
        waits = self._deps(eng, reads, writes)
        pv = self.dma_val[semkey]
        wd = self.waited[eng]
        if pv > 0 and wd.get(semkey, 0) < pv:
            wd[semkey] = pv
            waits.append((semkey, pv))
        self.dma_val[semkey] = pv + 16
        event = (semkey, pv + 16)
        self.q[eng].append((waits, fn, (semkey, 16)))
        self._commit(event, reads, writes)
        self.ninstr += 1
        return event

    def barrier(self):
        tot = dict(self.dma_val)
        for e in ENGS:
            tot[e] = self.cnt[e]
        for eng in ENGS:
            waits = []
            wd = self.waited[eng]
            for k, v in tot.items():
                if v > 0 and k != eng and wd.get(k, 0) < v:
                    wd[k] = v
                    waits.append((k, v))
            if waits:
                self.q[eng].append((waits, None, None))
        self.res_w = {}
        self.res_r = {}

    def emit(self):
        nc = self.nc
        sem = self.sem
        q = self.q
        with nc.Block() as block:
            def run(engobj, lst):
                for waits, fn, inc in lst:
                    for k, v in waits:
                        engobj.wait_ge(sem[k], v)
                    if fn is not None:
                        getattr(engobj, fn[0])(**fn[1]).then_inc(sem[inc[0]], inc[1])

            @block.tensor
            def _(e):
                run(e, q["pe"])

            @block.scalar
            def _(e):
                run(e, q["act"])

            @block.vector
            def _(e):
                run(e, q["dve"])

            @block.gpsimd
            def _(e):
                run(e, q["pool"])

            @block.sync
            def _(e):
                run(e, q["sp"])


def host_consts():
    c = {}
    c["c_ident"] = np.eye(128, dtype=np.float32)
    c["c_iota16"] = np.tile(np.arange(16, dtype=np.float32)[None, :], (128, 1))
    half = 32
    inv = 1.0 / (10000.0 ** (np.arange(half, dtype=np.float64) * 2.0 / 64.0))
    c["c_invf"] = np.tile(inv, 4).reshape(128, 1).astype(np.float32) / np.float32(2 * math.pi)
    cc = np.arange(64)
    cs = np.clip(cc - 8, 0, 48)
    k = np.arange(64)
    c["c_colmask"] = ((k[:, None] >= cs[None, :]) & (k[:, None] < cs[None, :] + 16)).astype(np.float32)
    c["c_antiI"] = np.ascontiguousarray(np.eye(31, dtype=np.float32)[::-1])
    c["c_pos"] = np.tile(np.arange(L, dtype=np.float32)[None, :], (128, 1))
    return c


CONST_SHAPES = {"c_ident": [128, 128], "c_iota16": [128, 16], "c_invf": [128, 1],
                "c_colmask": [64, 64], "c_antiI": [31, 31], "c_pos": [128, L]}

WEIGHT_SHAPES = {
    "meta_tokens": [16, D], "norm_mix": [4, D], "w_in": [4, D, 5120],
    "lambda_q1": [4, 64], "lambda_k1": [4, 64], "lambda_q2": [4, 64], "lambda_k2": [4, 64],
    "subln_gain": [4, 128], "na_rpb": [4, 8, 15, 31], "w_branch_a": [4, 512, D],
    "w_branch_b": [4, 512, D], "w_out": [4, D, D], "norm_ffn": [4, D],
    "peer_wq": [4, D, 2048], "peer_keys": [4, 8, 2, 128, 128],
    "peer_u": [4, NEXP, D], "peer_v": [4, NEXP, D], "norm_final": [1, D],
}


def build(nseq=SEQ_PER_CORE, depth=DEPTH, dbg=False, stop_after=None):
    nc = bass.Bass("TRN2", target_bir_lowering=False)
    dr = {}
    dr["x"] = nc.dram_tensor("x", [nseq, SEQ, D], F32, kind="ExternalInput").ap()
    for k, shp in WEIGHT_SHAPES.items():
        dr[k] = nc.dram_tensor(k, shp, F32, kind="ExternalInput").ap()
    for k, shp in CONST_SHAPES.items():
        dr[k] = nc.dram_tensor(k, shp, F32, kind="ExternalInput").ap()
    y_d = nc.dram_tensor("y", [nseq, SEQ, D], F32, kind="ExternalOutput").ap()
    scr = {}
    for nm in ("QTda", "KTda", "QTna", "KTna", "ODT", "ONT"):
        scr[nm] = nc.dram_tensor("scr_" + nm, [4, 128, L], BF16).ap()
    scr["Vda"] = nc.dram_tensor("scr_Vda", [L, 4, 130], BF16).ap()
    scr["Vna"] = nc.dram_tensor("scr_Vna", [L, 8, 66], BF16).ap()
    scr["G"] = nc.dram_tensor("scr_G", [L, 2048], BF16).ap()
    PW = 160
    Dsk_t = nc.dram_tensor("scr_Dsk", [4, 120, 64, PW], F32)
    scr["Dsk"] = Dsk_t.ap()
    scr["E"] = nc.dram_tensor("scr_E", [4, 64, 7680], BF16).ap()
    scr["UV"] = nc.dram_tensor("scr_UV", [4 * NEXP, 2048], BF16).ap()

    with ExitStack() as es:
        P = Prog(nc, es)
        gsem = [P.new_dma_sem("gat%d" % i) for i in range(10)]

        uid = [0]

        def sb(es_, name, shape, dt):
            uid[0] += 1
            if shape[0] < 128:
                t_ = es_.enter_context(nc.sbuf_tensor("sb_%s_%d" % (name, uid[0]), [128] + list(shape[1:]), dt))
                return t_[0:shape[0]]
            return es_.enter_context(nc.sbuf_tensor("sb_%s_%d" % (name, uid[0]), shape, dt))

        def ps(es_, name, shape, dt):
            uid[0] += 1
            return es_.enter_context(nc.psum_tensor("ps_%s_%d" % (name, uid[0]), shape, dt))

        H = sb(es, "H", [128, NT, D], F32)
        ident_f = sb(es, "ident_f", [128, 128], F32)
        ident_b = sb(es, "ident_b", [128, 128], BF16)
        iota16 = sb(es, "iota16", [128, 16], F32)
        cosT = sb(es, "cosT", [128, L], BF16)
        sinT = sb(es, "sinT", [128, L], BF16)
        lam = sb(es, "lam", [128, 8], F32)
        gain_bc = sb(es, "gain_bc", [128, 4, 128], F32)
        gfin_bc = sb(es, "gfin_bc", [128, D], F32)
        small = sb(es, "small", [128, 64], F32)

        P.dma("sp", ("dma_start", dict(out=ident_f[:], in_=dr["c_ident"])), writes=["ident_f"])
        P.dma("sp", ("dma_start", dict(out=iota16[:], in_=dr["c_iota16"])), writes=["iota16"])
        P.op("dve", ("tensor_copy", dict(out=ident_b[:], in_=ident_f[:])), reads=["ident_f"], writes=["ident_b"])
        P.dma("sp", ("dma_start", dict(out=gfin_bc[:], in_=dr["norm_final"].partition_broadcast(128))),
              writes=["gfin_bc"])
        with ExitStack() as s1:
            invf = sb(s1, "invf", [128, 1], F32)
            pos = sb(s1, "pos", [128, L], F32)
            yy = sb(s1, "yy", [128, L], F32)
            yi = sb(s1, "yi", [128, L], I32)
            yf = sb(s1, "yf", [128, L], F32)
            lq = sb(s1, "lq", [128, 4, 4, 64], F32)
            junk = sb(s1, "junk", [128, 64], F32)
            P.dma("sp", ("dma_start", dict(out=invf[:], in_=dr["c_invf"])), writes=["invf"])
            P.dma("sp", ("dma_start", dict(out=pos[:], in_=dr["c_pos"])), writes=["pos"])
            for which, tab in ((0, sinT), (1, cosT)):
                P.op("dve", ("tensor_scalar", dict(
                    out=yy[:], in0=pos[:], scalar1=invf[:, 0:1], scalar2=0.25 * which,
                    op0=ALU.mult, op1=ALU.add)), reads=["pos", "invf"], writes=["yy"])
                P.op("dve", ("tensor_copy", dict(out=yi[:], in_=yy[:])), reads=["yy"], writes=["yi"])
                P.op("dve", ("tensor_copy", dict(out=yf[:], in_=yi[:])), reads=["yi"], writes=["yf"])
                P.op("dve", ("tensor_tensor", dict(out=yy[:], in0=yy[:], in1=yf[:], op=ALU.subtract)),
                     reads=["yy", "yf"], writes=["yy"])
                P.op("dve", ("tensor_scalar", dict(out=yy[:], in0=yy[:], scalar1=0.5, scalar2=-0.5,
                                                      op0=ALU.min, op1=ALU.max)), reads=["yy"], writes=["yy"])
                P.op("act", ("activation", dict(out=tab[:], in_=yy[:], func=AF.Sin,
                                                            scale=2 * math.pi)),
                     reads=["yy"], writes=["tab%d" % which])
            for i, nm in enumerate(("lambda_q1", "lambda_k1", "lambda_q2", "lambda_k2")):
                for l in range(4):
                    P.dma("sp", ("dma_start", dict(
                        out=lq[:, i, l, :], in_=dr[nm][l:l + 1, :].partition_broadcast(128))),
                        writes=["lq"])
            for l in range(4):
                lam_init = 0.8 - 0.6 * math.exp(-0.3 * l)
                P.op("dve", ("scalar_tensor_tensor", dict(
                    out=junk[:], in0=lq[:, 0, l, :], scalar=1.0, in1=lq[:, 1, l, :],
                    op0=ALU.mult, op1=ALU.mult, accum_out=small[:, 0:1])),
                    reads=["lq"], writes=["junk", "small"])
                P.op("dve", ("scalar_tensor_tensor", dict(
                    out=junk[:], in0=lq[:, 2, l, :], scalar=1.0, in1=lq[:, 3, l, :],
                    op0=ALU.mult, op1=ALU.mult, accum_out=small[:, 1:2])),
                    reads=["lq", "small"], writes=["junk", "small"])
                P.op("act", ("activation", dict(out=small[:, 2:4], in_=small[:, 0:2], func=AF.Exp)),
                     reads=["small"], writes=["small"])
                P.op("dve", ("scalar_tensor_tensor", dict(
                    out=lam[:, l:l + 1], in0=small[:, 2:3], scalar=lam_init, in1=small[:, 3:4],
                    op0=ALU.add, op1=ALU.subtract)), reads=["small"], writes=["lam"])
                P.op("dve", ("tensor_scalar", dict(
                    out=lam[:, 4 + l:5 + l], in0=lam[:, l:l + 1], scalar1=-1.0, scalar2=None, op0=ALU.mult)),
                    reads=["lam"], writes=["lam"])
                P.dma("sp", ("dma_start", dict(
                    out=gain_bc[:, l, :], in_=dr["subln_gain"][l:l + 1, :].partition_broadcast(128))),
                    writes=["gain_bc"])
                P.op("dve", ("tensor_scalar", dict(
                    out=gain_bc[:, l, :], in0=gain_bc[:, l, :], scalar1=1.0 - lam_init, scalar2=None, op0=ALU.mult)),
                    reads=["gain_bc"], writes=["gain_bc"])
            P.barrier()
        with ExitStack() as s1:
          if stop_after != "S0":
              rT = sb(s1, "rT", [31, 120], F32)
              antiI = sb(s1, "antiI", [31, 31], F32)
              colmask = sb(s1, "colmask", [64, 64], F32)
              Ppad = sb(s1, "Ppad", [120, PW], F32)
              Esk = sb(s1, "Esk", [64, 120, 64], F32)
              Ebf = sb(s1, "Ebf", [64, 120, 64], BF16)
              prev = ps(s1, "prev", [128, 512], F32)
              P.dma("sp", ("dma_start", dict(out=antiI[:], in_=dr["c_antiI"])), writes=["antiI"])
              P.dma("sp", ("dma_start", dict(out=colmask[:], in_=dr["c_colmask"])), writes=["colmask"])
              for l in range(depth):
                  P.dma("sp", ("dma_start", dict(
                      out=rT[:], in_=dr["na_rpb"][l].rearrange("h r j -> j (h r)"), allow_slow_non_contiguous=True)),
                      writes=["rT"])
                  P.op("pe", ("matmul", dict(out=prev[0:120, 0:31], lhsT=rT[:, :], rhs=antiI[:, :],
                                                start=True, stop=True)), reads=["rT", "antiI"], writes=["prev"])
                  P.op("dve", ("memset", dict(ap=Ppad[:], constant=-30000.0)), writes=["Ppad"])
                  P.op("dve", ("tensor_copy", dict(out=Ppad[:, 64:95], in_=prev[0:120, 0:31])),
                       reads=["prev"], writes=["Ppad"])
                  P.op("act", ("activation", dict(out=Ppad[:], in_=Ppad[:], func=AF.Exp)),
                       reads=["Ppad"], writes=["Ppad"])
                  P.dma("sp", ("dma_start", dict(
                      out=scr["Dsk"][l], in_=Ppad[:].unsqueeze(1).to_broadcast([120, 64, PW]))),
                      reads=["Ppad"], writes=["Dsk"])
                  src = bass.AP(Dsk_t, l * 120 * 64 * PW + 79, [[PW - 1, 64], [64 * PW, 120], [1, 64]])
                  P.dma("sp", ("dma_start", dict(out=Esk[:], in_=src)), reads=["Dsk"], writes=["Esk"])
                  P.op("dve", ("tensor_tensor", dict(
                      out=Ebf[:], in0=Esk[:], in1=colmask[:].unsqueeze(1).to_broadcast([64, 120, 64]),
                      op=ALU.mult)), reads=["Esk", "colmask"], writes=["Ebf"])
                  P.dma("sp", ("dma_start", dict(
                      out=scr["E"][l], in_=Ebf[:].rearrange("k a q -> k (a q)"))), reads=["Ebf"], writes=["E_d"])
              P.barrier()

        with ExitStack() as s1:
          if stop_after not in ("S0", "S"):
            u32 = [sb(s1, "u32_%d" % i, [128, 4, D], F32) for i in range(2)]
            v32 = [sb(s1, "v32_%d" % i, [128, 4, D], F32) for i in range(2)]
            uv16 = [sb(s1, "uv16_%d" % i, [128, 4, 2, D], BF16) for i in range(2)]
            it = 0
            for l in range(depth):
                for blk in range(NEXP // 512):
                    i = it % 2
                    it += 1
                    e0 = blk * 512
                    P.dma("sp", ("dma_start", dict(
                        out=u32[i][:], in_=dr["peer_u"][l, e0:e0 + 512, :].rearrange("(p k) d -> p k d", k=4))),
                        writes=["u32_%d" % i])
                    P.dma("sp", ("dma_start", dict(
                        out=v32[i][:], in_=dr["peer_v"][l, e0:e0 + 512, :].rearrange("(p k) d -> p k d", k=4))),
                        writes=["v32_%d" % i])
                    P.op("dve", ("tensor_copy", dict(out=uv16[i][:, :, 0, :], in_=u32[i][:])),
                         reads=["u32_%d" % i], writes=["uv16u_%d" % i])
                    P.op("act", ("activation", dict(out=uv16[i][:, 0:2, 1, :], in_=v32[i][:, 0:2, :], func=AF.Copy)),
                         reads=["v32_%d" % i], writes=["uv16va_%d" % i])
                    P.op("pool", ("tensor_copy", dict(out=uv16[i][:, 2:4, 1, :], in_=v32[i][:, 2:4, :])),
                         reads=["v32_%d" % i], writes=["uv16vb_%d" % i])
                    r0 = l * NEXP + e0
                    P.dma("sp", ("dma_start", dict(
                        out=scr["UV"][r0:r0 + 512, :].rearrange("(p k) c -> p k c", k=4),
                        in_=uv16[i][:].rearrange("p k two d -> p k (two d)"))),
                        reads=["uv16u_%d" % i, "uv16va_%d" % i, "uv16vb_%d" % i], writes=[("UV", l, blk)])
            P.barrier()
        def rms_rstd(Xap, n, col, tag):
            width = Xap.shape[-1]
            P.op("act", ("activation", dict(out=sq_junk[0:n, 0:width], in_=Xap, func=AF.Square,
                                               accum_out=small[0:n, col:col + 1])),
                 reads=[tag], writes=["sq_junk", "small"])
            P.op("dve", ("tensor_scalar", dict(out=small[0:n, col:col + 1], in0=small[0:n, col:col + 1],
                                                  scalar1=1.0 / width, scalar2=EPS, op0=ALU.mult, op1=ALU.add)),
                 reads=["small"], writes=["small"])
            P.op("act", ("activation", dict(out=small[0:n, col:col + 1], in_=small[0:n, col:col + 1],
                                               func=AF.Sqrt)), reads=["small"], writes=["small"])
            P.op("dve", ("reciprocal", dict(out=small[0:n, col:col + 1], in_=small[0:n, col:col + 1])),
                 reads=["small"], writes=["small"])

        sq_junk = sb(es, "sq_junk", [128, D], F32)
        wstage = [None, None]
        wst_i = [0]

        def alloc_wstage(es_):
            for i in range(2):
                wstage[i] = sb(es_, "wstage%d" % i, [128, 4096], F32)

        def load_w_bf16(dst_tile, dst_ap, src_ap, dst_name, shape3):
            a, b = shape3
            i = wst_i[0] % 2
            wst_i[0] += 1
            st = wstage[i]
            stv = st[:, 0:a * b].rearrange("p (a b) -> p a b", a=a)
            P.dma("sp", ("dma_start", dict(out=stv, in_=src_ap)), writes=["wstage%d" % i])
            P.op("pool", ("tensor_copy", dict(out=dst_ap, in_=stv)), reads=["wstage%d" % i], writes=[dst_name])
            return stv, i

        def norm_transpose(t, gbc, AT_dst, pst, xkeep=None, tag_out="AT"):
            off, n = TILES[t]
            rms_rstd(H[0:n, t, :], n, 8, ("H", t))
            dst = xkeep if xkeep is not None else abuf
            P.op("dve", ("scalar_tensor_tensor", dict(
                out=dst[0:n, :], in0=H[0:n, t, :], scalar=small[0:n, 8:9], in1=gbc[0:n, :],
                op0=ALU.mult, op1=ALU.mult)), reads=[("H", t), "small", "gbc"], writes=["abuf"])
            if xkeep is not None:
                P.op("act", ("activation", dict(out=abuf[0:n, :], in_=xkeep[0:n, :], func=AF.Copy)),
                     reads=["abuf"], writes=["abuf_b"])
                rtag = "abuf_b"
            else:
                rtag = "abuf"
            for c in range(8):
                P.op("pe", ("transpose", dict(out=pst[:, c, 0:n], in_=abuf[0:n, c * 128:(c + 1) * 128],
                                                      identity=ident_b[0:n, 0:n])),
                     reads=[rtag, "ident_b"], writes=["pst"])
            P.op("act", ("activation", dict(out=AT_dst, in_=pst[:, :, 0:n], func=AF.Copy)),
                 reads=["pst"], writes=[tag_out])

        abuf = sb(es, "abuf", [128, D], BF16)

        for s in range(nseq if stop_after not in ("S0", "S") else 0):
            P.dma("sp", ("dma_start", dict(out=H[0:16, 0, :], in_=dr["meta_tokens"])), writes=[("H", 0)])
            for t in range(1, NT):
                P.dma("sp", ("dma_start", dict(out=H[:, t, :], in_=dr["x"][s, 128 * (t - 1):128 * t, :])),
                      writes=[("H", t)])
            for l in range(depth):
                with ExitStack() as pa:
                    alloc_wstage(pa)
                    AT = sb(pa, "AT", [128, 8, L], BF16)
                    gbc = sb(pa, "gbc", [128, D], F32)
                    abuf_f = None
                    wbf = [sb(pa, "wbf%d" % i, [128, 8, 512], BF16) for i in range(2)]
                    wrot = sb(pa, "wrot", [128, 8, 512], BF16)
                    ev_f = sb(pa, "ev_f", [128, 2, 512], F32)
                    ev_b = [sb(pa, "ev_b%d" % i, [128, 520], BF16) for i in range(2)]
                    vda_b = [sb(pa, "vda_b%d" % i, [128, 4, 130], BF16) for i in range(2)]
                    vna_b = [sb(pa, "vna_b%d" % i, [128, 8, 66], BF16) for i in range(2)]
                    pst = ps(pa, "pstA", [128, 8, 128], BF16)
                    pz = [ps(pa, "pzA%d" % i, [128, 512], F32) for i in range(4)]
                    P.dma("sp", ("dma_start", dict(out=gbc[:], in_=dr["norm_mix"][l:l + 1, :].partition_broadcast(128))),
                          writes=["gbc"])
                    for i in range(2):
                        P.op("pool", ("memset", dict(ap=vda_b[i][:], constant=1.0)), writes=["vda_b%d" % i])
                        P.op("pool", ("memset", dict(ap=vna_b[i][:], constant=1.0)), writes=["vna_b%d" % i])
                    for t in range(NT):
                        off, n = TILES[t]
                        norm_transpose(t, gbc, AT[:, :, off:off + n], pst, tag_out=("AT", t))
                    AT_all = [("AT", t) for t in range(NT)]
                    w3 = dr["w_in"][l].rearrange("(kc p) c -> p kc c", p=128)
                    ecnt = [0]
                    for cc in range(10):
                        wb = wbf[cc % 2]
                        wname = "wbf%d" % (cc % 2)
                        stv, sti = load_w_bf16(wb, wb[:], w3[:, :, cc * 512:(cc + 1) * 512], wname, (8, 512))
                        if cc in (0, 1):
                            st5 = stv.rearrange("p a (g two h) -> p a g two h", two=2, h=32)
                            wr5 = wrot[:].rearrange("p a (g two h) -> p a g two h", two=2, h=32)
                            for kc in range(8):
                                P.op("pool", ("tensor_scalar", dict(
                                    out=wr5[:, kc, :, 0, :], in0=st5[:, kc, :, 1, :], scalar1=-1.0, scalar2=None,
                                    op0=ALU.mult)), reads=["wstage%d" % sti], writes=["wrot"])
                                P.op("pool", ("tensor_copy", dict(
                                    out=wr5[:, kc, :, 1, :], in_=st5[:, kc, :, 0, :])),
                                    reads=["wstage%d" % sti], writes=["wrot"])
                        if cc in (0, 1, 3, 4):
                            dstn = {0: "QTda", 1: "KTda", 3: "QTna", 4: "KTna"}[cc]
                            for hb in range(4):
                                for (q0, nq, _tl) in QCHUNKS:
                                    k = ecnt[0]
                                    ecnt[0] += 1
                                    pa_ = pz[(2 * k) % 4]
                                    pb_ = pz[(2 * k + 1) % 4]
                                    pan = "pzA%d" % ((2 * k) % 4)
                                    pbn = "pzA%d" % ((2 * k + 1) % 4)
                                    evb = ev_b[k % 2]
                                    evn = "ev_b%d" % (k % 2)
                                    for kc in range(8):
                                        P.op("pe", ("matmul", dict(
                                            out=pa_[:, 0:nq], lhsT=wb[:, kc, hb * 128:(hb + 1) * 128],
                                            rhs=AT[:, kc, q0:q0 + nq], start=(kc == 0), stop=(kc == 7))),
                                            reads=[wname] + AT_all, writes=[pan])
                                    if cc in (0, 1):
                                        for kc in range(8):
                                            P.op("pe", ("matmul", dict(
                                                out=pb_[:, 0:nq], lhsT=wrot[:, kc, hb * 128:(hb + 1) * 128],
                                                rhs=AT[:, kc, q0:q0 + nq], start=(kc == 0), stop=(kc == 7))),
                                                reads=["wrot"] + AT_all, writes=[pbn])
                                        P.op("dve", ("tensor_tensor", dict(
                                            out=ev_f[:, 0, 0:nq], in0=pa_[:, 0:nq], in1=cosT[:, q0:q0 + nq], op=ALU.mult)),
                                            reads=[pan, "tab1"], writes=["ev_f0"])
                                        P.op("dve", ("tensor_tensor", dict(
                                            out=ev_f[:, 1, 0:nq], in0=pb_[:, 0:nq], in1=sinT[:, q0:q0 + nq], op=ALU.mult)),
                                            reads=[pbn, "tab0"], writes=["ev_f1"])
                                        P.op("dve", ("tensor_tensor", dict(
                                            out=evb[:, 0:nq], in0=ev_f[:, 0, 0:nq], in1=ev_f[:, 1, 0:nq], op=ALU.add)),
                                            reads=["ev_f0", "ev_f1"], writes=[evn])
                                    else:
                                        P.op("act", ("activation", dict(
                                            out=evb[:, 0:nq], in_=pa_[:, 0:nq], func=AF.Copy)),
                                            reads=[pan], writes=[evn])
                                    P.dma("sp", ("dma_start", dict(
                                        out=scr[dstn][hb, :, q0:q0 + nq], in_=evb[:, 0:nq])),
                                        reads=[evn], writes=[(dstn, hb, q0)])
                        else:
                            for t in range(NT):
                                off, n = TILES[t]
                                k = ecnt[0]
                                ecnt[0] += 1
                                pa_ = pz[k % 4]
                                pan = "pzA%d" % (k % 4)
                                for kc in range(8):
                                    P.op("pe", ("matmul", dict(
                                        out=pa_[0:n, :], lhsT=AT[:, kc, off:off + n], rhs=wb[:, kc, :],
                                        start=(kc == 0), stop=(kc == 7))), reads=[wname, ("AT", t)], writes=[pan])
                                if cc == 2:
                                    vb = vda_b[k % 2]
                                    vn = "vda_b%d" % (k % 2)
                                    P.op("act", ("activation", dict(
                                        out=vb[0:n, :, 0:128], in_=pa_[0:n, :].rearrange("p (h e) -> p h e", h=4),
                                        func=AF.Copy)), reads=[pan], writes=[vn])
                                    P.dma("sp", ("dma_start", dict(
                                        out=scr["Vda"][off:off + n], in_=vb[0:n])), reads=[vn], writes=[("Vda", t)])
                                elif cc == 5:
                                    vb = vna_b[k % 2]
                                    vn = "vna_b%d" % (k % 2)
                                    P.op("act", ("activation", dict(
                                        out=vb[0:n, :, 0:64], in_=pa_[0:n, :].rearrange("p (h e) -> p h e", h=8),
                                        func=AF.Copy)), reads=[pan], writes=[vn])
                                    P.dma("sp", ("dma_start", dict(
                                        out=scr["Vna"][off:off + n], in_=vb[0:n])), reads=[vn], writes=[("Vna", t)])
                                else:
                                    evb = ev_b[k % 2]
                                    evn = "ev_b%d" % (k % 2)
                                    P.op("act", ("activation", dict(
                                        out=evb[0:n, 0:512], in_=pa_[0:n, :], func=AF.Sigmoid)),
                                        reads=[pan], writes=[evn])
                                    gc0 = (cc - 6) * 512
                                    P.dma("sp", ("dma_start", dict(
                                        out=scr["G"][off:off + n, gc0:gc0 + 512], in_=evb[0:n, 0:512])),
                                        reads=[evn], writes=[("G", t, cc)])
                    P.barrier()
                if stop_after == "A":
                    break
                with ExitStack() as pb:
                    KT = sb(pb, "KT", [128, 4, L], BF16)
                    VA = sb(pb, "VA", [128, NT, 4, 130], BF16)
                    QT = [sb(pb, "QT%d" % i, [128, 4, 512], BF16) for i in range(2)]
                    PT = [sb(pb, "PT%d" % i, [128, 512], BF16) for i in range(3)]
                    OD = sb(pb, "OD", [128, 4, 512], BF16)
                    ODTs = sb(pb, "ODTs", [128, 4, 512], BF16)
                    of = sb(pb, "of", [128, 2, 128], F32)
                    psS = [ps(pb, "psS%d" % i, [128, 512], F32) for i in range(2)]
                    acc = [[ps(pb, "acc%d_%d" % (m, i), [128, 2, 256], F32) for i in range(2)] for m in range(2)]
                    pstB = ps(pb, "pstB", [128, 1024], BF16)[:, 0:512].rearrange("p (h t) -> p h t", h=4)
                    P.dma("sp", ("dma_start", dict(out=KT[:], in_=scr["KTda"].rearrange("h p t -> p h t"))),
                          reads=[("KTda", hb, q0) for hb in range(4) for (q0, _a, _b) in QCHUNKS], writes=["KT"])
                    for t in range(NT):
                        off, n = TILES[t]
                        P.dma("sp", ("dma_start", dict(out=VA[0:n, t], in_=scr["Vda"][off:off + n])),
                              reads=[("Vda", t)], writes=["VA"])
                    sc_cnt = [0]
                    for ci, (q0, nq, tl) in enumerate(QCHUNKS):
                        qt = QT[ci % 2]
                        qtn = "QT%d" % (ci % 2)
                        P.dma("sp", ("dma_start", dict(
                            out=qt[:, :, 0:nq], in_=scr["QTda"][:, :, q0:q0 + nq].rearrange("h p t -> p h t"))),
                            reads=[("QTda", hb, q0) for hb in range(4)], writes=[qtn])
                        nsub = len(tl)
                        for h in range(4):
                            items = [(m, kt) for m in range(2) for kt in range(NT)]
                            bufs = []

                            def emit_score(m, kt):
                                nonlocal_sc = sc_cnt[0]
                                sc_cnt[0] += 1
                                bp = 64 * m
                                koff, kn = TILES[kt]
                                pS = psS[nonlocal_sc % 2]
                                pSn = "psS%d" % (nonlocal_sc % 2)
                                pt = PT[nonlocal_sc % 3]
                                ptn = "PT%d" % (nonlocal_sc % 3)
                                P.op("pe", ("matmul", dict(
                                    out=pS[0:kn, 0:nq], lhsT=KT[bp:bp + 64, h, koff:koff + kn],
                                    rhs=qt[bp:bp + 64, h, 0:nq], start=True, stop=True)),
                                    reads=["KT", qtn], writes=[pSn])
                                P.op("act", ("activation", dict(
                                    out=pt[0:kn, 0:nq], in_=pS[0:kn, 0:nq], func=AF.Exp, scale=0.125)),
                                    reads=[pSn], writes=[ptn])
                                return pt, ptn

                            def emit_pv(m, kt, pt, ptn):
                                koff, kn = TILES[kt]
                                for j in range(nsub):
                                    nqs = TILES[tl[j]][1]
                                    a_ = acc[m][j // 2]
                                    an = "acc%d_%d" % (m, j // 2)
                                    P.op("pe", ("matmul", dict(
                                        out=a_[0:nqs, j % 2, 0:129], lhsT=pt[0:kn, j * 128:j * 128 + nqs],
                                        rhs=VA[0:kn, kt, h, 0:129], start=(kt == 0 and j % 2 == 0), stop=(kt == NT - 1))),
                                        reads=[ptn, "VA"], writes=[an])

                            cur = emit_score(*items[0])
                            for ii, (m, kt) in enumerate(items):
                                nxt = emit_score(*items[ii + 1]) if ii + 1 < len(items) else None
                                emit_pv(m, kt, cur[0], cur[1])
                                cur = nxt
                            for j in range(nsub):
                                nqs = TILES[tl[j]][1]
                                a0 = acc[0][j // 2]
                                a1 = acc[1][j // 2]
                                a0n = "acc0_%d" % (j // 2)
                                a1n = "acc1_%d" % (j // 2)
                                P.op("dve", ("reciprocal", dict(
                                    out=small[0:nqs, 16:17], in_=a0[0:nqs, j % 2, 128:129])), reads=[a0n], writes=["small"])
                                P.op("dve", ("reciprocal", dict(
                                    out=small[0:nqs, 17:18], in_=a1[0:nqs, j % 2, 128:129])), reads=[a1n, "small"], writes=["small"])
                                P.op("dve", ("tensor_scalar", dict(
                                    out=small[0:nqs, 17:18], in0=small[0:nqs, 17:18], scalar1=lam[0:nqs, 4 + l:5 + l],
                                    scalar2=None, op0=ALU.mult)), reads=["small", "lam"], writes=["small"])
                                P.op("dve", ("tensor_scalar", dict(
                                    out=of[0:nqs, 0, :], in0=a1[0:nqs, j % 2, 0:128], scalar1=small[0:nqs, 17:18],
                                    scalar2=None, op0=ALU.mult)), reads=[a1n, "small"], writes=["of0"])
                                P.op("dve", ("scalar_tensor_tensor", dict(
                                    out=of[0:nqs, 1, :], in0=a0[0:nqs, j % 2, 0:128], scalar=small[0:nqs, 16:17],
                                    in1=of[0:nqs, 0, :], op0=ALU.mult, op1=ALU.add)),
                                    reads=[a0n, "small", "of0"], writes=["of1"])
                                rms_rstd(of[0:nqs, 1, :], nqs, 18, "of1")
                                P.op("dve", ("scalar_tensor_tensor", dict(
                                    out=OD[0:nqs, j, h * 128:(h + 1) * 128], in0=of[0:nqs, 1, :],
                                    scalar=small[0:nqs, 18:19], in1=gain_bc[0:nqs, l, :], op0=ALU.mult, op1=ALU.mult)),
                                    reads=["of1", "small", "gain_bc"], writes=["OD"])
                        if stop_after == "B":
                            for j in range(nsub):
                                tj = tl[j]
                                if tj == 0:
                                    continue
                                P.op("act", ("activation", dict(out=sq_junk[:, 0:512], in_=OD[:, j, :], func=AF.Copy)),
                                     reads=["OD"], writes=["sq_junk"])
                                P.dma("sp", ("dma_start", dict(out=y_d[s, 128 * (tj - 1):128 * tj, 0:512], in_=sq_junk[:, 0:512])),
                                      reads=["sq_junk"], writes=[("y", tj)])
                        for j in range(nsub):
                            nqs = TILES[tl[j]][1]
                            for h in range(4):
                                P.op("pe", ("transpose", dict(
                                    out=pstB[:, h, 0:nqs], in_=OD[0:nqs, j, h * 128:(h + 1) * 128],
                                    identity=ident_b[0:nqs, 0:nqs])), reads=["OD", "ident_b"], writes=["pstB"])
                            P.op("act", ("activation", dict(
                                out=ODTs[:, :, j * 128:j * 128 + nqs], in_=pstB[:, :, 0:nqs], func=AF.Copy)),
                                reads=["pstB"], writes=["ODTs"])
                        P.dma("sp", ("dma_start", dict(
                            out=scr["ODT"][:, :, q0:q0 + nq].rearrange("h p t -> p h t"), in_=ODTs[:, :, 0:nq])),
                            reads=["ODTs"], writes=[("ODT", ci)])
                    P.barrier()
                if stop_after == "B":
                    break
                with ExitStack() as pc:
                    KT = sb(pc, "KTn", [128, 4, L], BF16)
                    QT = sb(pc, "QTn", [128, 4, L], BF16)
                    VN = sb(pc, "VN", [64, 33, 8, 66], BF16)
                    Et = sb(pc, "Et", [64, 8, 16, 64], BF16)
                    Pn = [sb(pc, "Pn%d" % i, [64, 4, 8, 64], BF16) for i in range(2)]
                    Pm = [sb(pc, "Pm%d" % i, [16, 4, 64], BF16) for i in range(2)]
                    ONr = [sb(pc, "ONr%d" % i, [64, 512], BF16) for i in range(2)]
                    ONT2 = [sb(pc, "ONT2_%d" % i, [128, 4, 128], BF16) for i in range(2)]
                    psN = [ps(pc, "psN%d" % i, [128, 512], F32) for i in range(4)]
                    psM = ps(pc, "psM", [128, 512], F32)[:, 0:256].rearrange("p (i q) -> p i q", i=4)
                    psO = ps(pc, "psO", [128, 512], F32)[:, 0:512].rearrange("p (i e) -> p i e", i=4)
                    pstC = ps(pc, "pstC", [128, 1024], BF16)[:, 0:256].rearrange("p (c q) -> p c q", c=4)
                    P.dma("sp", ("dma_start", dict(out=KT[:], in_=scr["KTna"].rearrange("h p t -> p h t"))),
                          reads=[("KTna", hb, q0) for hb in range(4) for (q0, _a, _b) in QCHUNKS], writes=["KTn"])
                    P.dma("sp", ("dma_start", dict(out=QT[:], in_=scr["QTna"].rearrange("h p t -> p h t"))),
                          reads=[("QTna", hb, q0) for hb in range(4) for (q0, _a, _b) in QCHUNKS], writes=["QTn"])
                    P.dma("sp", ("dma_start", dict(out=VN[0:16, 0], in_=scr["Vna"][0:16])),
                          reads=[("Vna", t) for t in range(NT)], writes=["VN"])
                    P.dma("sp", ("dma_start", dict(
                        out=VN[:, 1:33], in_=scr["Vna"][16:L].rearrange("(r k) h e -> k r h e", k=64))),
                        reads=[("Vna", t) for t in range(NT)], writes=["VN"])
                    P.dma("sp", ("dma_start", dict(out=Et[:, :, 0:15, :].rearrange("k h r q -> k h (r q)"),
                                                   in_=scr["E"][l].rearrange("k (h x) -> k h x", h=8))),
                          reads=["E_d"], writes=["Et"])
                    P.op("dve", ("memset", dict(ap=Et[:, :, 15, :], constant=0.0)), writes=["Et15"])

                    def na_block(qoff, nqr, rs, rho0, nkr, out_tile, out_name, gi):
                        for hg in range(2):
                            pn = Pn[gi[0] % 2]
                            pnn = "Pn%d" % (gi[0] % 2)
                            pm = Pm[gi[0] % 2]
                            pmn = "Pm%d" % (gi[0] % 2)
                            gi[0] += 1
                            for i in range(4):
                                h = 4 * hg + i
                                pr, bp = h // 2, 64 * (h % 2)
                                for jr in range(nkr):
                                    k0 = 16 + 64 * (rs + jr)
                                    P.op("pe", ("matmul", dict(
                                        out=psN[i][0:64, jr * 64:jr * 64 + nqr], lhsT=KT[bp:bp + 64, pr, k0:k0 + 64],
                                        rhs=QT[bp:bp + 64, pr, qoff:qoff + nqr], start=True, stop=True)),
                                        reads=["KTn", "QTn"], writes=["psN%d" % i])
                                P.op("pe", ("matmul", dict(
                                    out=psM[0:16, i, 0:nqr], lhsT=KT[bp:bp + 64, pr, 0:16],
                                    rhs=QT[bp:bp + 64, pr, qoff:qoff + nqr], start=True, stop=True)),
                                    reads=["KTn", "QTn"], writes=["psM"])
                            if nkr > 0:
                                for i in range(4):
                                    P.op("act", ("activation", dict(
                                        out=pn[:, i, 0:nkr, 0:nqr],
                                        in_=psN[i][0:64, 0:nkr * 64].rearrange("p (r q) -> p r q", q=64)[:, :, 0:nqr],
                                        func=AF.Exp, scale=0.125)), reads=["psN%d" % i], writes=[pnn])
                                P.op("dve", ("tensor_tensor", dict(
                                    out=pn[:, :, 0:nkr, 0:nqr], in0=pn[:, :, 0:nkr, 0:nqr],
                                    in1=Et[:, 4 * hg:4 * hg + 4, rho0:rho0 + nkr, 0:nqr], op=ALU.mult)),
                                    reads=[pnn, "Et", "Et15"], writes=[pnn])
                            import os
                            NSTEP = int(os.environ.get("NA_STEP", "9"))
                            if NSTEP < 2:
                                continue
                            P.op("act", ("activation", dict(
                                out=pm[0:16, :, 0:nqr], in_=psM[0:16, :, 0:nqr], func=AF.Exp, scale=0.125)),
                                reads=["psM"], writes=[pmn])
                            if NSTEP < 3:
                                continue
                            for i in range(4):
                                h = 4 * hg + i
                                for jr in range(nkr):
                                    P.op("pe", ("matmul", dict(
                                        out=psO[0:nqr, i, 0:65], lhsT=pn[:, i, jr, 0:nqr], rhs=VN[:, 1 + rs + jr, h, 0:65],
                                        start=(jr == 0), stop=False)), reads=[pnn, "VN"], writes=["psO"])
                                P.op("pe", ("matmul", dict(
                                    out=psO[0:nqr, i, 0:65], lhsT=pm[0:16, i, 0:nqr], rhs=VN[0:16, 0, h, 0:65],
                                    start=(nkr == 0), stop=True)), reads=[pmn, "VN"], writes=["psO"])
                            if NSTEP < 4:
                                continue
                            P.op("dve", ("reciprocal", dict(out=small[0:nqr, 24:28], in_=psO[0:nqr, :, 64])),
                                 reads=["psO"], writes=["small"])
                            if NSTEP < 5:
                                continue
                            P.op("dve", ("tensor_tensor", dict(
                                out=out_tile[0:nqr, hg * 256:(hg + 1) * 256].rearrange("p (h e) -> p h e", h=4),
                                in0=psO[0:nqr, :, 0:64],
                                in1=small[0:nqr, 24:28].unsqueeze(2).to_broadcast([nqr, 4, 64]), op=ALU.mult)),
                                reads=["psO", "small"], writes=[out_name])

                    gi = [0]
                    import os
                    NR = int(os.environ.get("NA_ROWS", "32"))
                    for r in range(NR):
                        rs = min(max(r - 4, 0), 24)
                        rho0 = rs - r + 7
                        onr = ONr[r % 2]
                        onrn = "ONr%d" % (r % 2)
                        na_block(16 + 64 * r, 64, rs, rho0, 8, onr, onrn, gi)
                        if stop_after == "C":
                            P.op("act", ("activation", dict(out=sq_junk[0:64, 0:512], in_=onr[0:64, :], func=AF.Copy)),
                                 reads=[onrn], writes=["sq_junk"])
                            P.dma("sp", ("dma_start", dict(out=y_d[s, 64 * r:64 * r + 64, 0:512], in_=sq_junk[0:64, 0:512])),
                                  reads=["sq_junk"], writes=[("y", r)])
                        o2 = ONT2[(r // 2) % 2]
                        o2n = "ONT2_%d" % ((r // 2) % 2)
                        for c4 in range(4):
                            P.op("pe", ("transpose", dict(
                                out=pstC[:, c4, 0:64], in_=onr[0:64, c4 * 128:(c4 + 1) * 128], identity=ident_b[0:64, 0:64])),
                                reads=[onrn, "ident_b"], writes=["pstC"])
                        P.op("act", ("activation", dict(
                            out=o2[:, :, (r % 2) * 64:(r % 2) * 64 + 64], in_=pstC[:, :, 0:64], func=AF.Copy)),
                            reads=["pstC"], writes=[o2n])
                        if r % 2 == 1:
                            t0 = 16 + 64 * (r - 1)
                            P.dma("sp", ("dma_start", dict(
                                out=scr["ONT"][:, :, t0:t0 + 128].rearrange("h p t -> p h t"), in_=o2[:])),
                                reads=[o2n], writes=[("ONT", 1 + r // 2)])
                    import os
                    NDBG = int(os.environ.get("NA_DBG", "9"))
                    NDBG = int(os.environ.get("NA_DBG", "9"))
                    if os.environ.get("NA_DUP"):
                        na_block(16 + 64 * 31, 64, 24, 0, 8, ONr[1], "ONr1", gi)
                    if NDBG >= 2:
                        na_block(0, 64, 0, 15, 1, ONr[0], "ONr0", gi)
                    for c4 in range(4 if NDBG >= 3 else 0):
                        P.op("pe", ("transpose", dict(
                            out=pstC[:, c4, 0:16], in_=ONr[0][0:16, c4 * 128:(c4 + 1) * 128], identity=ident_b[0:16, 0:16])),
                            reads=["ONr0", "ident_b"], writes=["pstC"])
                    if NDBG >= 3:
                        P.op("act", ("activation", dict(out=ONT2[0][:, :, 0:16], in_=pstC[:, :, 0:16], func=AF.Copy)),
                             reads=["pstC"], writes=["ONT2_0"])
                    else:
                        P.op("dve", ("memset", dict(ap=ONT2[0][:, :, 0:16], constant=0.0)), writes=["ONT2_0"])
                    P.dma("sp", ("dma_start", dict(out=scr["ONT"][:, :, 0:16].rearrange("h p t -> p h t"),
                                                     in_=ONT2[0][:, :, 0:16])), reads=["ONT2_0"], writes=[("ONT", 0)])
                    P.barrier()
                if stop_after == "C":
                    break
                with ExitStack() as pd:
                    alloc_wstage(pd)
                    Wa = sb(pd, "Wa", [128, 4, D], BF16)
                    Wb = sb(pd, "Wb", [128, 4, D], BF16)
                    Wo = sb(pd, "Wo", [128, 8, D], BF16)
                    odt = [sb(pd, "odt%d" % i, [128, 4, 128], BF16) for i in range(2)]
                    ont = [sb(pd, "ont%d" % i, [128, 4, 128], BF16) for i in range(2)]
                    gt = [sb(pd, "gt%d" % i, [128, 2048], BF16) for i in range(2)]
                    mf = sb(pd, "mf", [128, 2, D], F32)
                    mb = sb(pd, "mb", [128, D], BF16)
                    mT = sb(pd, "mT", [128, 8, 128], BF16)
                    pya = ps(pd, "pya", [128, 2, 512], F32)
                    pyb = ps(pd, "pyb", [128, 2, 512], F32)
                    pyo = ps(pd, "pyo", [128, 2, 512], F32)
                    pstD = ps(pd, "pstD", [128, 8, 128], BF16)
                    for half in range(2):
                        load_w_bf16(Wa, Wa[:, :, half * 512:(half + 1) * 512],
                                    dr["w_branch_a"][l].rearrange("(kc p) c -> p kc c", p=128)[:, :, half * 512:(half + 1) * 512],
                                    "Wa", (4, 512))
                        load_w_bf16(Wb, Wb[:, :, half * 512:(half + 1) * 512],
                                    dr["w_branch_b"][l].rearrange("(kc p) c -> p kc c", p=128)[:, :, half * 512:(half + 1) * 512],
                                    "Wb", (4, 512))
                        load_w_bf16(Wo, Wo[:, :, half * 512:(half + 1) * 512],
                                    dr["w_out"][l].rearrange("(kc p) c -> p kc c", p=128)[:, :, half * 512:(half + 1) * 512],
                                    "Wo", (8, 512))
                    for t in range(NT):
                        off, n = TILES[t]
                        od, on_, g_ = odt[t % 2], ont[t % 2], gt[t % 2]
                        odn, onn, gn = "odt%d" % (t % 2), "ont%d" % (t % 2), "gt%d" % (t % 2)
                        P.dma("sp", ("dma_start", dict(
                            out=od[:, :, 0:n], in_=scr["ODT"][:, :, off:off + n].rearrange("h p t -> p h t"))),
                            reads=[("ODT", ci) for ci in range(5)], writes=[odn])
                        P.dma("sp", ("dma_start", dict(
                            out=on_[:, :, 0:n], in_=scr["ONT"][:, :, off:off + n].rearrange("h p t -> p h t"))),
                            reads=[("ONT", i) for i in range(17)], writes=[onn])
                        P.dma("sp", ("dma_start", dict(out=g_[0:n, :], in_=scr["G"][off:off + n, :])),
                              reads=[("G", t, cc) for cc in range(6, 10)], writes=[gn])
                        for half in range(2):
                            for c4 in range(4):
                                P.op("pe", ("matmul", dict(
                                    out=pya[0:n, half, :], lhsT=od[:, c4, 0:n], rhs=Wa[:, c4, half * 512:(half + 1) * 512],
                                    start=(c4 == 0), stop=(c4 == 3))), reads=[odn, "Wa"], writes=["pya"])
                            for c4 in range(4):
                                P.op("pe", ("matmul", dict(
                                    out=pyb[0:n, half, :], lhsT=on_[:, c4, 0:n], rhs=Wb[:, c4, half * 512:(half + 1) * 512],
                                    start=(c4 == 0), stop=(c4 == 3))), reads=[onn, "Wb"], writes=["pyb"])
                        P.op("dve", ("tensor_tensor", dict(
                            out=mf[0:n, 0, :], in0=pya[0:n].rearrange("p a b -> p (a b)"), in1=g_[0:n, 0:1024], op=ALU.mult)),
                            reads=["pya", gn], writes=["mf0"])
                        P.op("dve", ("tensor_tensor", dict(
                            out=mf[0:n, 1, :], in0=pyb[0:n].rearrange("p a b -> p (a b)"), in1=g_[0:n, 1024:2048], op=ALU.mult)),
                            reads=["pyb", gn], writes=["mf1"])
                        P.op("dve", ("tensor_tensor", dict(
                            out=mb[0:n, :], in0=mf[0:n, 0, :], in1=mf[0:n, 1, :], op=ALU.add)),
                            reads=["mf0", "mf1"], writes=["mb"])
                        for c in range(8):
                            P.op("pe", ("transpose", dict(
                                out=pstD[:, c, 0:n], in_=mb[0:n, c * 128:(c + 1) * 128], identity=ident_b[0:n, 0:n])),
                                reads=["mb", "ident_b"], writes=["pstD"])
                        P.op("act", ("activation", dict(out=mT[:, :, 0:n], in_=pstD[:, :, 0:n], func=AF.Copy)),
                             reads=["pstD"], writes=["mT"])
                        for half in range(2):
                            for c in range(8):
                                P.op("pe", ("matmul", dict(
                                    out=pyo[0:n, half, :], lhsT=mT[:, c, 0:n], rhs=Wo[:, c, half * 512:(half + 1) * 512],
                                    start=(c == 0), stop=(c == 7))), reads=["mT", "Wo"], writes=["pyo"])
                        P.op("dve", ("tensor_tensor", dict(
                            out=H[0:n, t, :], in0=H[0:n, t, :], in1=pyo[0:n].rearrange("p a b -> p (a b)"), op=ALU.add)),
                            reads=["pyo", ("H", t)], writes=[("H", t)])
                    P.barrier()
                if stop_after == "D":
                    break
                with ExitStack() as pe_:
                    Wq = sb(pe_, "Wq", [128, 8, 2048], BF16)
                    keysT = sb(pe_, "keysT", [128, 16, 128], BF16)
                    gbc = sb(pe_, "gbcE", [128, D], F32)
                    P.dma("sp", ("dma_start", dict(out=gbc[:], in_=dr["norm_ffn"][l:l + 1, :].partition_broadcast(128))),
                          writes=["gbc"])
                    with ExitStack() as pw:
                        alloc_wstage(pw)
                        keysf = sb(pw, "keysf", [128, 16, 128], F32)
                        pkT = ps(pw, "pkT", [128, 4, 128], F32)
                        wq3 = dr["peer_wq"][l].rearrange("(kc p) c -> p kc c", p=128)
                        for c4 in range(4):
                            load_w_bf16(Wq, Wq[:, :, c4 * 512:(c4 + 1) * 512], wq3[:, :, c4 * 512:(c4 + 1) * 512], "Wq", (8, 512))
                        P.dma("sp", ("dma_start", dict(
                            out=keysf[:], in_=dr["peer_keys"][l].rearrange("h p k d -> k (h p) d"))), writes=["keysf"])
                        for g4 in range(4):
                            for i in range(4):
                                gq = 4 * g4 + i
                                P.op("pe", ("transpose", dict(
                                    out=pkT[:, i, :], in_=keysf[:, gq, :], identity=ident_f[:, :])),
                                    reads=["keysf", "ident_f"], writes=["pkT"])
                            P.op("act", ("activation", dict(out=keysT[:, 4 * g4:4 * g4 + 4, :], in_=pkT[:], func=AF.Copy)),
                                 reads=["pkT"], writes=["keysT"])
                        P.barrier()
                    CT = sb(pe_, "CT", [128, 8, 128], BF16)
                    qT = sb(pe_, "qT", [128, 16, 128], BF16)
                    ssb = sb(pe_, "ssb", [128, 16, 128], F32)
                    swk = sb(pe_, "swk", [128, 256], F32)
                    vals = sb(pe_, "vals", [128, 16, 16], F32)
                    idxu = sb(pe_, "idxu", [128, 16, 16], U32)
                    idxf = sb(pe_, "idxf", [128, 16, 16], F32)
                    cs_ = sb(pe_, "cs", [128, 8, 256], F32)
                    scv = sb(pe_, "scv", [128, 8, 16], F32)
                    posu = sb(pe_, "posu", [128, 8, 16], U32)
                    posf = sb(pe_, "posf", [128, 8, 16], F32)
                    ai_ = sb(pe_, "ai", [128, 8, 16], I32)
                    af_ = sb(pe_, "af", [128, 8, 16], F32)
                    bf_ = sb(pe_, "bf", [128, 8, 16], F32)
                    msk = ssb[:].rearrange("p g k -> p (g k)").rearrange("p (h a b) -> p h a b", h=8, a=16)
                    i1s = sb(pe_, "i1s", [128, 8, 16], F32)
                    i2s = sb(pe_, "i2s", [128, 8, 16], F32)
                    ef = sb(pe_, "ef", [128, 128], F32)
                    ei = [sb(pe_, "ei%d" % i, [128, 128], I32) for i in range(2)]
                    gg = sb(pe_, "gg", [128, 8, 16], F32)
                    gg2 = sb(pe_, "gg2", [128, 8, 16], F32)
                    abuf2 = sb(pe_, "abuf2", [128, D], BF16)
                    ei2 = sb(pe_, "ei2", [128, 128], I32)
                    hid = sb(pe_, "hid", [128, 128], F32)
                    gl = sb(pe_, "gl", [128, 128], F32)
                    wv = sb(pe_, "wv", [128, 128], F32)
                    djunk = sq_junk[:].bitcast(BF16)[:, 0:D]
                    NG = 10
                    gat = [sb(pe_, "gat%d" % i, [128, 2 * D], BF16) for i in range(NG)]
                    dg = [sb(pe_, "dg%d" % i, [128, 128], BF16) for i in range(4)]
                    pacc = ps(pe_, "pacc", [128, 2, 512], F32)
                    pstE = ps(pe_, "pstE", [128, 8, 128], BF16)
                    pq = [ps(pe_, "pq%d" % i, [128, 4, 128], F32) for i in range(4)]
                    for i in range(2):
                        P.op("pool", ("memset", dict(ap=ei[i][:], constant=0)), writes=["ei%d" % i])
                    gslot = [0]
                    abufs = [abuf, abuf2]
                    ggs = [gg, gg2]
                    eis = [ei[0], ei2]

                    def prologue(t):
                        off, n = TILES[t]
                        pb = t % 2
                        ab, abn = abufs[pb], "abuf%d" % pb
                        g_, gn = ggs[pb], "gg%d" % pb
                        if n == 128:
                            eit, ein = eis[pb], "eiN%d" % pb
                        else:
                            eit, ein = ei[1], "ei1"
                        st = []

                        def s_norm():
                            rms_rstd(H[0:n, t, :], n, 8, ("H", t))
                            P.op("dve", ("scalar_tensor_tensor", dict(
                                out=ab[0:n, :], in0=H[0:n, t, :], scalar=small[0:n, 8:9], in1=gbc[0:n, :],
                                op0=ALU.mult, op1=ALU.mult)), reads=[("H", t), "small", "gbc"], writes=[abn])
                            for c in range(8):
                                P.op("pe", ("transpose", dict(out=pstE[:, c, 0:n], in_=ab[0:n, c * 128:(c + 1) * 128],
                                                              identity=ident_b[0:n, 0:n])),
                                     reads=[abn, "ident_b"], writes=["pstE"])
                            P.op("act", ("activation", dict(out=CT[:, :, 0:n], in_=pstE[:, :, 0:n], func=AF.Copy)),
                                 reads=["pstE"], writes=["CT"])
                        st.append(s_norm)

                        def s_q(g4):
                            for gq in range(4 * g4, 4 * g4 + 4):
                                pqq = pq[gq // 4]
                                for kc in range(8):
                                    P.op("pe", ("matmul", dict(
                                        out=pqq[:, gq % 4, 0:n], lhsT=Wq[:, kc, gq * 128:(gq + 1) * 128], rhs=CT[:, kc, 0:n],
                                        start=(kc == 0), stop=(kc == 7))), reads=["Wq", "CT"], writes=["pq%d" % (gq // 4)])
                            P.op("act", ("activation", dict(
                                out=qT[:, 4 * g4:4 * g4 + 4, 0:n], in_=pq[g4][:, :, 0:n], func=AF.Copy)),
                                reads=["pq%d" % g4], writes=[("qT", g4)])
                        for g4 in range(4):
                            st.append(lambda g4=g4: s_q(g4))

                        def s_sc():
                            for gq in range(16):
                                pqq = pq[gq // 4]
                                P.op("pe", ("matmul", dict(
                                    out=pqq[0:n, gq % 4, :], lhsT=qT[:, gq, 0:n], rhs=keysT[:, gq, :], start=True, stop=True)),
                                    reads=[("qT", gq // 4), "keysT"], writes=["pq%d" % (gq // 4)])
                            for g4 in range(4):
                                P.op("act", ("activation", dict(
                                    out=ssb[0:n, 4 * g4:4 * g4 + 4, :], in_=pq[g4][0:n, :, :], func=AF.Copy)),
                                    reads=["pq%d" % g4], writes=[("ssb", g4)])
                        st.append(s_sc)

                        def s_top(g4):
                            for gq in range(4 * g4, 4 * g4 + 4):
                                P.op("dve", ("max", dict(out=vals[0:n, gq, 0:8], in_=ssb[0:n, gq, :])),
                                     reads=[("ssb", g4)], writes=["vals"])
                                P.op("dve", ("max_index", dict(out=idxu[0:n, gq, 0:8], in_max=vals[0:n, gq, 0:8],
                                                               in_values=ssb[0:n, gq, :])),
                                     reads=[("ssb", g4), "vals"], writes=["idxu"])
                                P.op("dve", ("match_replace", dict(
                                    out=swk[0:n, 0:128], in_to_replace=vals[0:n, gq, 0:8], in_values=ssb[0:n, gq, :],
                                    imm_value=-1e30)), reads=[("ssb", g4), "vals"], writes=["swk"])
                                P.op("dve", ("max", dict(out=vals[0:n, gq, 8:16], in_=swk[0:n, 0:128])),
                                     reads=["swk"], writes=["vals"])
                                P.op("dve", ("max_index", dict(out=idxu[0:n, gq, 8:16], in_max=vals[0:n, gq, 8:16],
                                                               in_values=swk[0:n, 0:128])),
                                     reads=["swk", "vals"], writes=["idxu"])
                        for g4 in range(4):
                            st.append(lambda g4=g4: s_top(g4))

                        v4 = vals[:].rearrange("p (h two) k -> p h two k", two=2)
                        i4 = idxf[:].rearrange("p (h two) k -> p h two k", two=2)

                        def s_cs():
                            P.op("dve", ("tensor_copy", dict(out=idxf[0:n], in_=idxu[0:n])), reads=["idxu"], writes=["idxf"])
                            P.op("dve", ("tensor_tensor", dict(
                                out=cs_[0:n].rearrange("p h (a b) -> p h a b", a=16),
                                in0=v4[0:n, :, 0, :].unsqueeze(3).to_broadcast([n, 8, 16, 16]),
                                in1=v4[0:n, :, 1, :].unsqueeze(2).to_broadcast([n, 8, 16, 16]), op=ALU.add)),
                                reads=["vals"], writes=["cs"])
                        st.append(s_cs)

                        def s_top2(h0):
                            for h in range(h0, h0 + 4):
                                P.op("dve", ("max", dict(out=scv[0:n, h, 0:8], in_=cs_[0:n, h, :])),
                                     reads=["cs"], writes=["scv"])
                                P.op("dve", ("max_index", dict(out=posu[0:n, h, 0:8], in_max=scv[0:n, h, 0:8],
                                                               in_values=cs_[0:n, h, :])),
                                     reads=["cs", "scv"], writes=["posu"])
                                P.op("dve", ("match_replace", dict(
                                    out=swk[0:n, :], in_to_replace=scv[0:n, h, 0:8], in_values=cs_[0:n, h, :], imm_value=-1e30)),
                                    reads=["cs", "scv"], writes=["swk"])
                                P.op("dve", ("max", dict(out=scv[0:n, h, 8:16], in_=swk[0:n, :])),
                                     reads=["swk"], writes=["scv"])
                                P.op("dve", ("max_index", dict(out=posu[0:n, h, 8:16], in_max=scv[0:n, h, 8:16],
                                                               in_values=swk[0:n, :])),
                                     reads=["swk", "scv"], writes=["posu"])
                        st.append(lambda: s_top2(0))
                        st.append(lambda: s_top2(4))

                        def s_dec():
                            P.op("dve", ("tensor_copy", dict(out=posf[0:n], in_=posu[0:n])), reads=["posu"], writes=["posf"])
                            P.op("dve", ("tensor_scalar", dict(out=ai_[0:n], in0=posf[0:n], scalar1=-7.5, scalar2=1.0 / 16,
                                                               op0=ALU.add, op1=ALU.mult)), reads=["posf"], writes=["ai"])
                            P.op("dve", ("tensor_copy", dict(out=af_[0:n], in_=ai_[0:n])), reads=["ai"], writes=["af"])
                            P.op("dve", ("scalar_tensor_tensor", dict(
                                out=bf_[0:n], in0=af_[0:n], scalar=-16.0, in1=posf[0:n], op0=ALU.mult, op1=ALU.add)),
                                reads=["af", "posf"], writes=["bf"])
                            for (src_, two, dst_, dn) in ((af_, 0, i1s, "i1s"), (bf_, 1, i2s, "i2s")):
                                P.op("dve", ("tensor_tensor", dict(
                                    out=msk[0:n], in0=src_[0:n].unsqueeze(3).to_broadcast([n, 8, 16, 16]),
                                    in1=iota16[0:n, :].unsqueeze(1).unsqueeze(1).to_broadcast([n, 8, 16, 16]), op=ALU.is_equal)),
                                    reads=["af", "bf", "iota16"], writes=[("ssb", 0), ("ssb", 1), ("ssb", 2), ("ssb", 3)])
                                P.op("dve", ("tensor_tensor", dict(
                                    out=msk[0:n], in0=msk[0:n],
                                    in1=i4[0:n, :, two, :].unsqueeze(2).to_broadcast([n, 8, 16, 16]), op=ALU.mult)),
                                    reads=[("ssb", 0), ("ssb", 1), ("ssb", 2), ("ssb", 3)] + ["idxf"], writes=[("ssb", 0), ("ssb", 1), ("ssb", 2), ("ssb", 3)])
                                P.op("dve", ("tensor_reduce", dict(
                                    out=dst_[0:n], in_=msk[0:n], axis=AX.X, op=ALU.add)), reads=[("ssb", 0), ("ssb", 1), ("ssb", 2), ("ssb", 3)], writes=[dn])
                            P.op("dve", ("scalar_tensor_tensor", dict(
                                out=ef[0:n, :].rearrange("p (h k) -> p h k", h=8), in0=i1s[0:n], scalar=128.0, in1=i2s[0:n],
                                op0=ALU.mult, op1=ALU.add)), reads=["i1s", "i2s"], writes=["ef"])
                            P.op("dve", ("tensor_scalar", dict(out=eit[0:n, :], in0=ef[0:n, :], scalar1=float(l * NEXP),
                                                               scalar2=None, op0=ALU.add)), reads=["ef"], writes=[ein])
                        st.append(s_dec)

                        def s_gate():
                            P.op("dve", ("tensor_tensor", dict(
                                out=g_[0:n], in0=scv[0:n], in1=scv[0:n, :, 0:1].to_broadcast([n, 8, 16]), op=ALU.subtract)),
                                reads=["scv"], writes=[gn])
                            P.op("act", ("activation", dict(out=g_[0:n], in_=g_[0:n], func=AF.Exp)),
                                 reads=[gn], writes=[gn])
                            P.op("dve", ("tensor_reduce", dict(out=small[0:n, 32:40], in_=g_[0:n], axis=AX.X, op=ALU.add)),
                                 reads=[gn], writes=["small"])
                            P.op("dve", ("reciprocal", dict(out=small[0:n, 32:40], in_=small[0:n, 32:40])),
                                 reads=["small"], writes=["small"])
                            P.op("dve", ("tensor_tensor", dict(
                                out=g_[0:n], in0=g_[0:n], in1=small[0:n, 32:40].unsqueeze(2).to_broadcast([n, 8, 16]), op=ALU.mult)),
                                reads=[gn, "small"], writes=[gn])
                        st.append(s_gate)
                        return st

                    def run_slots(t, nxt_stages):
                        off, n = TILES[t]
                        pb = t % 2
                        ab, abn = abufs[pb], "abuf%d" % pb
                        g_, gn = ggs[pb], "gg%d" % pb
                        if n == 128:
                            eit, ein = eis[pb], "eiN%d" % pb
                        else:
                            eit, ein = ei[1], "ei1"
                        gflat = g_[:].rearrange("p h k -> p (h k)")
                        slots = []
                        every = 7

                        def slot_tail(j, sl):
                            dgi = j % 4
                            P.op("act", ("activation", dict(
                                out=wv[0:n, j:j + 1], in_=gl[0:n, j:j + 1], func=AF.Copy, scale=gflat[0:n, j:j + 1])),
                                reads=[("gl", j), gn], writes=[("wv", j)])
                            P.op("act", ("activation", dict(
                                out=dg[dgi][0:n, 0:n], in_=ident_b[0:n, 0:n], func=AF.Copy, scale=wv[0:n, j:j + 1])),
                                reads=["ident_b", ("wv", j)], writes=["dg%d" % dgi])
                            for half in range(2):
                                P.op("pe", ("matmul", dict(
                                    out=pacc[0:n, half, :], lhsT=dg[dgi][0:n, 0:n],
                                    rhs=gat[sl][0:n, D + half * 512:D + (half + 1) * 512],
                                    start=(j == 0), stop=(j == 127))),
                                    reads=["dg%d" % dgi, "gat%d" % sl], writes=["pacc"])
                        for j in range(128):
                            sl = gslot[0] % NG
                            gslot[0] += 1
                            P.dma("pool", ("indirect_dma_start", dict(
                                out=gat[sl][:], out_offset=None, in_=scr["UV"],
                                in_offset=bass.IndirectOffsetOnAxis(ap=eit[:, j:j + 1], axis=0))),
                                reads=[ein] + ([("UV", l, b_) for b_ in range(NEXP // 512)] if (t == 0 and j == 0) else []),
                                writes=["gat%d" % sl], semkey=gsem[sl])
                            P.op("dve", ("scalar_tensor_tensor", dict(
                                out=djunk[0:n, :], in0=gat[sl][0:n, 0:D], scalar=1.0, in1=ab[0:n, :], op0=ALU.mult, op1=ALU.mult,
                                accum_out=hid[0:n, j:j + 1])), reads=["gat%d" % sl, abn], writes=[("hid", j)])
                            P.op("act", ("activation", dict(out=gl[0:n, j:j + 1], in_=hid[0:n, j:j + 1], func=AF.Gelu)),
                                 reads=[("hid", j)], writes=[("gl", j)])
                            if j >= 1:
                                slot_tail(j - 1, slots[j - 1])
                            slots.append(sl)
                            if nxt_stages and j % every == every - 1:
                                nxt_stages.pop(0)()
                        slot_tail(127, slots[127])
                        while nxt_stages:
                            nxt_stages.pop(0)()
                        P.op("dve", ("tensor_tensor", dict(
                            out=H[0:n, t, :], in0=H[0:n, t, :], in1=pacc[0:n].rearrange("p a b -> p (a b)"), op=ALU.add)),
                            reads=["pacc", ("H", t)], writes=[("H", t)])

                    for st_ in prologue(0):
                        st_()
                    for t in range(NT):
                        run_slots(t, prologue(t + 1) if t + 1 < NT else [])
                    P.barrier()
            if stop_after in ("B", "C"):
                continue
            if stop_after is not None:
                for t in range(1, NT):
                    P.dma("sp", ("dma_start", dict(out=y_d[s, 128 * (t - 1):128 * t, :], in_=H[:, t, :])),
                          reads=[("H", t)], writes=[("y", s, t)])
                continue
            for t in range(1, NT):
                rms_rstd(H[:, t, :], 128, 8, ("H", t))
                P.op("dve", ("scalar_tensor_tensor", dict(
                    out=sq_junk[:, :], in0=H[:, t, :], scalar=small[:, 8:9], in1=gfin_bc[:, :],
                    op0=ALU.mult, op1=ALU.mult)), reads=[("H", t), "small", "gfin_bc"], writes=["sq_junk"])
                P.dma("sp", ("dma_start", dict(out=y_d[s, 128 * (t - 1):128 * t, :], in_=sq_junk[:, :])),
                      reads=["sq_junk"], writes=[("y", s, t)])
            P.barrier()
        P.barrier()
        P.emit()
    return nc, P


def make_in_maps(inputs, nseq=SEQ_PER_CORE, ncores=NCORES):
    consts = host_consts()
    xp = np.asarray(inputs["x_prompt"], dtype=np.float32)
    xs = np.asarray(inputs["x_sample"], dtype=np.float32)
    allx = np.concatenate([xp, xs], axis=0)
    maps = []
    for c in range(ncores):
        m = {"x": np.ascontiguousarray(allx[c * nseq:(c + 1) * nseq])}
        for k, shp in WEIGHT_SHAPES.items():
            m[k] = np.ascontiguousarray(np.asarray(inputs[k], dtype=np.float32).reshape(shp))
        m.update(consts)
        maps.append(m)
    return maps


def kernel(**inputs):
    nc, _ = build()
    maps = make_in_maps(inputs)
    res = run_bass_kernel_spmd(nc, maps, core_ids=list(range(NCORES)))
    ys = np.concatenate([r["y"] for r in res.results], axis=0)
    nb = np.asarray(inputs["x_prompt"]).shape[0]
    return (np.ascontiguousarray(ys[:nb]), np.ascontiguousarray(ys[nb:]))
```

```python
import math
import numpy as np
from contextlib import ExitStack
import concourse.bass as bass
import concourse.mybir as mybir
from concourse.bass_utils import run_bass_kernel_spmd

F32 = mybir.dt.float32
BF16 = mybir.dt.bfloat16
I32 = mybir.dt.int32
U32 = mybir.dt.uint32
ALU = mybir.AluOpType
AF = mybir.ActivationFunctionType
AX = mybir.AxisListType

D = 1024
NMETA = 16
SEQ = 2048
L = SEQ + NMETA
DEPTH = 4
NCORES = 8
SEQ_PER_CORE = 5
EPS = 1e-6
TILES = [(0, 16)] + [(16 + 128 * i, 128) for i in range(16)]
NT = len(TILES)
QCHUNKS = [(16 + 512 * c, 512, [1 + 4 * c + j for j in range(4)]) for c in range(4)] + [(0, 16, [0])]
NEXP = 16384
ENGS = ("pe", "act", "dve", "pool", "sp")


class Prog:
    def __init__(self, nc, es, n_dma_sems=24):
        self.nc = nc
        self.es = es
        self.q = {e: [] for e in ENGS}
        self.cnt = {e: 0 for e in ENGS}
        self.waited = {e: {} for e in ENGS}
        self.res_w = {}
        self.res_r = {}
        self.sem = {}
        for e in ENGS:
            self.sem[e] = es.enter_context(nc.semaphore("s_" + e))
        self.dma_sems = []
        self.dma_val = {}
        for i in range(n_dma_sems):
            self.dma_sems.append(self.new_dma_sem("dma%d" % i))
        self.dma_rr = 0
        self.ninstr = 0

    def new_dma_sem(self, name):
        self.sem[name] = self.es.enter_context(self.nc.semaphore("s_" + name))
        self.dma_val[name] = 0
        return name

    def _deps(self, eng, reads, writes):
        ev = {}
        for r in reads:
            e = self.res_w.get(r)
            if e is not None and ev.get(e[0], 0) < e[1]:
                ev[e[0]] = e[1]
        for w in writes:
            e = self.res_w.get(w)
            if e is not None and ev.get(e[0], 0) < e[1]:
                ev[e[0]] = e[1]
            rr = self.res_r.get(w)
            if rr:
                for k, v in rr.items():
                    if ev.get(k, 0) < v:
                        ev[k] = v
        waits = []
        wd = self.waited[eng]
        for k, v in ev.items():
            if k == "pe" and eng == "pe":
                continue
            if wd.get(k, 0) < v:
                wd[k] = v
                waits.append((k, v))
        return waits

    def _commit(self, event, reads, writes):
        for w in writes:
            self.res_w[w] = event
            self.res_r[w] = {}
        k, v = event
        for r in reads:
            d = self.res_r.setdefault(r, {})
            if d.get(k, 0) < v:
                d[k] = v

    def op(self, eng, fn, reads=(), writes=()):
        waits = self._deps(eng, reads, writes)
        self.cnt[eng] += 1
        event = (eng, self.cnt[eng])
        self.q[eng].append((waits, fn, (eng, 1)))
        self._commit(event, reads, writes)
        self.ninstr += 1

    def dma(self, eng, fn, reads=(), writes=(), semkey=None):
        if semkey is None:
            semkey = self.dma_sems[self.dma_rr % len(self.dma_sems)]
            self.dma_rr += 1
        waits = self._deps(eng, reads, writes)
        pv = self.dma_val[semkey]
        wd = self.waited[eng]
        if pv > 0 and wd.get(semkey, 0) < pv:
            wd[semkey] = pv
            waits.append((semkey, pv))
        self.dma_val[semkey] = pv + 16
        event = (semkey, pv + 16)
        self.q[eng].append((waits, fn, (semkey, 16)))
        self._commit(event, reads, writes)
        self.ninstr += 1
        return event

    def barrier(self):
        tot = dict(self.dma_val)
        for e in ENGS:
            tot[e] = self.cnt[e]
        for eng in ENGS:
            waits = []
            wd = self.waited[eng]
            for k, v in tot.items():
                if v > 0 and k != eng and wd.get(k, 0) < v:
                    wd[k] = v
                    waits.append((k, v))
            if waits:
                self.q[eng].append((waits, None, None))
        self.res_w = {}
        self.res_r = {}

    def emit(self):
        nc = self.nc
        sem = self.sem
        q = self.q
        with nc.Block() as block:
            def run(engobj, lst):
                for waits, fn, inc in lst:
                    for k, v in waits:
                        engobj.wait_ge(sem[k], v)
                    if fn is not None:
                        getattr(engobj, fn[0])(**fn[1]).then_inc(sem[inc[0]], inc[1])

            @block.tensor
            def _(e):
                run(e, q["pe"])

            @block.scalar
            def _(e):
                run(e, q["act"])

            @block.vector
            def _(e):
                run(e, q["dve"])

            @block.gpsimd
            def _(e):
                run(e, q["pool"])

            @block.sync
            def _(e):
                run(e, q["sp"])


def host_consts():
    c = {}
    c["c_ident"] = np.eye(128, dtype=np.float32)
    c["c_iota16"] = np.tile(np.arange(16, dtype=np.float32)[None, :], (128, 1))
    half = 32
    inv = 1.0 / (10000.0 ** (np.arange(half, dtype=np.float64) * 2.0 / 64.0))
    c["c_invf"] = np.tile(inv, 4).reshape(128, 1).astype(np.float32) / np.float32(2 * math.pi)
    cc = np.arange(64)
    cs = np.clip(cc - 8, 0, 48)
    k = np.arange(64)
    c["c_colmask"] = ((k[:, None] >= cs[None, :]) & (k[:, None] < cs[None, :] + 16)).astype(np.float32)
    c["c_antiI"] = np.ascontiguousarray(np.eye(31, dtype=np.float32)[::-1])
    c["c_pos"] = np.tile(np.arange(L, dtype=np.float32)[None, :], (128, 1))
    return c


CONST_SHAPES = {"c_ident": [128, 128], "c_iota16": [128, 16], "c_invf": [128, 1],
                "c_colmask": [64, 64], "c_antiI": [31, 31], "c_pos": [128, L]}

WEIGHT_SHAPES = {
    "meta_tokens": [16, D], "norm_mix": [4, D], "w_in": [4, D, 5120],
    "lambda_q1": [4, 64], "lambda_k1": [4, 64], "lambda_q2": [4, 64], "lambda_k2": [4, 64],
    "subln_gain": [4, 128], "na_rpb": [4, 8, 15, 31], "w_branch_a": [4, 512, D],
    "w_branch_b": [4, 512, D], "w_out": [4, D, D], "norm_ffn": [4, D],
    "peer_wq": [4, D, 2048], "peer_keys": [4, 8, 2, 128, 128],
    "peer_u": [4, NEXP, D], "peer_v": [4, NEXP, D], "norm_final": [1, D],
}


def build(nseq=SEQ_PER_CORE, depth=DEPTH, dbg=False, stop_after=None):
    nc = bass.Bass("TRN2", target_bir_lowering=False)
    dr = {}
    dr["x"] = nc.dram_tensor("x", [nseq, SEQ, D], F32, kind="ExternalInput").ap()
    for k, shp in WEIGHT_SHAPES.items():
        dr[k] = nc.dram_tensor(k, shp, F32, kind="ExternalInput").ap()
    for k, shp in CONST_SHAPES.items():
        dr[k] = nc.dram_tensor(k, shp, F32, kind="ExternalInput").ap()
    y_d = nc.dram_tensor("y", [nseq, SEQ, D], F32, kind="ExternalOutput").ap()
    scr = {}
    for nm in ("QTda", "KTda", "QTna", "KTna", "ODT", "ONT"):
        scr[nm] = nc.dram_tensor("scr_" + nm, [4, 128, L], BF16).ap()
    scr["Vda"] = nc.dram_tensor("scr_Vda", [L, 4, 130], BF16).ap()
    scr["Vna"] = nc.dram_tensor("scr_Vna", [L, 8, 66], BF16).ap()
    scr["G"] = nc.dram_tensor("scr_G", [L, 2048], BF16).ap()
    PW = 160
    Dsk_t = nc.dram_tensor("scr_Dsk", [4, 120, 64, PW], F32)
    scr["Dsk"] = Dsk_t.ap()
    scr["E"] = nc.dram_tensor("scr_E", [4, 64, 7680], BF16).ap()
    scr["UV"] = nc.dram_tensor("scr_UV", [4 * NEXP, 2048], BF16).ap()

    with ExitStack() as es:
        P = Prog(nc, es)
        gsem = [P.new_dma_sem("gat%d" % i) for i in range(10)]

        uid = [0]

        def sb(es_, name, shape, dt):
            uid[0] += 1
            if shape[0] < 128:
                t_ = es_.enter_context(nc.sbuf_tensor("sb_%s_%d" % (name, uid[0]), [128] + list(shape[1:]), dt))
                return t_[0:shape[0]]
            return es_.enter_context(nc.sbuf_tensor("sb_%s_%d" % (name, uid[0]), shape, dt))

        def ps(es_, name, shape, dt):
            uid[0] += 1
            return es_.enter_context(nc.psum_tensor("ps_%s_%d" % (name, uid[0]), shape, dt))

        H = sb(es, "H", [128, NT, D], F32)
        ident_f = sb(es, "ident_f", [128, 128], F32)
        ident_b = sb(es, "ident_b", [128, 128], BF16)
        iota16 = sb(es, "iota16", [128, 16], F32)
        cosT = sb(es, "cosT", [128, L], BF16)
        sinT = sb(es, "sinT", [128, L], BF16)
        lam = sb(es, "lam", [128, 8], F32)
        gain_bc = sb(es, "gain_bc", [128, 4, 128], F32)
        gfin_bc = sb(es, "gfin_bc", [128, D], F32)
        small = sb(es, "small", [128, 64], F32)

        P.dma("sp", ("dma_start", dict(out=ident_f[:], in_=dr["c_ident"])), writes=["ident_f"])
        P.dma("sp", ("dma_start", dict(out=iota16[:], in_=dr["c_iota16"])), writes=["iota16"])
        P.op("dve", ("tensor_copy", dict(out=ident_b[:], in_=ident_f[:])), reads=["ident_f"], writes=["ident_b"])
        P.dma("sp", ("dma_start", dict(out=gfin_bc[:], in_=dr["norm_final"].partition_broadcast(128))),
              writes=["gfin_bc"])
        with ExitStack() as s1:
            invf = sb(s1, "invf", [128, 1], F32)
            pos = sb(s1, "pos", [128, L], F32)
            yy = sb(s1, "yy", [128, L], F32)
            yi = sb(s1, "yi", [128, L], I32)
            yf = sb(s1, "yf", [128, L], F32)
            lq = sb(s1, "lq", [128, 4, 4, 64], F32)
            junk = sb(s1, "junk", [128, 64], F32)
            P.dma("sp", ("dma_start", dict(out=invf[:], in_=dr["c_invf"])), writes=["invf"])
            P.dma("sp", ("dma_start", dict(out=pos[:], in_=dr["c_pos"])), writes=["pos"])
            for which, tab in ((0, sinT), (1, cosT)):
                P.op("dve", ("tensor_scalar", dict(
                    out=yy[:], in0=pos[:], scalar1=invf[:, 0:1], scalar2=0.25 * which,
                    op0=ALU.mult, op1=ALU.add)), reads=["pos", "invf"], writes=["yy"])
                P.op("dve", ("tensor_copy", dict(out=yi[:], in_=yy[:])), reads=["yy"], writes=["yi"])
                P.op("dve", ("tensor_copy", dict(out=yf[:], in_=yi[:])), reads=["yi"], writes=["yf"])
                P.op("dve", ("tensor_tensor", dict(out=yy[:], in0=yy[:], in1=yf[:], op=ALU.subtract)),
                     reads=["yy", "yf"], writes=["yy"])
                P.op("dve", ("tensor_scalar", dict(out=yy[:], in0=yy[:], scalar1=0.5, scalar2=-0.5,
                                                      op0=ALU.min, op1=ALU.max)), reads=["yy"], writes=["yy"])
                P.op("act", ("activation", dict(out=tab[:], in_=yy[:], func=AF.Sin,
                                                            scale=2 * math.pi)),
                     reads=["yy"], writes=["tab%d" % which])
            for i, nm in enumerate(("lambda_q1", "lambda_k1", "lambda_q2", "lambda_k2")):
                for l in range(4):
                    P.dma("sp", ("dma_start", dict(
                        out=lq[:, i, l, :], in_=dr[nm][l:l + 1, :].partition_broadcast(128))),
                        writes=["lq"])
            for l in range(4):
                lam_init = 0.8 - 0.6 * math.exp(-0.3 * l)
                P.op("dve", ("scalar_tensor_tensor", dict(
                    out=junk[:], in0=lq[:, 0, l, :], scalar=1.0, in1=lq[:, 1, l, :],
                    op0=ALU.mult, op1=ALU.mult, accum_out=small[:, 0:1])),
                    reads=["lq"], writes=["junk", "small"])
                P.op("dve", ("scalar_tensor_tensor", dict(
                    out=junk[:], in0=lq[:, 2, l, :], scalar=1.0, in1=lq[:, 3, l, :],
                    op0=ALU.mult, op1=ALU.mult, accum_out=small[:, 1:2])),
                    reads=["lq", "small"], writes=["junk", "small"])
                P.op("act", ("activation", dict(out=small[:, 2:4], in_=small[:, 0:2], func=AF.Exp)),
                     reads=["small"], writes=["small"])
                P.op("dve", ("scalar_tensor_tensor", dict(
                    out=lam[:, l:l + 1], in0=small[:, 2:3], scalar=lam_init, in1=small[:, 3:4],
                    op0=ALU.add, op1=ALU.subtract)), reads=["small"], writes=["lam"])
                P.op("dve", ("tensor_scalar", dict(
                    out=lam[:, 4 + l:5 + l], in0=lam[:, l:l + 1], scalar1=-1.0, scalar2=None, op0=ALU.mult)),
                    reads=["lam"], writes=["lam"])
                P.dma("sp", ("dma_start", dict(
                    out=gain_bc[:, l, :], in_=dr["subln_gain"][l:l + 1, :].partition_broadcast(128))),
                    writes=["gain_bc"])
                P.op("dve", ("tensor_scalar", dict(
                    out=gain_bc[:, l, :], in0=gain_bc[:, l, :], scalar1=1.0 - lam_init, scalar2=None, op0=ALU.mult)),
                    reads=["gain_bc"], writes=["gain_bc"])
            P.barrier()
        with ExitStack() as s1:
          if stop_after != "S0":
              rT = sb(s1, "rT", [31, 120], F32)
              antiI = sb(s1, "antiI", [31, 31], F32)
              colmask = sb(s1, "colmask", [64, 64], F32)
              Ppad = sb(s1, "Ppad", [120, PW], F32)
              Esk = sb(s1, "Esk", [64, 120, 64], F32)
              Ebf = sb(s1, "Ebf", [64, 120, 64], BF16)
              prev = ps(s1, "prev", [128, 512], F32)
              P.dma("sp", ("dma_start", dict(out=antiI[:], in_=dr["c_antiI"])), writes=["antiI"])
              P.dma("sp", ("dma_start", dict(out=colmask[:], in_=dr["c_colmask"])), writes=["colmask"])
              for l in range(depth):
                  P.dma("sp", ("dma_start", dict(
                      out=rT[:], in_=dr["na_rpb"][l].rearrange("h r j -> j (h r)"), allow_slow_non_contiguous=True)),
                      writes=["rT"])
                  P.op("pe", ("matmul", dict(out=prev[0:120, 0:31], lhsT=rT[:, :], rhs=antiI[:, :],
                                                start=True, stop=True)), reads=["rT", "antiI"], writes=["prev"])
                  P.op("dve", ("memset", dict(ap=Ppad[:], constant=-30000.0)), writes=["Ppad"])
                  P.op("dve", ("tensor_copy", dict(out=Ppad[:, 64:95], in_=prev[0:120, 0:31])),
                       reads=["prev"], writes=["Ppad"])
                  P.op("act", ("activation", dict(out=Ppad[:], in_=Ppad[:], func=AF.Exp)),
                       reads=["Ppad"], writes=["Ppad"])
                  P.dma("sp", ("dma_start", dict(
                      out=scr["Dsk"][l], in_=Ppad[:].unsqueeze(1).to_broadcast([120, 64, PW]))),
                      reads=["Ppad"], writes=["Dsk"])
                  src = bass.AP(Dsk_t, l * 120 * 64 * PW + 79, [[PW - 1, 64], [64 * PW, 120], [1, 64]])
                  P.dma("sp", ("dma_start", dict(out=Esk[:], in_=src)), reads=["Dsk"], writes=["Esk"])
                  P.op("dve", ("tensor_tensor", dict(
                      out=Ebf[:], in0=Esk[:], in1=colmask[:].unsqueeze(1).to_broadcast([64, 120, 64]),
                      op=ALU.mult)), reads=["Esk", "colmask"], writes=["Ebf"])
                  P.dma("sp", ("dma_start", dict(
                      out=scr["E"][l], in_=Ebf[:].rearrange("k a q -> k (a q)"))), reads=["Ebf"], writes=["E_d"])
              P.barrier()

        with ExitStack() as s1:
          if stop_after not in ("S0", "S"):
            u32 = [sb(s1, "u32_%d" % i, [128, 4, D], F32) for i in range(2)]
            v32 = [sb(s1, "v32_%d" % i, [128, 4, D], F32) for i in range(2)]
            uv16 = [sb(s1, "uv16_%d" % i, [128, 4, 2, D], BF16) for i in range(2)]
            it = 0
            for l in range(depth):
                for blk in range(NEXP // 512):
                    i = it % 2
                    it += 1
                    e0 = blk * 512
                    P.dma("sp", ("dma_start", dict(
                        out=u32[i][:], in_=dr["peer_u"][l, e0:e0 + 512, :].rearrange("(p k) d -> p k d", k=4))),
                        writes=["u32_%d" % i])
                    P.dma("sp", ("dma_start", dict(
                        out=v32[i][:], in_=dr["peer_v"][l, e0:e0 + 512, :].rearrange("(p k) d -> p k d", k=4))),
                        writes=["v32_%d" % i])
                    P.op("dve", ("tensor_copy", dict(out=uv16[i][:, :, 0, :], in_=u32[i][:])),
                         reads=["u32_%d" % i], writes=["uv16u_%d" % i])
                    P.op("act", ("activation", dict(out=uv16[i][:, 0:2, 1, :], in_=v32[i][:, 0:2, :], func=AF.Copy)),
                         reads=["v32_%d" % i], writes=["uv16va_%d" % i])
                    P.op("pool", ("tensor_copy", dict(out=uv16[i][:, 2:4, 1, :], in_=v32[i][:, 2:4, :])),
                         reads=["v32_%d" % i], writes=["uv16vb_%d" % i])
                    r0 = l * NEXP + e0
                    P.dma("sp", ("dma_start", dict(
                        out=scr["UV"][r0:r0 + 512, :].rearrange("(p k) c -> p k c", k=4),
                        in_=uv16[i][:].rearrange("p k two d -> p k (two d)"))),
                        reads=["uv16u_%d" % i, "uv16va_%d" % i, "uv16vb_%d" % i], writes=[("UV", l, blk)])
            P.barrier()
        def rms_rstd(Xap, n, col, tag):
            width = Xap.shape[-1]
            P.op("act", ("activation", dict(out=sq_junk[0:n, 0:width], in_=Xap, func=AF.Square,
                                               accum_out=small[0:n, col:col + 1])),
                 reads=[tag], writes=["sq_junk", "small"])
            P.op("dve", ("tensor_scalar", dict(out=small[0:n, col:col + 1], in0=small[0:n, col:col + 1],
                                                  scalar1=1.0 / width, scalar2=EPS, op0=ALU.mult, op1=ALU.add)),
                 reads=["small"], writes=["small"])
            P.op("act", ("activation", dict(out=small[0:n, col:col + 1], in_=small[0:n, col:col + 1],
                                               func=AF.Sqrt)), reads=["small"], writes=["small"])
            P.op("dve", ("reciprocal", dict(out=small[0:n, col:col + 1], in_=small[0:n, col:col + 1])),
                 reads=["small"], writes=["small"])

        sq_junk = sb(es, "sq_junk", [128, D], F32)
        wstage = [None, None]
        wst_i = [0]

        def alloc_wstage(es_):
            for i in range(2):
                wstage[i] = sb(es_, "wstage%d" % i, [128, 4096], F32)

        def load_w_bf16(dst_tile, dst_ap, src_ap, dst_name, shape3):
            a, b = shape3
            i = wst_i[0] % 2
            wst_i[0] += 1
            st = wstage[i]
            stv = st[:, 0:a * b].rearrange("p (a b) -> p a b", a=a)
            P.dma("sp", ("dma_start", dict(out=stv, in_=src_ap)), writes=["wstage%d" % i])
            P.op("pool", ("tensor_copy", dict(out=dst_ap, in_=stv)), reads=["wstage%d" % i], writes=[dst_name])
            return stv, i

        def norm_transpose(t, gbc, AT_dst, pst, xkeep=None, tag_out="AT"):
            off, n = TILES[t]
            rms_rstd(H[0:n, t, :], n, 8, ("H", t))
            dst = xkeep if xkeep is not None else abuf
            P.op("dve", ("scalar_tensor_tensor", dict(
                out=dst[0:n, :], in0=H[0:n, t, :], scalar=small[0:n, 8:9], in1=gbc[0:n, :],
                op0=ALU.mult, op1=ALU.mult)), reads=[("H", t), "small", "gbc"], writes=["abuf"])
            if xkeep is not None:
                P.op("act", ("activation", dict(out=abuf[0:n, :], in_=xkeep[0:n, :], func=AF.Copy)),
                     reads=["abuf"], writes=["abuf_b"])
                rtag = "abuf_b"
            else:
                rtag = "abuf"
            for c in range(8):
                P.op("pe", ("transpose", dict(out=pst[:, c, 0:n], in_=abuf[0:n, c * 128:(c + 1) * 128],
                                                      identity=ident_b[0:n, 0:n])),
                     reads=[rtag, "ident_b"], writes=["pst"])
            P.op("act", ("activation", dict(out=AT_dst, in_=pst[:, :, 0:n], func=AF.Copy)),
                 reads=["pst"], writes=[tag_out])

        abuf = sb(es, "abuf", [128, D], BF16)

        for s in range(nseq if stop_after not in ("S0", "S") else 0):
            P.dma("sp", ("dma_start", dict(out=H[0:16, 0, :], in_=dr["meta_tokens"])), writes=[("H", 0)])
            for t in range(1, NT):
                P.dma("sp", ("dma_start", dict(out=H[:, t, :], in_=dr["x"][s, 128 * (t - 1):128 * t, :])),
                      writes=[("H", t)])
            for l in range(depth):
                with ExitStack() as pa:
                    alloc_wstage(pa)
                    AT = sb(pa, "AT", [128, 8, L], BF16)
                    gbc = sb(pa, "gbc", [128, D], F32)
                    abuf_f = None
                    wbf = [sb(pa, "wbf%d" % i, [128, 8, 512], BF16) for i in range(2)]
                    wrot = sb(pa, "wrot", [128, 8, 512], BF16)
                    ev_f = sb(pa, "ev_f", [128, 2, 512], F32)
                    ev_b = [sb(pa, "ev_b%d" % i, [128, 520], BF16) for i in range(2)]
                    vda_b = [sb(pa, "vda_b%d" % i, [128, 4, 130], BF16) for i in range(2)]
                    vna_b = [sb(pa, "vna_b%d" % i, [128, 8, 66], BF16) for i in range(2)]
                    pst = ps(pa, "pstA", [128, 8, 128], BF16)
                    pz = [ps(pa, "pzA%d" % i, [128, 512], F32) for i in range(4)]
                    P.dma("sp", ("dma_start", dict(out=gbc[:], in_=dr["norm_mix"][l:l + 1, :].partition_broadcast(128))),
                          writes=["gbc"])
                    for i in range(2):
                        P.op("pool", ("memset", dict(ap=vda_b[i][:], constant=1.0)), writes=["vda_b%d" % i])
                        P.op("pool", ("memset", dict(ap=vna_b[i][:], constant=1.0)), writes=["vna_b%d" % i])
                    for t in range(NT):
                        off, n = TILES[t]
                        norm_transpose(t, gbc, AT[:, :, off:off + n], pst, tag_out=("AT", t))
                    AT_all = [("AT", t) for t in range(NT)]
                    w3 = dr["w_in"][l].rearrange("(kc p) c -> p kc c", p=128)
                    ecnt = [0]
                    def issue_wload(cc_):
                        i_ = wst_i[0] % 2
                        wst_i[0] += 1
                        stv_ = wstage[i_][:, 0:4096].rearrange("p (a b) -> p a b", a=8)
                        P.dma("sp", ("dma_start", dict(out=stv_, in_=w3[:, :, cc_ * 512:(cc_ + 1) * 512])),
                              writes=["wstage%d" % i_])
                        return stv_, i_

                    pre = issue_wload(0)
                    for cc in range(10):
                        wb = wbf[cc % 2]
                        wname = "wbf%d" % (cc % 2)
                        stv, sti = pre
                        P.op("pool", ("tensor_copy", dict(out=wb[:], in_=stv)), reads=["wstage%d" % sti], writes=[wname])
                        pre = issue_wload(cc + 1) if cc + 1 < 10 else None
                        if cc in (0, 1):
                            st5 = stv.rearrange("p a (g two h) -> p a g two h", two=2, h=32)
                            wr5 = wrot[:].rearrange("p a (g two h) -> p a g two h", two=2, h=32)
                            for kc in range(8):
                                P.op("pool", ("tensor_scalar", dict(
                                    out=wr5[:, kc, :, 0, :], in0=st5[:, kc, :, 1, :], scalar1=-1.0, scalar2=None,
                                    op0=ALU.mult)), reads=["wstage%d" % sti], writes=["wrot"])
                                P.op("pool", ("tensor_copy", dict(
                                    out=wr5[:, kc, :, 1, :], in_=st5[:, kc, :, 0, :])),
                                    reads=["wstage%d" % sti], writes=["wrot"])
                        if cc in (0, 1, 3, 4):
                            dstn = {0: "QTda", 1: "KTda", 3: "QTna", 4: "KTna"}[cc]
                            for hb in range(4):
                                for (q0, nq, _tl) in QCHUNKS:
                                    k = ecnt[0]
                                    ecnt[0] += 1
                                    pa_ = pz[(2 * k) % 4]
                                    pb_ = pz[(2 * k + 1) % 4]
                                    pan = "pzA%d" % ((2 * k) % 4)
                                    pbn = "pzA%d" % ((2 * k + 1) % 4)
                                    evb = ev_b[k % 2]
                                    evn = "ev_b%d" % (k % 2)
                                    for kc in range(8):
                                        P.op("pe", ("matmul", dict(
                                            out=pa_[:, 0:nq], lhsT=wb[:, kc, hb * 128:(hb + 1) * 128],
                                            rhs=AT[:, kc, q0:q0 + nq], start=(kc == 0), stop=(kc == 7))),
                                            reads=[wname] + AT_all, writes=[pan])
                                    if cc in (0, 1):
                                        for kc in range(8):
                                            P.op("pe", ("matmul", dict(
                                                out=pb_[:, 0:nq], lhsT=wrot[:, kc, hb * 128:(hb + 1) * 128],
                                                rhs=AT[:, kc, q0:q0 + nq], start=(kc == 0), stop=(kc == 7))),
                                                reads=["wrot"] + AT_all, writes=[pbn])
                                        P.op("dve", ("tensor_tensor", dict(
                                            out=ev_f[:, 0, 0:nq], in0=pa_[:, 0:nq], in1=cosT[:, q0:q0 + nq], op=ALU.mult)),
                                            reads=[pan, "tab1"], writes=["ev_f0"])
                                        P.op("dve", ("tensor_tensor", dict(
                                            out=ev_f[:, 1, 0:nq], in0=pb_[:, 0:nq], in1=sinT[:, q0:q0 + nq], op=ALU.mult)),
                                            reads=[pbn, "tab0"], writes=["ev_f1"])
                                        P.op("dve", ("tensor_tensor", dict(
                                            out=evb[:, 0:nq], in0=ev_f[:, 0, 0:nq], in1=ev_f[:, 1, 0:nq], op=ALU.add)),
                                            reads=["ev_f0", "ev_f1"], writes=[evn])
                                    else:
                                        P.op("act", ("activation", dict(
                                            out=evb[:, 0:nq], in_=pa_[:, 0:nq], func=AF.Copy)),
                                            reads=[pan], writes=[evn])
                                    P.dma("sp", ("dma_start", dict(
                                        out=scr[dstn][hb, :, q0:q0 + nq], in_=evb[:, 0:nq])),
                                        reads=[evn], writes=[(dstn, hb, q0)])
                        else:
                            for t in range(NT):
                                off, n = TILES[t]
                                k = ecnt[0]
                                ecnt[0] += 1
                                pa_ = pz[k % 4]
                                pan = "pzA%d" % (k % 4)
                                for kc in range(8):
                                    P.op("pe", ("matmul", dict(
                                        out=pa_[0:n, :], lhsT=AT[:, kc, off:off + n], rhs=wb[:, kc, :],
                                        start=(kc == 0), stop=(kc == 7))), reads=[wname, ("AT", t)], writes=[pan])
                                if cc == 2:
                                    vb = vda_b[k % 2]
                                    vn = "vda_b%d" % (k % 2)
                                    P.op("act", ("activation", dict(
                                        out=vb[0:n, :, 0:128], in_=pa_[0:n, :].rearrange("p (h e) -> p h e", h=4),
                                        func=AF.Copy)), reads=[pan], writes=[vn])
                                    P.dma("sp", ("dma_start", dict(
                                        out=scr["Vda"][off:off + n], in_=vb[0:n])), reads=[vn], writes=[("Vda", t)])
                                elif cc == 5:
                                    vb = vna_b[k % 2]
                                    vn = "vna_b%d" % (k % 2)
                                    P.op("act", ("activation", dict(
                                        out=vb[0:n, :, 0:64], in_=pa_[0:n, :].rearrange("p (h e) -> p h e", h=8),
                                        func=AF.Copy)), reads=[pan], writes=[vn])
                                    P.dma("sp", ("dma_start", dict(
                                        out=scr["Vna"][off:off + n], in_=vb[0:n])), reads=[vn], writes=[("Vna", t)])
                                else:
                                    evb = ev_b[k % 2]
                                    evn = "ev_b%d" % (k % 2)
                                    P.op("act", ("activation", dict(
                                        out=evb[0:n, 0:512], in_=pa_[0:n, :], func=AF.Sigmoid)),
                                        reads=[pan], writes=[evn])
                                    gc0 = (cc - 6) * 512
                                    P.dma("sp", ("dma_start", dict(
                                        out=scr["G"][off:off + n, gc0:gc0 + 512], in_=evb[0:n, 0:512])),
                                        reads=[evn], writes=[("G", t, cc)])
                    P.barrier()
                if stop_after == "A":
                    break
                with ExitStack() as pb:
                    KT = sb(pb, "KT", [128, 4, L], BF16)
                    VA = sb(pb, "VA", [128, NT, 4, 130], BF16)
                    QT = [sb(pb, "QT%d" % i, [128, 4, 512], BF16) for i in range(2)]
                    PT = [sb(pb, "PT%d" % i, [128, 512], BF16) for i in range(3)]
                    OD = sb(pb, "OD", [128, 4, 512], BF16)
                    ODTs = sb(pb, "ODTs", [128, 4, 512], BF16)
                    of = sb(pb, "of", [128, 2, 128], F32)
                    psS = [ps(pb, "psS%d" % i, [128, 512], F32) for i in range(2)]
                    acc = [[ps(pb, "acc%d_%d" % (m, i), [128, 2, 256], F32) for i in range(2)] for m in range(2)]
                    pstB = ps(pb, "pstB", [128, 1024], BF16)[:, 0:512].rearrange("p (h t) -> p h t", h=4)
                    P.dma("sp", ("dma_start", dict(out=KT[:], in_=scr["KTda"].rearrange("h p t -> p h t"))),
                          reads=[("KTda", hb, q0) for hb in range(4) for (q0, _a, _b) in QCHUNKS], writes=["KT"])
                    for t in range(NT):
                        off, n = TILES[t]
                        P.dma("sp", ("dma_start", dict(out=VA[0:n, t], in_=scr["Vda"][off:off + n])),
                              reads=[("Vda", t)], writes=["VA"])
                    sc_cnt = [0]
                    for ci, (q0, nq, tl) in enumerate(QCHUNKS):
                        qt = QT[ci % 2]
                        qtn = "QT%d" % (ci % 2)
                        P.dma("sp", ("dma_start", dict(
                            out=qt[:, :, 0:nq], in_=scr["QTda"][:, :, q0:q0 + nq].rearrange("h p t -> p h t"))),
                            reads=[("QTda", hb, q0) for hb in range(4)], writes=[qtn])
                        nsub = len(tl)
                        for h in range(4):
                            items = [(m, kt) for m in range(2) for kt in range(NT)]
                            bufs = []

                            def emit_score(m, kt):
                                nonlocal_sc = sc_cnt[0]
                                sc_cnt[0] += 1
                                bp = 64 * m
                                koff, kn = TILES[kt]
                                pS = psS[nonlocal_sc % 2]
                                pSn = "psS%d" % (nonlocal_sc % 2)
                                pt = PT[nonlocal_sc % 3]
                                ptn = "PT%d" % (nonlocal_sc % 3)
                                P.op("pe", ("matmul", dict(
                                    out=pS[0:kn, 0:nq], lhsT=KT[bp:bp + 64, h, koff:koff + kn],
                                    rhs=qt[bp:bp + 64, h, 0:nq], start=True, stop=True)),
                                    reads=["KT", qtn], writes=[pSn])
                                P.op("act", ("activation", dict(
                                    out=pt[0:kn, 0:nq], in_=pS[0:kn, 0:nq], func=AF.Exp, scale=0.125)),
                                    reads=[pSn], writes=[ptn])
                                return pt, ptn

                            def emit_pv(m, kt, pt, ptn):
                                koff, kn = TILES[kt]
                                for j in range(nsub):
                                    nqs = TILES[tl[j]][1]
                                    a_ = acc[m][j // 2]
                                    an = "acc%d_%d" % (m, j // 2)
                                    P.op("pe", ("matmul", dict(
                                        out=a_[0:nqs, j % 2, 0:129], lhsT=pt[0:kn, j * 128:j * 128 + nqs],
                                        rhs=VA[0:kn, kt, h, 0:129], start=(kt == 0 and j % 2 == 0), stop=(kt == NT - 1))),
                                        reads=[ptn, "VA"], writes=[an])

                            cur = emit_score(*items[0])
                            for ii, (m, kt) in enumerate(items):
                                nxt = emit_score(*items[ii + 1]) if ii + 1 < len(items) else None
                                emit_pv(m, kt, cur[0], cur[1])
                                cur = nxt
                            for j in range(nsub):
                                nqs = TILES[tl[j]][1]
                                a0 = acc[0][j // 2]
                                a1 = acc[1][j // 2]
                                a0n = "acc0_%d" % (j // 2)
                                a1n = "acc1_%d" % (j // 2)
                                P.op("dve", ("reciprocal", dict(
                                    out=small[0:nqs, 16:17], in_=a0[0:nqs, j % 2, 128:129])), reads=[a0n], writes=["small"])
                                P.op("dve", ("reciprocal", dict(
                                    out=small[0:nqs, 17:18], in_=a1[0:nqs, j % 2, 128:129])), reads=[a1n, "small"], writes=["small"])
                                P.op("dve", ("tensor_scalar", dict(
                                    out=small[0:nqs, 17:18], in0=small[0:nqs, 17:18], scalar1=lam[0:nqs, 4 + l:5 + l],
                                    scalar2=None, op0=ALU.mult)), reads=["small", "lam"], writes=["small"])
                                P.op("dve", ("tensor_scalar", dict(
                                    out=of[0:nqs, 0, :], in0=a1[0:nqs, j % 2, 0:128], scalar1=small[0:nqs, 17:18],
                                    scalar2=None, op0=ALU.mult)), reads=[a1n, "small"], writes=["of0"])
                                P.op("dve", ("scalar_tensor_tensor", dict(
                                    out=of[0:nqs, 1, :], in0=a0[0:nqs, j % 2, 0:128], scalar=small[0:nqs, 16:17],
                                    in1=of[0:nqs, 0, :], op0=ALU.mult, op1=ALU.add)),
                                    reads=[a0n, "small", "of0"], writes=["of1"])
                                rms_rstd(of[0:nqs, 1, :], nqs, 18, "of1")
                                P.op("dve", ("scalar_tensor_tensor", dict(
                                    out=OD[0:nqs, j, h * 128:(h + 1) * 128], in0=of[0:nqs, 1, :],
                                    scalar=small[0:nqs, 18:19], in1=gain_bc[0:nqs, l, :], op0=ALU.mult, op1=ALU.mult)),
                                    reads=["of1", "small", "gain_bc"], writes=["OD"])
                        if stop_after == "B":
                            for j in range(nsub):
                                tj = tl[j]
                                if tj == 0:
                                    continue
                                P.op("act", ("activation", dict(out=sq_junk[:, 0:512], in_=OD[:, j, :], func=AF.Copy)),
                                     reads=["OD"], writes=["sq_junk"])
                                P.dma("sp", ("dma_start", dict(out=y_d[s, 128 * (tj - 1):128 * tj, 0:512], in_=sq_junk[:, 0:512])),
                                      reads=["sq_junk"], writes=[("y", tj)])
                        for j in range(nsub):
                            nqs = TILES[tl[j]][1]
                            for h in range(4):
                                P.op("pe", ("transpose", dict(
                                    out=pstB[:, h, 0:nqs], in_=OD[0:nqs, j, h * 128:(h + 1) * 128],
                                    identity=ident_b[0:nqs, 0:nqs])), reads=["OD", "ident_b"], writes=["pstB"])
                            P.op("act", ("activation", dict(
                                out=ODTs[:, :, j * 128:j * 128 + nqs], in_=pstB[:, :, 0:nqs], func=AF.Copy)),
                                reads=["pstB"], writes=["ODTs"])
                        P.dma("sp", ("dma_start", dict(
                            out=scr["ODT"][:, :, q0:q0 + nq].rearrange("h p t -> p h t"), in_=ODTs[:, :, 0:nq])),
                            reads=["ODTs"], writes=[("ODT", ci)])
                    P.barrier()
                if stop_after == "B":
                    break
                with ExitStack() as pc:
                    KT = sb(pc, "KTn", [128, 4, L], BF16)
                    QT = sb(pc, "QTn", [128, 4, L], BF16)
                    VN = sb(pc, "VN", [64, 33, 8, 66], BF16)
                    Et = sb(pc, "Et", [64, 8, 16, 64], BF16)
                    Pn = [sb(pc, "Pn%d" % i, [64, 4, 8, 64], BF16) for i in range(2)]
                    Pm = [sb(pc, "Pm%d" % i, [16, 4, 64], BF16) for i in range(2)]
                    ONr = [sb(pc, "ONr%d" % i, [64, 512], BF16) for i in range(2)]
                    ONT2 = [sb(pc, "ONT2_%d" % i, [128, 4, 128], BF16) for i in range(2)]
                    psN = [ps(pc, "psN%d" % i, [128, 512], F32) for i in range(4)]
                    psM = ps(pc, "psM", [128, 512], F32)[:, 0:256].rearrange("p (i q) -> p i q", i=4)
                    psO = ps(pc, "psO", [128, 512], F32)[:, 0:512].rearrange("p (i e) -> p i e", i=4)
                    pstC = ps(pc, "pstC", [128, 1024], BF16)[:, 0:256].rearrange("p (c q) -> p c q", c=4)
                    P.dma("sp", ("dma_start", dict(out=KT[:], in_=scr["KTna"].rearrange("h p t -> p h t"))),
                          reads=[("KTna", hb, q0) for hb in range(4) for (q0, _a, _b) in QCHUNKS], writes=["KTn"])
                    P.dma("sp", ("dma_start", dict(out=QT[:], in_=scr["QTna"].rearrange("h p t -> p h t"))),
                          reads=[("QTna", hb, q0) for hb in range(4) for (q0, _a, _b) in QCHUNKS], writes=["QTn"])
                    P.dma("sp", ("dma_start", dict(out=VN[0:16, 0], in_=scr["Vna"][0:16])),
                          reads=[("Vna", t) for t in range(NT)], writes=["VN"])
                    P.dma("sp", ("dma_start", dict(
                        out=VN[:, 1:33], in_=scr["Vna"][16:L].rearrange("(r k) h e -> k r h e", k=64))),
                        reads=[("Vna", t) for t in range(NT)], writes=["VN"])
                    P.dma("sp", ("dma_start", dict(out=Et[:, :, 0:15, :].rearrange("k h r q -> k h (r q)"),
                                                   in_=scr["E"][l].rearrange("k (h x) -> k h x", h=8))),
                          reads=["E_d"], writes=["Et"])
                    P.op("dve", ("memset", dict(ap=Et[:, :, 15, :], constant=0.0)), writes=["Et15"])

                    def na_block(qoff, nqr, rs, rho0, nkr, out_tile, out_name, gi):
                        for hg in range(2):
                            pn = Pn[gi[0] % 2]
                            pnn = "Pn%d" % (gi[0] % 2)
                            pm = Pm[gi[0] % 2]
                            pmn = "Pm%d" % (gi[0] % 2)
                            gi[0] += 1
                            for i in range(4):
                                h = 4 * hg + i
                                pr, bp = h // 2, 64 * (h % 2)
                                for jr in range(nkr):
                                    k0 = 16 + 64 * (rs + jr)
                                    P.op("pe", ("matmul", dict(
                                        out=psN[i][0:64, jr * 64:jr * 64 + nqr], lhsT=KT[bp:bp + 64, pr, k0:k0 + 64],
                                        rhs=QT[bp:bp + 64, pr, qoff:qoff + nqr], start=True, stop=True)),
                                        reads=["KTn", "QTn"], writes=["psN%d" % i])
                                P.op("pe", ("matmul", dict(
                                    out=psM[0:16, i, 0:nqr], lhsT=KT[bp:bp + 64, pr, 0:16],
                                    rhs=QT[bp:bp + 64, pr, qoff:qoff + nqr], start=True, stop=True)),
                                    reads=["KTn", "QTn"], writes=["psM"])
                            if nkr > 0:
                                for i in range(4):
                                    P.op("act", ("activation", dict(
                                        out=pn[:, i, 0:nkr, 0:nqr],
                                        in_=psN[i][0:64, 0:nkr * 64].rearrange("p (r q) -> p r q", q=64)[:, :, 0:nqr],
                                        func=AF.Exp, scale=0.125)), reads=["psN%d" % i], writes=[pnn])
                                P.op("dve", ("tensor_tensor", dict(
                                    out=pn[:, :, 0:nkr, 0:nqr], in0=pn[:, :, 0:nkr, 0:nqr],
                                    in1=Et[:, 4 * hg:4 * hg + 4, rho0:rho0 + nkr, 0:nqr], op=ALU.mult)),
                                    reads=[pnn, "Et", "Et15"], writes=[pnn])
                            import os
                            NSTEP = int(os.environ.get("NA_STEP", "9"))
                            if NSTEP < 2:
                                continue
                            P.op("act", ("activation", dict(
                                out=pm[0:16, :, 0:nqr], in_=psM[0:16, :, 0:nqr], func=AF.Exp, scale=0.125)),
                                reads=["psM"], writes=[pmn])
                            if NSTEP < 3:
                                continue
                            for i in range(4):
                                h = 4 * hg + i
                                for jr in range(nkr):
                                    P.op("pe", ("matmul", dict(
                                        out=psO[0:nqr, i, 0:65], lhsT=pn[:, i, jr, 0:nqr], rhs=VN[:, 1 + rs + jr, h, 0:65],
                                        start=(jr == 0), stop=False)), reads=[pnn, "VN"], writes=["psO"])
                                P.op("pe", ("matmul", dict(
                                    out=psO[0:nqr, i, 0:65], lhsT=pm[0:16, i, 0:nqr], rhs=VN[0:16, 0, h, 0:65],
                                    start=(nkr == 0), stop=True)), reads=[pmn, "VN"], writes=["psO"])
                            if NSTEP < 4:
                                continue
                            P.op("dve", ("reciprocal", dict(out=small[0:nqr, 24:28], in_=psO[0:nqr, :, 64])),
                                 reads=["psO"], writes=["small"])
                            if NSTEP < 5:
                                continue
                            P.op("dve", ("tensor_tensor", dict(
                                out=out_tile[0:nqr, hg * 256:(hg + 1) * 256].rearrange("p (h e) -> p h e", h=4),
                                in0=psO[0:nqr, :, 0:64],
                                in1=small[0:nqr, 24:28].unsqueeze(2).to_broadcast([nqr, 4, 64]), op=ALU.mult)),
                                reads=["psO", "small"], writes=[out_name])

                    gi = [0]
                    import os
                    NR = int(os.environ.get("NA_ROWS", "32"))
                    for r in range(NR):
                        rs = min(max(r - 4, 0), 24)
                        rho0 = rs - r + 7
                        onr = ONr[r % 2]
                        onrn = "ONr%d" % (r % 2)
                        na_block(16 + 64 * r, 64, rs, rho0, 8, onr, onrn, gi)
                        if stop_after == "C":
                            P.op("act", ("activation", dict(out=sq_junk[0:64, 0:512], in_=onr[0:64, :], func=AF.Copy)),
                                 reads=[onrn], writes=["sq_junk"])
                            P.dma("sp", ("dma_start", dict(out=y_d[s, 64 * r:64 * r + 64, 0:512], in_=sq_junk[0:64, 0:512])),
                                  reads=["sq_junk"], writes=[("y", r)])
                        o2 = ONT2[(r // 2) % 2]
                        o2n = "ONT2_%d" % ((r // 2) % 2)
                        for c4 in range(4):
                            P.op("pe", ("transpose", dict(
                                out=pstC[:, c4, 0:64], in_=onr[0:64, c4 * 128:(c4 + 1) * 128], identity=ident_b[0:64, 0:64])),
                                reads=[onrn, "ident_b"], writes=["pstC"])
                        P.op("act", ("activation", dict(
                            out=o2[:, :, (r % 2) * 64:(r % 2) * 64 + 64], in_=pstC[:, :, 0:64], func=AF.Copy)),
                            reads=["pstC"], writes=[o2n])
                        if r % 2 == 1:
                            t0 = 16 + 64 * (r - 1)
                            P.dma("sp", ("dma_start", dict(
                                out=scr["ONT"][:, :, t0:t0 + 128].rearrange("h p t -> p h t"), in_=o2[:])),
                                reads=[o2n], writes=[("ONT", 1 + r // 2)])
                    import os
                    NDBG = int(os.environ.get("NA_DBG", "9"))
                    NDBG = int(os.environ.get("NA_DBG", "9"))
                    if os.environ.get("NA_DUP"):
                        na_block(16 + 64 * 31, 64, 24, 0, 8, ONr[1], "ONr1", gi)
                    if NDBG >= 2:
                        na_block(0, 64, 0, 15, 1, ONr[0], "ONr0", gi)
                    for c4 in range(4 if NDBG >= 3 else 0):
                        P.op("pe", ("transpose", dict(
                            out=pstC[:, c4, 0:16], in_=ONr[0][0:16, c4 * 128:(c4 + 1) * 128], identity=ident_b[0:16, 0:16])),
                            reads=["ONr0", "ident_b"], writes=["pstC"])
                    if NDBG >= 3:
                        P.op("act", ("activation", dict(out=ONT2[0][:, :, 0:16], in_=pstC[:, :, 0:16], func=AF.Copy)),
                             reads=["pstC"], writes=["ONT2_0"])
                    else:
                        P.op("dve", ("memset", dict(ap=ONT2[0][:, :, 0:16], constant=0.0)), writes=["ONT2_0"])
                    P.dma("sp", ("dma_start", dict(out=scr["ONT"][:, :, 0:16].rearrange("h p t -> p h t"),
                                                     in_=ONT2[0][:, :, 0:16])), reads=["ONT2_0"], writes=[("ONT", 0)])
                    P.barrier()
                if stop_after == "C":
                    break
                with ExitStack() as pd:
                    alloc_wstage(pd)
                    Wa = sb(pd, "Wa", [128, 4, D], BF16)
                    Wb = sb(pd, "Wb", [128, 4, D], BF16)
                    Wo = sb(pd, "Wo", [128, 8, D], BF16)
                    odt = [sb(pd, "odt%d" % i, [128, 4, 128], BF16) for i in range(2)]
                    ont = [sb(pd, "ont%d" % i, [128, 4, 128], BF16) for i in range(2)]
                    gt = [sb(pd, "gt%d" % i, [128, 2048], BF16) for i in range(2)]
                    mf = sb(pd, "mf", [128, 2, D], F32)
                    mb = sb(pd, "mb", [128, D], BF16)
                    mT = sb(pd, "mT", [128, 8, 128], BF16)
                    pya = ps(pd, "pya", [128, 2, 512], F32)
                    pyb = ps(pd, "pyb", [128, 2, 512], F32)
                    pyo = ps(pd, "pyo", [128, 2, 512], F32)
                    pstD = ps(pd, "pstD", [128, 8, 128], BF16)
                    for half in range(2):
                        load_w_bf16(Wa, Wa[:, :, half * 512:(half + 1) * 512],
                                    dr["w_branch_a"][l].rearrange("(kc p) c -> p kc c", p=128)[:, :, half * 512:(half + 1) * 512],
                                    "Wa", (4, 512))
                        load_w_bf16(Wb, Wb[:, :, half * 512:(half + 1) * 512],
                                    dr["w_branch_b"][l].rearrange("(kc p) c -> p kc c", p=128)[:, :, half * 512:(half + 1) * 512],
                                    "Wb", (4, 512))
                        load_w_bf16(Wo, Wo[:, :, half * 512:(half + 1) * 512],
                                    dr["w_out"][l].rearrange("(kc p) c -> p kc c", p=128)[:, :, half * 512:(half + 1) * 512],
                                    "Wo", (8, 512))
                    for t in range(NT):
                        off, n = TILES[t]
                        od, on_, g_ = odt[t % 2], ont[t % 2], gt[t % 2]
                        odn, onn, gn = "odt%d" % (t % 2), "ont%d" % (t % 2), "gt%d" % (t % 2)
                        P.dma("sp", ("dma_start", dict(
                            out=od[:, :, 0:n], in_=scr["ODT"][:, :, off:off + n].rearrange("h p t -> p h t"))),
                            reads=[("ODT", ci) for ci in range(5)], writes=[odn])
                        P.dma("sp", ("dma_start", dict(
                            out=on_[:, :, 0:n], in_=scr["ONT"][:, :, off:off + n].rearrange("h p t -> p h t"))),
                            reads=[("ONT", i) for i in range(17)], writes=[onn])
                        P.dma("sp", ("dma_start", dict(out=g_[0:n, :], in_=scr["G"][off:off + n, :])),
                              reads=[("G", t, cc) for cc in range(6, 10)], writes=[gn])
                        for half in range(2):
                            for c4 in range(4):
                                P.op("pe", ("matmul", dict(
                                    out=pya[0:n, half, :], lhsT=od[:, c4, 0:n], rhs=Wa[:, c4, half * 512:(half + 1) * 512],
                                    start=(c4 == 0), stop=(c4 == 3))), reads=[odn, "Wa"], writes=["pya"])
                            for c4 in range(4):
                                P.op("pe", ("matmul", dict(
                                    out=pyb[0:n, half, :], lhsT=on_[:, c4, 0:n], rhs=Wb[:, c4, half * 512:(half + 1) * 512],
                                    start=(c4 == 0), stop=(c4 == 3))), reads=[onn, "Wb"], writes=["pyb"])
                        P.op("dve", ("tensor_tensor", dict(
                            out=mf[0:n, 0, :], in0=pya[0:n].rearrange("p a b -> p (a b)"), in1=g_[0:n, 0:1024], op=ALU.mult)),
                            reads=["pya", gn], writes=["mf0"])
                        P.op("dve", ("tensor_tensor", dict(
                            out=mf[0:n, 1, :], in0=pyb[0:n].rearrange("p a b -> p (a b)"), in1=g_[0:n, 1024:2048], op=ALU.mult)),
                            reads=["pyb", gn], writes=["mf1"])
                        P.op("dve", ("tensor_tensor", dict(
                            out=mb[0:n, :], in0=mf[0:n, 0, :], in1=mf[0:n, 1, :], op=ALU.add)),
                            reads=["mf0", "mf1"], writes=["mb"])
                        for c in range(8):
                            P.op("pe", ("transpose", dict(
                                out=pstD[:, c, 0:n], in_=mb[0:n, c * 128:(c + 1) * 128], identity=ident_b[0:n, 0:n])),
                                reads=["mb", "ident_b"], writes=["pstD"])
                        P.op("act", ("activation", dict(out=mT[:, :, 0:n], in_=pstD[:, :, 0:n], func=AF.Copy)),
                             reads=["pstD"], writes=["mT"])
                        for half in range(2):
                            for c in range(8):
                                P.op("pe", ("matmul", dict(
                                    out=pyo[0:n, half, :], lhsT=mT[:, c, 0:n], rhs=Wo[:, c, half * 512:(half + 1) * 512],
                                    start=(c == 0), stop=(c == 7))), reads=["mT", "Wo"], writes=["pyo"])
                        P.op("dve", ("tensor_tensor", dict(
                            out=H[0:n, t, :], in0=H[0:n, t, :], in1=pyo[0:n].rearrange("p a b -> p (a b)"), op=ALU.add)),
                            reads=["pyo", ("H", t)], writes=[("H", t)])
                    P.barrier()
                if stop_after == "D":
                    break
                with ExitStack() as pe_:
                    Wq = sb(pe_, "Wq", [128, 8, 2048], BF16)
                    keysT = sb(pe_, "keysT", [128, 16, 128], BF16)
                    gbc = sb(pe_, "gbcE", [128, D], F32)
                    P.dma("sp", ("dma_start", dict(out=gbc[:], in_=dr["norm_ffn"][l:l + 1, :].partition_broadcast(128))),
                          writes=["gbc"])
                    with ExitStack() as pw:
                        alloc_wstage(pw)
                        keysf = sb(pw, "keysf", [128, 16, 128], F32)
                        pkT = ps(pw, "pkT", [128, 4, 128], F32)
                        wq3 = dr["peer_wq"][l].rearrange("(kc p) c -> p kc c", p=128)
                        for c4 in range(4):
                            load_w_bf16(Wq, Wq[:, :, c4 * 512:(c4 + 1) * 512], wq3[:, :, c4 * 512:(c4 + 1) * 512], "Wq", (8, 512))
                        P.dma("sp", ("dma_start", dict(
                            out=keysf[:], in_=dr["peer_keys"][l].rearrange("h p k d -> k (h p) d"))), writes=["keysf"])
                        for g4 in range(4):
                            for i in range(4):
                                gq = 4 * g4 + i
                                P.op("pe", ("transpose", dict(
                                    out=pkT[:, i, :], in_=keysf[:, gq, :], identity=ident_f[:, :])),
                                    reads=["keysf", "ident_f"], writes=["pkT"])
                            P.op("act", ("activation", dict(out=keysT[:, 4 * g4:4 * g4 + 4, :], in_=pkT[:], func=AF.Copy)),
                                 reads=["pkT"], writes=["keysT"])
                        P.barrier()
                    CT = sb(pe_, "CT", [128, 8, 128], BF16)
                    qT = sb(pe_, "qT", [128, 16, 128], BF16)
                    ssb = sb(pe_, "ssb", [128, 16, 128], F32)
                    swk = sb(pe_, "swk", [128, 256], F32)
                    vals = sb(pe_, "vals", [128, 16, 16], F32)
                    idxu = sb(pe_, "idxu", [128, 16, 16], U32)
                    idxf = sb(pe_, "idxf", [128, 16, 16], F32)
                    cs_ = sb(pe_, "cs", [128, 8, 256], F32)
                    scv = sb(pe_, "scv", [128, 8, 16], F32)
                    posu = sb(pe_, "posu", [128, 8, 16], U32)
                    posf = sb(pe_, "posf", [128, 8, 16], F32)
                    ai_ = sb(pe_, "ai", [128, 8, 16], I32)
                    af_ = sb(pe_, "af", [128, 8, 16], F32)
                    bf_ = sb(pe_, "bf", [128, 8, 16], F32)
                    msk = ssb[:].rearrange("p g k -> p (g k)").rearrange("p (h a b) -> p h a b", h=8, a=16)
                    i1s = sb(pe_, "i1s", [128, 8, 16], F32)
                    i2s = sb(pe_, "i2s", [128, 8, 16], F32)
                    ef = sb(pe_, "ef", [128, 128], F32)
                    ei = [sb(pe_, "ei%d" % i, [128, 128], I32) for i in range(2)]
                    gg = sb(pe_, "gg", [128, 8, 16], F32)
                    gg2 = sb(pe_, "gg2", [128, 8, 16], F32)
                    abuf2 = sb(pe_, "abuf2", [128, D], BF16)
                    ei2 = sb(pe_, "ei2", [128, 128], I32)
                    hid = sb(pe_, "hid", [128, 128], F32)
                    gl = sb(pe_, "gl", [128, 128], F32)
                    wv = sb(pe_, "wv", [128, 128], F32)
                    djunk = sq_junk[:].bitcast(BF16)[:, 0:D]
                    NG = 10
                    gat = [sb(pe_, "gat%d" % i, [128, 2 * D], BF16) for i in range(NG)]
                    dg = [sb(pe_, "dg%d" % i, [128, 128], BF16) for i in range(4)]
                    pacc = ps(pe_, "pacc", [128, 2, 512], F32)
                    pstE = ps(pe_, "pstE", [128, 8, 128], BF16)
                    pq = [ps(pe_, "pq%d" % i, [128, 4, 128], F32) for i in range(4)]
                    for i in range(2):
                        P.op("pool", ("memset", dict(ap=ei[i][:], constant=0)), writes=["ei%d" % i])
                    gslot = [0]
                    abufs = [abuf, abuf2]
                    ggs = [gg, gg2]
                    eis = [ei[0], ei2]

                    def prologue(t):
                        off, n = TILES[t]
                        pb = t % 2
                        ab, abn = abufs[pb], "abuf%d" % pb
                        g_, gn = ggs[pb], "gg%d" % pb
                        if n == 128:
                            eit, ein = eis[pb], "eiN%d" % pb
                        else:
                            eit, ein = ei[1], "ei1"
                        st = []

                        def s_norm():
                            rms_rstd(H[0:n, t, :], n, 8, ("H", t))
                            P.op("dve", ("scalar_tensor_tensor", dict(
                                out=ab[0:n, :], in0=H[0:n, t, :], scalar=small[0:n, 8:9], in1=gbc[0:n, :],
                                op0=ALU.mult, op1=ALU.mult)), reads=[("H", t), "small", "gbc"], writes=[abn])
                            for c in range(8):
                                P.op("pe", ("transpose", dict(out=pstE[:, c, 0:n], in_=ab[0:n, c * 128:(c + 1) * 128],
                                                              identity=ident_b[0:n, 0:n])),
                                     reads=[abn, "ident_b"], writes=["pstE"])
                            P.op("act", ("activation", dict(out=CT[:, :, 0:n], in_=pstE[:, :, 0:n], func=AF.Copy)),
                                 reads=["pstE"], writes=["CT"])
                        st.append(s_norm)

                        def s_q(g4):
                            for gq in range(4 * g4, 4 * g4 + 4):
                                pqq = pq[gq // 4]
                                for kc in range(8):
                                    P.op("pe", ("matmul", dict(
                                        out=pqq[:, gq % 4, 0:n], lhsT=Wq[:, kc, gq * 128:(gq + 1) * 128], rhs=CT[:, kc, 0:n],
                                        start=(kc == 0), stop=(kc == 7))), reads=["Wq", "CT"], writes=["pq%d" % (gq // 4)])
                            P.op("act", ("activation", dict(
                                out=qT[:, 4 * g4:4 * g4 + 4, 0:n], in_=pq[g4][:, :, 0:n], func=AF.Copy)),
                                reads=["pq%d" % g4], writes=[("qT", g4)])
                        for g4 in range(4):
                            st.append(lambda g4=g4: s_q(g4))

                        def s_sc():
                            for gq in range(16):
                                pqq = pq[gq // 4]
                                P.op("pe", ("matmul", dict(
                                    out=pqq[0:n, gq % 4, :], lhsT=qT[:, gq, 0:n], rhs=keysT[:, gq, :], start=True, stop=True)),
                                    reads=[("qT", gq // 4), "keysT"], writes=["pq%d" % (gq // 4)])
                            for g4 in range(4):
                                P.op("act", ("activation", dict(
                                    out=ssb[0:n, 4 * g4:4 * g4 + 4, :], in_=pq[g4][0:n, :, :], func=AF.Copy)),
                                    reads=["pq%d" % g4], writes=[("ssb", g4)])
                        st.append(s_sc)

                        def s_top(g4):
                            for gq in range(4 * g4, 4 * g4 + 4):
                                P.op("dve", ("max", dict(out=vals[0:n, gq, 0:8], in_=ssb[0:n, gq, :])),
                                     reads=[("ssb", g4)], writes=["vals"])
                                P.op("dve", ("max_index", dict(out=idxu[0:n, gq, 0:8], in_max=vals[0:n, gq, 0:8],
                                                               in_values=ssb[0:n, gq, :])),
                                     reads=[("ssb", g4), "vals"], writes=["idxu"])
                                P.op("dve", ("match_replace", dict(
                                    out=swk[0:n, 0:128], in_to_replace=vals[0:n, gq, 0:8], in_values=ssb[0:n, gq, :],
                                    imm_value=-1e30)), reads=[("ssb", g4), "vals"], writes=["swk"])
                                P.op("dve", ("max", dict(out=vals[0:n, gq, 8:16], in_=swk[0:n, 0:128])),
                                     reads=["swk"], writes=["vals"])
                                P.op("dve", ("max_index", dict(out=idxu[0:n, gq, 8:16], in_max=vals[0:n, gq, 8:16],
                                                               in_values=swk[0:n, 0:128])),
                                     reads=["swk", "vals"], writes=["idxu"])
                        for g4 in range(4):
                            st.append(lambda g4=g4: s_top(g4))

                        v4 = vals[:].rearrange("p (h two) k -> p h two k", two=2)
                        i4 = idxf[:].rearrange("p (h two) k -> p h two k", two=2)

                        def s_cs():
                            P.op("dve", ("tensor_copy", dict(out=idxf[0:n], in_=idxu[0:n])), reads=["idxu"], writes=["idxf"])
                            P.op("dve", ("tensor_tensor", dict(
                                out=cs_[0:n].rearrange("p h (a b) -> p h a b", a=16),
                                in0=v4[0:n, :, 0, :].unsqueeze(3).to_broadcast([n, 8, 16, 16]),
                                in1=v4[0:n, :, 1, :].unsqueeze(2).to_broadcast([n, 8, 16, 16]), op=ALU.add)),
                                reads=["vals"], writes=["cs"])
                        st.append(s_cs)

                        def s_top2(h0):
                            for h in range(h0, h0 + 4):
                                P.op("dve", ("max", dict(out=scv[0:n, h, 0:8], in_=cs_[0:n, h, :])),
                                     reads=["cs"], writes=["scv"])
                                P.op("dve", ("max_index", dict(out=posu[0:n, h, 0:8], in_max=scv[0:n, h, 0:8],
                                                               in_values=cs_[0:n, h, :])),
                                     reads=["cs", "scv"], writes=["posu"])
                                P.op("dve", ("match_replace", dict(
                                    out=swk[0:n, :], in_to_replace=scv[0:n, h, 0:8], in_values=cs_[0:n, h, :], imm_value=-1e30)),
                                    reads=["cs", "scv"], writes=["swk"])
                                P.op("dve", ("max", dict(out=scv[0:n, h, 8:16], in_=swk[0:n, :])),
                                     reads=["swk"], writes=["scv"])
                                P.op("dve", ("max_index", dict(out=posu[0:n, h, 8:16], in_max=scv[0:n, h, 8:16],
                                                               in_values=swk[0:n, :])),
                                     reads=["swk", "scv"], writes=["posu"])
                        st.append(lambda: s_top2(0))
                        st.append(lambda: s_top2(4))

                        def s_dec():
                            P.op("dve", ("tensor_copy", dict(out=posf[0:n], in_=posu[0:n])), reads=["posu"], writes=["posf"])
                            P.op("dve", ("tensor_scalar", dict(out=ai_[0:n], in0=posf[0:n], scalar1=-7.5, scalar2=1.0 / 16,
                                                               op0=ALU.add, op1=ALU.mult)), reads=["posf"], writes=["ai"])
                            P.op("dve", ("tensor_copy", dict(out=af_[0:n], in_=ai_[0:n])), reads=["ai"], writes=["af"])
                            P.op("dve", ("scalar_tensor_tensor", dict(
                                out=bf_[0:n], in0=af_[0:n], scalar=-16.0, in1=posf[0:n], op0=ALU.mult, op1=ALU.add)),
                                reads=["af", "posf"], writes=["bf"])
                            for (src_, two, dst_, dn) in ((af_, 0, i1s, "i1s"), (bf_, 1, i2s, "i2s")):
                                P.op("dve", ("tensor_tensor", dict(
                                    out=msk[0:n], in0=src_[0:n].unsqueeze(3).to_broadcast([n, 8, 16, 16]),
                                    in1=iota16[0:n, :].unsqueeze(1).unsqueeze(1).to_broadcast([n, 8, 16, 16]), op=ALU.is_equal)),
                                    reads=["af", "bf", "iota16"], writes=[("ssb", 0), ("ssb", 1), ("ssb", 2), ("ssb", 3)])
                                P.op("dve", ("tensor_tensor", dict(
                                    out=msk[0:n], in0=msk[0:n],
                                    in1=i4[0:n, :, two, :].unsqueeze(2).to_broadcast([n, 8, 16, 16]), op=ALU.mult)),
                                    reads=[("ssb", 0), ("ssb", 1), ("ssb", 2), ("ssb", 3)] + ["idxf"], writes=[("ssb", 0), ("ssb", 1), ("ssb", 2), ("ssb", 3)])
                                P.op("dve", ("tensor_reduce", dict(
                                    out=dst_[0:n], in_=msk[0:n], axis=AX.X, op=ALU.add)), reads=[("ssb", 0), ("ssb", 1), ("ssb", 2), ("ssb", 3)], writes=[dn])
                            P.op("dve", ("scalar_tensor_tensor", dict(
                                out=ef[0:n, :].rearrange("p (h k) -> p h k", h=8), in0=i1s[0:n], scalar=128.0, in1=i2s[0:n],
                                op0=ALU.mult, op1=ALU.add)), reads=["i1s", "i2s"], writes=["ef"])
                            P.op("dve", ("tensor_scalar", dict(out=eit[0:n, :], in0=ef[0:n, :], scalar1=float(l * NEXP),
                                                               scalar2=None, op0=ALU.add)), reads=["ef"], writes=[ein])
                        st.append(s_dec)

                        def s_gate():
                            P.op("dve", ("tensor_tensor", dict(
                                out=g_[0:n], in0=scv[0:n], in1=scv[0:n, :, 0:1].to_broadcast([n, 8, 16]), op=ALU.subtract)),
                                reads=["scv"], writes=[gn])
                            P.op("act", ("activation", dict(out=g_[0:n], in_=g_[0:n], func=AF.Exp)),
                                 reads=[gn], writes=[gn])
                            P.op("dve", ("tensor_reduce", dict(out=small[0:n, 32:40], in_=g_[0:n], axis=AX.X, op=ALU.add)),
                                 reads=[gn], writes=["small"])
                            P.op("dve", ("reciprocal", dict(out=small[0:n, 32:40], in_=small[0:n, 32:40])),
                                 reads=["small"], writes=["small"])
                            P.op("dve", ("tensor_tensor", dict(
                                out=g_[0:n], in0=g_[0:n], in1=small[0:n, 32:40].unsqueeze(2).to_broadcast([n, 8, 16]), op=ALU.mult)),
                                reads=[gn, "small"], writes=[gn])
                        st.append(s_gate)
                        return st

                    def run_slots(t, nxt_stages):
                        off, n = TILES[t]
                        pb = t % 2
                        ab, abn = abufs[pb], "abuf%d" % pb
                        g_, gn = ggs[pb], "gg%d" % pb
                        if n == 128:
                            eit, ein = eis[pb], "eiN%d" % pb
                        else:
                            eit, ein = ei[1], "ei1"
                        gflat = g_[:].rearrange("p h k -> p (h k)")
                        slots = []
                        every = 7

                        def slot_tail(j, sl):
                            dgi = j % 4
                            P.op("act", ("activation", dict(
                                out=wv[0:n, j:j + 1], in_=gl[0:n, j:j + 1], func=AF.Copy, scale=gflat[0:n, j:j + 1])),
                                reads=[("gl", j), gn], writes=[("wv", j)])
                            P.op("act", ("activation", dict(
                                out=dg[dgi][0:n, 0:n], in_=ident_b[0:n, 0:n], func=AF.Copy, scale=wv[0:n, j:j + 1])),
                                reads=["ident_b", ("wv", j)], writes=["dg%d" % dgi])
                            for half in range(2):
                                P.op("pe", ("matmul", dict(
                                    out=pacc[0:n, half, :], lhsT=dg[dgi][0:n, 0:n],
                                    rhs=gat[sl][0:n, D + half * 512:D + (half + 1) * 512],
                                    start=(j == 0), stop=(j == 127))),
                                    reads=["dg%d" % dgi, "gat%d" % sl], writes=["pacc"])
                        for j in range(128):
                            sl = gslot[0] % NG
                            gslot[0] += 1
                            P.dma("pool", ("indirect_dma_start", dict(
                                out=gat[sl][:], out_offset=None, in_=scr["UV"],
                                in_offset=bass.IndirectOffsetOnAxis(ap=eit[:, j:j + 1], axis=0))),
                                reads=[ein] + ([("UV", l, b_) for b_ in range(NEXP // 512)] if (t == 0 and j == 0) else []),
                                writes=["gat%d" % sl], semkey=gsem[sl])
                            P.op("dve", ("scalar_tensor_tensor", dict(
                                out=djunk[0:n, :], in0=gat[sl][0:n, 0:D], scalar=1.0, in1=ab[0:n, :], op0=ALU.mult, op1=ALU.mult,
                                accum_out=hid[0:n, j:j + 1])), reads=["gat%d" % sl, abn], writes=[("hid", j)])
                            P.op("act", ("activation", dict(out=gl[0:n, j:j + 1], in_=hid[0:n, j:j + 1], func=AF.Gelu)),
                                 reads=[("hid", j)], writes=[("gl", j)])
                            if j >= 1:
                                slot_tail(j - 1, slots[j - 1])
                            slots.append(sl)
                            if nxt_stages and j % every == every - 1:
                                nxt_stages.pop(0)()
                        slot_tail(127, slots[127])
                        while nxt_stages:
                            nxt_stages.pop(0)()
                        P.op("dve", ("tensor_tensor", dict(
                            out=H[0:n, t, :], in0=H[0:n, t, :], in1=pacc[0:n].rearrange("p a b -> p (a b)"), op=ALU.add)),
                            reads=["pacc", ("H", t)], writes=[("H", t)])

                    for st_ in prologue(0):
                        st_()
                    for t in range(NT):
                        run_slots(t, prologue(t + 1) if t + 1 < NT else [])
                    P.barrier()
            if stop_after in ("B", "C"):
                continue
            if stop_after is not None:
                for t in range(1, NT):
                    P.dma("sp", ("dma_start", dict(out=y_d[s, 128 * (t - 1):128 * t, :], in_=H[:, t, :])),
                          reads=[("H", t)], writes=[("y", s, t)])
                continue
            for t in range(1, NT):
                rms_rstd(H[:, t, :], 128, 8, ("H", t))
                P.op("dve", ("scalar_tensor_tensor", dict(
                    out=sq_junk[:, :], in0=H[:, t, :], scalar=small[:, 8:9], in1=gfin_bc[:, :],
                    op0=ALU.mult, op1=ALU.mult)), reads=[("H", t), "small", "gfin_bc"], writes=["sq_junk"])
                P.dma("sp", ("dma_start", dict(out=y_d[s, 128 * (t - 1):128 * t, :], in_=sq_junk[:, :])),
                      reads=["sq_junk"], writes=[("y", s, t)])
            P.barrier()
        P.barrier()
        P.emit()
    return nc, P


def make_in_maps(inputs, nseq=SEQ_PER_CORE, ncores=NCORES):
    consts = host_consts()
    xp = np.asarray(inputs["x_prompt"], dtype=np.float32)
    xs = np.asarray(inputs["x_sample"], dtype=np.float32)
    allx = np.concatenate([xp, xs], axis=0)
    maps = []
    for c in range(ncores):
        m = {"x": np.ascontiguousarray(allx[c * nseq:(c + 1) * nseq])}
        for k, shp in WEIGHT_SHAPES.items():
            m[k] = np.ascontiguousarray(np.asarray(inputs[k], dtype=np.float32).reshape(shp))
        m.update(consts)
        maps.append(m)
    return maps


def kernel(**inputs):
    nc, _ = build()
    maps = make_in_maps(inputs)
    res = run_bass_kernel_spmd(nc, maps, core_ids=list(range(NCORES)))
    ys = np.concatenate([r["y"] for r in res.results], axis=0)
    nb = np.asarray(inputs["x_prompt"]).shape[0]
    return (np.ascontiguousarray(ys[:nb]), np.ascontiguousarray(ys[nb:]))
```
